# Optimizing a Trainium2 kernel written in Bass

```python
import math
import jax, jax.numpy as jnp
from jax import lax
import numpy as np

D_MODEL = 1024
BATCH = 8
SEQ = 2048
DEPTH = 1
DEC_BATCH = 32
DEC_SEQ = 1
PAST_LEN = 16384
PAGE_SIZE = 128

N_META = 16
A_HEADS = 8
A_NOPE = 64
A_ROPE = 32
A_QK = A_NOPE + A_ROPE
A_V = 64
A_WIDTH = A_HEADS * A_V
Q_LORA = 384
KV_LORA = 256
ROPE_THETA = 10000.0
Q_BLOCK = 128
B_HEADS = 4
B_DK = 128
B_DV = 128
B_FDIM = B_HEADS * B_DK
B_WIDTH = B_HEADS * B_DV
CHUNK = 64
EPS = 1e-6
SPLIT_SIZES = (Q_LORA, KV_LORA, A_ROPE, B_FDIM, B_FDIM, B_WIDTH, A_WIDTH, B_WIDTH, D_MODEL, D_MODEL)
SPLIT_IDX = tuple(int(c) for c in np.cumsum(SPLIT_SIZES)[:-1])
IN_COLS = sum(SPLIT_SIZES)

kernel_name = 'mla_hgrn2_gated_hybrid_step'


def rmsnorm(x, g):
    xf = x.astype(jnp.float32)
    r = lax.rsqrt(jnp.mean(xf * xf, axis=-1, keepdims=True) + EPS)
    return (xf * r).astype(x.dtype) * g


def rope(x, pos):
    half = A_ROPE // 2
    inv = ROPE_THETA ** (-jnp.arange(half, dtype=jnp.float32) / half)
    ang = pos.astype(jnp.float32)[:, None] * inv
    ang = ang.reshape(ang.shape[0], *([1] * (x.ndim - 3)), half)
    cos, sin = jnp.cos(ang).astype(x.dtype), jnp.sin(ang).astype(x.dtype)
    x1, x2 = x[..., :half], x[..., half:]
    return jnp.concatenate([x1 * cos - x2 * sin, x1 * sin + x2 * cos], axis=-1)


def mla_queries(cq, pos, g_cq, w_uq, g_qn):
    q = (rmsnorm(cq, g_cq) @ w_uq).reshape(*cq.shape[:-1], A_HEADS, A_QK)
    q = jnp.concatenate([q[..., :A_NOPE], rope(q[..., A_NOPE:], pos)], axis=-1)
    return rmsnorm(q, g_qn)


def mla_keys(ckv, kr, w_uk, g_kn):
    kn = jnp.einsum('bkc,chd->bkhd', ckv, w_uk)
    kr = jnp.broadcast_to(kr[:, :, None, :], kn.shape[:-1] + (A_ROPE,))
    return rmsnorm(jnp.concatenate([kn, kr], axis=-1), g_kn)


def mla_prompt(q, k, v, with_meta):
    b = q.shape[0]
    scale = A_QK ** -0.5
    kpos = jnp.arange(k.shape[1])

    def attend(qb, qpos, kk, vv, kp):
        s = jnp.einsum('bqhd,bkhd->bhqk', qb, kk).astype(jnp.float32) * scale
        s = jnp.where(kp[None, :] <= qpos[:, None], s, -jnp.inf)
        p = jax.nn.softmax(s, axis=-1).astype(vv.dtype)
        return jnp.einsum('bhqk,bkhd->bqhd', p, vv)

    nb = SEQ // Q_BLOCK
    qr = q[:, -SEQ:].reshape(b, nb, Q_BLOCK, A_HEADS, A_QK).swapaxes(0, 1)

    def block(args):
        qb, j = args
        return attend(qb, N_META + j * Q_BLOCK + jnp.arange(Q_BLOCK), k, v, kpos)

    o = lax.map(block, (qr, jnp.arange(nb)))
    o = o.swapaxes(0, 1).reshape(b, SEQ, A_HEADS, A_V)
    if with_meta:
        mpos = jnp.arange(N_META)
        om = attend(q[:, :N_META], mpos, k[:, :N_META], v[:, :N_META], mpos)
        o = jnp.concatenate([om, o], axis=1)
    return o


def online_update(carry, s, c):
    m, l_, acc = carry
    m_new = jnp.maximum(m, s.max(-1))
    a = jnp.exp(m - m_new)
    p = jnp.exp(s - m_new[..., None])
    acc = acc * a[..., None] + jnp.einsum('bhqk,bkc->bhqc', p, c.astype(jnp.float32))
    return (m_new, l_ * a + p.sum(-1), acc)


def mla_sample(q, ckv, kr, cache_lat, cache_kr, page_table, w_uk, w_uv, g_kn):
    db, t = q.shape[:2]
    scale = A_QK ** -0.5

    def scores(c, r):
        return jnp.einsum('bqhd,bkhd->bhqk', q, mla_keys(c, r, w_uk, g_kn)).astype(jnp.float32) * scale

    def page_step(carry, pt):
        c = cache_lat[pt]
        r = cache_kr[pt]
        return online_update(carry, scores(c, r), c), None

    init = (jnp.full((db, A_HEADS, t), -jnp.inf, jnp.float32),
            jnp.zeros((db, A_HEADS, t), jnp.float32),
            jnp.zeros((db, A_HEADS, t, KV_LORA), jnp.float32))
    carry, _ = lax.scan(page_step, init, page_table.T)
    causal = jnp.tril(jnp.ones((t, t), bool))
    s_new = jnp.where(causal, scores(ckv, kr), -jnp.inf)
    _, l_, acc = online_update(carry, s_new, ckv)
    o_lat = (acc / l_[..., None]).astype(w_uv.dtype)
    return jnp.einsum('bhqc,chd->bqhd', o_lat, w_uv)


def hgrn2_inputs(bq, bf, bi, lb):
    z = bf.astype(jnp.float32)
    logf = jnp.log(lb + (1.0 - lb) * jax.nn.sigmoid(z))
    k = (1.0 - lb) * jax.nn.sigmoid(-z)
    q = jax.nn.silu(bq.astype(jnp.float32))
    heads = lambda a: a.reshape(*a.shape[:-1], B_HEADS, -1)
    return heads(q), heads(k), heads(bi.astype(jnp.float32)), heads(logf)


def hgrn2_chunked(q, k, v, logf, s0, chunk):
    b, L = q.shape[:2]
    n = L // chunk
    chunks = lambda a: a.reshape(b, n, chunk, *a.shape[2:]).swapaxes(0, 1)
    causal = jnp.tril(jnp.ones((chunk, chunk), bool))[None, :, :, None, None]

    def step(S, xs):
        qc, kc, vc, gc = xs
        cum = jnp.cumsum(gc, axis=1)
        decay = jnp.exp(jnp.where(causal, cum[:, :, None] - cum[:, None, :], -jnp.inf))
        attn = jnp.einsum('bthk,bshk,btshk->bhts', qc, kc, decay)
        o = jnp.einsum('bhts,bshv->bthv', attn, vc) + jnp.einsum('bthk,bhkv->bthv', qc * jnp.exp(cum), S)
        last = cum[:, -1]
        S = jnp.exp(last)[..., None] * S + jnp.einsum('bshk,bshv->bhkv', kc * jnp.exp(last[:, None] - cum), vc)
        return S, o

    S, o = lax.scan(step, s0, (chunks(q), chunks(k), chunks(v), chunks(logf)))
    return o.swapaxes(0, 1).reshape(b, L, *o.shape[3:]), S


def merge(h, oa, ob, ga, gb, ma, mb, g_bn, w_oa, w_ob, w_o):
    ya = (oa.reshape(*oa.shape[:2], A_WIDTH) * jax.nn.silu(ga)) @ w_oa
    obn = rmsnorm(ob.astype(h.dtype), g_bn).reshape(*ob.shape[:2], B_WIDTH)
    yb = (obn * jax.nn.silu(gb)) @ w_ob
    return h + (jax.nn.sigmoid(ma) * ya + jax.nn.sigmoid(mb) * yb) @ w_o


def setup_inputs(seed: int = 0) -> dict:
    key = jax.random.key(seed)
    ks = jax.random.split(key, 24)
    f32 = jnp.float32
    n_pages = PAST_LEN // PAGE_SIZE
    n_used = DEC_BATCH * n_pages
    n_pool = n_used + max(1, n_used // 4)
    nrm = lambda k, shape, s=1.0: s * jax.random.normal(k, shape, f32)
    gain = lambda k, n: 1.0 + 0.01 * jax.random.normal(k, (DEPTH, n), f32)
    page_table = jax.random.permutation(ks[5], n_pool)[:n_used].reshape(DEC_BATCH, n_pages).astype(jnp.int32)
    return {
        'x_prompt': nrm(ks[0], (BATCH, SEQ, D_MODEL)),
        'x_sample': nrm(ks[1], (DEC_BATCH, DEC_SEQ, D_MODEL)),
        'cache_latent': nrm(ks[2], (DEPTH, n_pool, PAGE_SIZE, KV_LORA)),
        'cache_krope': nrm(ks[3], (DEPTH, n_pool, PAGE_SIZE, A_ROPE)),
        'state_hgrn': nrm(ks[4], (DEPTH, DEC_BATCH, B_HEADS, B_DK, B_DV), 0.3),
        'page_table': page_table,
        'meta_tokens': nrm(ks[6], (N_META, D_MODEL)),
        'norm_g': gain(ks[7], D_MODEL),
        'w_in': nrm(ks[8], (DEPTH, D_MODEL, IN_COLS), D_MODEL ** -0.5),
        'g_cq': gain(ks[9], Q_LORA),
        'w_uq': nrm(ks[10], (DEPTH, Q_LORA, A_HEADS * A_QK), Q_LORA ** -0.5),
        'g_ckv': gain(ks[11], KV_LORA),
        'w_uk': nrm(ks[12], (DEPTH, KV_LORA, A_HEADS, A_NOPE), KV_LORA ** -0.5),
        'w_uv': nrm(ks[13], (DEPTH, KV_LORA, A_HEADS, A_V), KV_LORA ** -0.5),
        'g_qn': gain(ks[14], A_QK),
        'g_kn': gain(ks[15], A_QK),
        'lb_logits': nrm(ks[16], (DEPTH + 1, B_FDIM), 0.5),
        'g_bn': gain(ks[17], B_DV),
        'w_oa': nrm(ks[18], (DEPTH, A_WIDTH, D_MODEL), A_WIDTH ** -0.5),
        'w_ob': nrm(ks[19], (DEPTH, B_WIDTH, D_MODEL), B_WIDTH ** -0.5),
        'w_o': nrm(ks[20], (DEPTH, D_MODEL, D_MODEL), D_MODEL ** -0.5),
    }


def reference(x_prompt, x_sample, cache_latent, cache_krope, state_hgrn, page_table, meta_tokens,
              norm_g, w_in, g_cq, w_uq, g_ckv, w_uk, w_uv, g_qn, g_kn, lb_logits, g_bn, w_oa, w_ob, w_o):
    f32 = jnp.float32
    lb_all = jnp.cumsum(jax.nn.softmax(lb_logits.astype(f32), axis=0), axis=0)
    b = x_prompt.shape[0]
    hp = jnp.concatenate([jnp.broadcast_to(meta_tokens[None].astype(x_prompt.dtype), (b, N_META, D_MODEL)), x_prompt], axis=1)
    hs = x_sample
    pos_p = jnp.arange(N_META + SEQ)
    pos_s = PAST_LEN + jnp.arange(DEC_SEQ)
    lat_p, kr_p, st_p, lat_s, kr_s, st_s = [], [], [], [], [], []
    for l in range(DEPTH):
        last = l == DEPTH - 1
        r0 = N_META if last else 0
        xn = rmsnorm(hp, norm_g[l])
        cq, ckv, kr, bq, bf, bi, ga, gb, ma, mb = jnp.split(xn @ w_in[l], SPLIT_IDX, axis=-1)
        ckv = rmsnorm(ckv, g_ckv[l])
        kr = rope(kr, pos_p)
        k = mla_keys(ckv, kr, w_uk[l], g_kn[l])
        v = jnp.einsum('bkc,chd->bkhd', ckv, w_uv[l])
        q = mla_queries(cq[:, r0:], pos_p[r0:], g_cq[l], w_uq[l], g_qn[l])
        oa = mla_prompt(q, k, v, not last)
        hq, hk, hv, hf = hgrn2_inputs(bq, bf, bi, lb_all[l])
        s0 = jnp.zeros((b, B_HEADS, B_DK, B_DV), f32)
        om, s_meta = hgrn2_chunked(hq[:, :N_META], hk[:, :N_META], hv[:, :N_META], hf[:, :N_META], s0, N_META)
        orl, s_fin = hgrn2_chunked(hq[:, N_META:], hk[:, N_META:], hv[:, N_META:], hf[:, N_META:], s_meta, CHUNK)
        ob = orl if last else jnp.concatenate([om, orl], axis=1)
        hp = merge(hp[:, r0:], oa, ob, ga[:, r0:], gb[:, r0:], ma[:, r0:], mb[:, r0:], g_bn[l], w_oa[l], w_ob[l], w_o[l])
        lat_p.append(ckv)
        kr_p.append(kr)
        st_p.append(s_fin.astype(state_hgrn.dtype))
        xn = rmsnorm(hs, norm_g[l])
        cq, ckv, kr, bq, bf, bi, ga, gb, ma, mb = jnp.split(xn @ w_in[l], SPLIT_IDX, axis=-1)
        ckv = rmsnorm(ckv, g_ckv[l])
        kr = rope(kr, pos_s)
        q = mla_queries(cq, pos_s, g_cq[l], w_uq[l], g_qn[l])
        oa = mla_sample(q, ckv, kr, cache_latent[l], cache_krope[l], page_table, w_uk[l], w_uv[l], g_kn[l])
        hq, hk, hv, hf = hgrn2_inputs(bq, bf, bi, lb_all[l])
        ob, s_new = hgrn2_chunked(hq, hk, hv, hf, state_hgrn[l].astype(f32), math.gcd(DEC_SEQ, CHUNK))
        hs = merge(hs, oa, ob, ga, gb, ma, mb, g_bn[l], w_oa[l], w_ob[l], w_o[l])
        lat_s.append(ckv)
        kr_s.append(kr)
        st_s.append(s_new.astype(state_hgrn.dtype))
    return (hp, hs, jnp.stack(lat_p), jnp.stack(kr_p), jnp.stack(st_p), jnp.stack(lat_s), jnp.stack(kr_s), jnp.stack(st_s))
```

```python
import contextlib
import math
import numpy as np
import concourse.bass as bass
import concourse.mybir as mybir
from concourse.bass_utils import run_bass_kernel_spmd

F32 = mybir.dt.float32
BF16 = mybir.dt.bfloat16
I32 = mybir.dt.int32
AF = mybir.ActivationFunctionType
ALU = mybir.AluOpType
AX = mybir.AxisListType

NCORES = 8
D = 1024
SEQ = 2048
NMETA = 16
TP = SEQ + NMETA
NS = 4
NTOK = TP + NS
INC = 5280
EPS = 1e-6
NPOOL = 5120
NPAGES = 128
SCALE = 96 ** -0.5
SBIAS = -(96 ** 0.5)
COMPUTE = ("pe", "act", "dve", "pool")
DEBUG = False


class Sched:
    def __init__(self, nc):
        self.nc = nc
        self.ops = []
        self.lastw = {}
        self.readers = {}
        self.chan_last = {}
        self.chan_count = {}
        self.pending = {}

    def barrier(self, fns):
        ids = [self.add(e, fns[e], (), [("bar", e)]) for e in COMPUTE]
        allc = set(ids) | set(self.chan_last.values())
        for e in COMPUTE + ("sp",):
            self.pending.setdefault(e, set()).update(allc)

    def add(self, eng, fn, reads=(), writes=(), dma=None):
        idx = len(self.ops)
        deps = set(self.pending.pop(eng, ()))
        for r in reads:
            w = self.lastw.get(r)
            if w is not None:
                deps.add(w)
        for w_ in writes:
            w = self.lastw.get(w_)
            if w is not None:
                deps.add(w)
            rd = self.readers.get(w_)
            if rd:
                deps.update(rd.values())
        if dma is not None:
            if dma in self.chan_last:
                deps.add(self.chan_last[dma])
            self.chan_count[dma] = self.chan_count.get(dma, 0) + 1
        op = dict(eng=eng, fn=fn, deps=deps, dma=dma, idx=idx,
                  cnt=self.chan_count.get(dma, 0) if dma is not None else 0)
        self.ops.append(op)
        for r in reads:
            d = self.readers.setdefault(r, {})
            d[eng if dma is None else ("dma", dma)] = idx
        for w_ in writes:
            self.lastw[w_] = idx
            self.readers[w_] = {}
        if dma is not None:
            self.chan_last[dma] = idx
        return idx

    def pe(self, fn, r=(), w=()):
        return self.add("pe", fn, r, w)

    def act(self, fn, r=(), w=()):
        return self.add("act", fn, r, w)

    def dve(self, fn, r=(), w=()):
        return self.add("dve", fn, r, w)

    def pool(self, fn, r=(), w=()):
        return self.add("pool", fn, r, w)

    def dma(self, chan, fn, r=(), w=(), q="sp"):
        return self.add(q, fn, r, w, dma=chan)

    def emit(self, stack):
        nc = self.nc
        ops = self.ops
        waited = {}
        signal = set()
        for op in ops:
            e = op["eng"]
            waits = []
            for d in sorted(op["deps"]):
                p = ops[d]
                if p["dma"] is not None:
                    key = ("dma", p["dma"])
                    if waited.get((e, key), 0) >= p["cnt"]:
                        continue
                    waited[(e, key)] = p["cnt"]
                    waits.append(("dma", p["dma"], p["cnt"]))
                else:
                    pe_ = p["eng"]
                    if pe_ == "pe" and e == "pe" and op["dma"] is None:
                        continue
                    key = ("eng", pe_)
                    if waited.get((e, key), -1) >= d:
                        continue
                    waited[(e, key)] = d
                    signal.add(d)
                    waits.append(("eng", pe_, d))
            op["waits"] = waits
        cnt = {e: 0 for e in COMPUTE}
        for op in ops:
            if op["idx"] in signal:
                cnt[op["eng"]] += 1
                op["ticket"] = cnt[op["eng"]]
        esem = {e: stack.enter_context(nc.semaphore("s_" + e)) for e in COMPUTE}
        csem = {c: stack.enter_context(nc.semaphore("c_%d" % i))
                for i, c in enumerate(self.chan_count)}
        per = {e: [] for e in COMPUTE + ("sp",)}
        for op in ops:
            per[op["eng"]].append(op)
        chan_count = self.chan_count

        def run(engh, lst, final=False):
            for op in lst:
                for w in op["waits"]:
                    if w[0] == "dma":
                        engh.wait_ge(csem[w[1]], 16 * w[2])
                    else:
                        engh.wait_ge(esem[w[1]], ops[w[2]]["ticket"])
                inst = op["fn"](engh)
                if op["dma"] is not None:
                    inst.then_inc(csem[op["dma"]], 16)
                elif op["idx"] in signal:
                    inst.then_inc(esem[op["eng"]], 1)
            if final:
                for c, n in chan_count.items():
                    engh.wait_ge(csem[c], 16 * n)

        block = stack.enter_context(nc.Block())

        @block.sync
        def _(e):
            run(e, per["sp"], final=True)

        @block.tensor
        def _(e):
            run(e, per["pe"])

        @block.scalar
        def _(e):
            run(e, per["act"])

        @block.vector
        def _(e):
            run(e, per["dve"])

        @block.gpsimd
        def _(e):
            run(e, per["pool"])


def _inv_freq():
    j = np.arange(16, dtype=np.float32)
    return (np.float32(10000.0) ** (-j / np.float32(16))).astype(np.float32)


def build(dbg=False, stop_after=99):
    nc = bass.Bass("TRN2", target_bir_lowering=False)
    din = lambda n, s, d=F32: nc.dram_tensor(n, s, d, kind="ExternalInput").ap()
    dout = lambda n, s, d=F32: nc.dram_tensor(n, s, d, kind="ExternalOutput").ap()
    xp = din("xp", [SEQ, D]); xs = din("xs", [NS, D]); meta = din("meta", [NMETA, D])
    ccomb = din("ccomb", [NPOOL * 32, 4 * 288])
    st_in = din("st_in", [NS, 4, 128, 128]); pt = din("pt", [NS, NPAGES], I32)
    norm_g = din("norm_g", [1, D]); w_in = din("w_in", [D, INC]); g_cq = din("g_cq", [1, 384])
    w_uq = din("w_uq", [384, 768]); g_ckv = din("g_ckv", [1, 256]); w_uk = din("w_uk", [256, 512])
    w_uv = din("w_uv", [256, 512]); g_qn = din("g_qn", [1, 96]); g_kn = din("g_kn", [1, 96])
    lb = din("lb", [2, 512]); g_bn = din("g_bn", [1, 128]); w_oa = din("w_oa", [512, D])
    w_ob = din("w_ob", [512, D]); w_o = din("w_o", [D, D])
    y_p = dout("y_p", [SEQ, D]); y_s = dout("y_s", [NS, D]); lat_p = dout("lat_p", [TP, 256])
    kr_p = dout("kr_p", [TP, 32]); hg_p = dout("hg_p", [4, 128, 128]); lat_s = dout("lat_s", [NS, 256])
    kr_s = dout("kr_s", [NS, 32]); hg_s = dout("hg_s", [NS, 4, 128, 128])
    dbgo = {}

    with contextlib.ExitStack() as st:
        T = lambda n, s, d=F32: st.enter_context(nc.sbuf_tensor(n, s, d))
        s = Sched(nc)
        psum_all = st.enter_context(nc.psum_tensor("psum_all", [128, 4096], F32))
        banks = [psum_all[:, 512 * i:512 * (i + 1)] for i in range(8)]
        bkb = [b_.bitcast(BF16) for b_ in banks]

        def bk(i):
            return "bank%d" % i

        AKB = 150
        arena = T("arena", [128, AKB * 256])

        class Bump:
            def __init__(self, lo_kb, hi_kb):
                self.off = int(lo_kb * 256); self.hi = int(hi_kb * 256)

            def __call__(self, shape, dt=F32):
                free = list(shape[1:])
                n = 1
                for d_ in free:
                    n *= d_
                words = (n + 1) // 2 if dt == BF16 else n
                words = (words + 7) // 8 * 8
                v = arena[0:shape[0], self.off:self.off + words]
                self.off += words
                assert self.off <= self.hi, (self.off, self.hi)
                if dt == BF16:
                    v = v.bitcast(BF16)[:, 0:n]
                elif dt == I32:
                    v = v.bitcast(I32)[:, 0:n]
                else:
                    v = v[:, 0:n]
                if len(free) > 1:
                    names = " ".join("a%d" % i for i in range(len(free)))
                    v = v.rearrange("p (%s) -> p %s" % (names, names), **{"a%d" % i: free[i] for i in range(1, len(free))})
                return v

        scr_act = T("scr_act", [1, 8]); scr_dve = T("scr_dve", [1, 8]); scr_pool = T("scr_pool", [1, 8])
        scr_bf = T("scr_bf", [1, 8], BF16)

        def barrier():
            s.pe(lambda e: e.matmul(banks[7][0:1, 0:8], lhsT=scr_bf[0:1, 0:1], rhs=scr_bf[0:1, 0:8], start=True, stop=True), r=["scr_bf"], w=[bk(7), bk(4), bk(5)])
            s.barrier({
                "pe": lambda e: e.matmul(banks[7][0:1, 0:8], lhsT=scr_bf[0:1, 0:1], rhs=scr_bf[0:1, 0:8], start=True, stop=True),
                "act": lambda e: e.activation(out=scr_act[:], in_=scr_act[:], func=AF.Copy),
                "dve": lambda e: e.memset(scr_dve[:], 0.0),
                "pool": lambda e: e.memset(scr_pool[:], 0.0),
            })
        s.pool(lambda e: e.memset(scr_bf[:], 0.0), w=["scr_bf"])
        s.pool(lambda e: e.memset(scr_act[:], 0.0), w=["scr_act"])

        ident = T("ident", [128, 128], BF16)
        identf = T("identf", [128, 128], F32)
        tri = T("tri", [128, 128], BF16)
        btri = T("btri", [128, 128], BF16)
        for tt, tn_ in ((ident, "ident"), (identf, "identf")):
            s.pool(lambda e, tt=tt: e.memset(tt[:], 0.0), w=[tn_])
            s.pool(lambda e, tt=tt: e.affine_select(out=tt[:], in_=tt[:], pattern=[[1, 128]], compare_op=ALU.not_equal,
                                                    fill=1.0, base=0, channel_multiplier=-1), r=[tn_], w=[tn_])
        s.pool(lambda e: e.memset(tri[:], 1.0), w=["tri"])
        s.pool(lambda e: e.affine_select(out=tri[:], in_=tri[:], pattern=[[1, 128]], compare_op=ALU.is_ge,
                                         fill=0.0, base=0, channel_multiplier=-1), r=["tri"], w=["tri"])
        s.pool(lambda e: e.tensor_copy(out=btri[:], in_=tri[:]), r=["tri"], w=["btri"])
        s.pool(lambda e: e.memset(btri[0:64, 64:128], 0.0), r=["btri"], w=["btri"])
        neghalf = T("neghalf", [128, 16])
        s.pool(lambda e: e.memset(neghalf[:], -0.5), w=["neghalf"])
        ones_bf = T("ones_bf", [128, 64], BF16)
        s.pool(lambda e: e.memset(ones_bf[:], 1.0), w=["ones_bf"])
        sbias = T("sbias", [128, 1])
        s.pool(lambda e: e.memset(sbias[:], SBIAS), w=["sbias"])
        epsc = T("epsc", [128, 1])
        s.pool(lambda e: e.memset(epsc[:], EPS), w=["epsc"])
        selrow = T("selrow", [65, 64])
        s.pool(lambda e: e.memset(selrow[:], 0.0), w=["selrow"])
        s.pool(lambda e: e.memset(selrow[64:65, :], 1.0), r=["selrow"], w=["selrow"])

        NT = 18
        pos = T("pos", [128, NT])
        ang = T("ang", [128, 2, NT, 16])
        invf = T("invf", [128, 16])
        kq = T("kq", [128, 2 * NT * 16])
        kqi = T("kqi", [128, 2 * NT * 16], I32)
        CC = T("CC", [128, NT, 32])
        SS = T("SS", [128, NT, 32])
        s.pool(lambda e: e.iota(pos[:], pattern=[[128, NT]], base=NMETA - 128, channel_multiplier=1,
                                allow_small_or_imprecise_dtypes=True), w=["pos"])
        s.pool(lambda e: e.iota(pos[:, 0:1], pattern=[[0, 1]], base=0, channel_multiplier=1,
                                allow_small_or_imprecise_dtypes=True), r=["pos"], w=["pos"])
        s.pool(lambda e: e.memset(pos[:, NT - 1:NT], 16384.0), r=["pos"], w=["pos"])
        for j, v in enumerate(_inv_freq()):
            s.pool(lambda e, j=j, v=float(v): e.memset(invf[:, j:j + 1], v), r=["invf"], w=["invf"])
        s.dve(lambda e: e.tensor_tensor(out=ang[:, 0], in0=pos[:].unsqueeze(2).to_broadcast([128, NT, 16]),
                                        in1=invf[:].unsqueeze(1).to_broadcast([128, NT, 16]), op=ALU.mult),
              r=["pos", "invf"], w=["ang"])
        s.dve(lambda e: e.tensor_scalar(out=ang[:, 1], in0=ang[:, 0], scalar1=math.pi / 2, scalar2=None, op0=ALU.add),
              r=["ang"], w=["ang"])
        angf = ang[:].rearrange("p a t j -> p (a t j)")
        s.dve(lambda e: e.tensor_scalar(out=kq[:], in0=angf, scalar1=1.0 / (2 * math.pi), scalar2=None, op0=ALU.mult),
              r=["ang"], w=["kq"])
        s.dve(lambda e: e.tensor_copy(out=kqi[:], in_=kq[:]), r=["kq"], w=["kqi"])
        s.dve(lambda e: e.tensor_copy(out=kq[:], in_=kqi[:]), r=["kqi"], w=["kq"])
        C1 = 6.28125
        C2 = 2 * math.pi - C1
        s.dve(lambda e: e.scalar_tensor_tensor(out=angf, in0=kq[:], scalar=-C1, in1=angf, op0=ALU.mult, op1=ALU.add),
              r=["kq", "ang"], w=["ang"])
        s.dve(lambda e: e.scalar_tensor_tensor(out=angf, in0=kq[:], scalar=-C2, in1=angf, op0=ALU.mult, op1=ALU.add),
              r=["kq", "ang"], w=["ang"])
        s.dve(lambda e: e.tensor_scalar(out=kq[:], in0=angf, scalar1=math.pi, scalar2=-2 * math.pi, op0=ALU.is_gt, op1=ALU.mult),
              r=["ang"], w=["kq"])
        s.dve(lambda e: e.tensor_tensor(out=angf, in0=angf, in1=kq[:], op=ALU.add), r=["ang", "kq"], w=["ang"])
        s.dve(lambda e: e.tensor_scalar(out=kq[:], in0=angf, scalar1=-math.pi, scalar2=2 * math.pi, op0=ALU.is_lt, op1=ALU.mult),
              r=["ang"], w=["kq"])
        s.dve(lambda e: e.tensor_tensor(out=angf, in0=angf, in1=kq[:], op=ALU.add), r=["ang", "kq"], w=["ang"])
        s.dve(lambda e: e.tensor_scalar(out=angf, in0=angf, scalar1=-math.pi, scalar2=math.pi, op0=ALU.max, op1=ALU.min),
              r=["ang"], w=["ang"])
        s.act(lambda e: e.activation(out=SS[:, :, 16:32], in_=ang[:, 0], func=AF.Sin), r=["ang"], w=["SS"])
        s.act(lambda e: e.activation(out=CC[:, :, 0:16], in_=ang[:, 1], func=AF.Sin), r=["ang"], w=["CC"])
        s.dve(lambda e: e.tensor_copy(out=CC[:, :, 16:32], in_=CC[:, :, 0:16]), r=["CC"], w=["CC"])
        s.dve(lambda e: e.tensor_scalar(out=SS[:, :, 0:16], in0=SS[:, :, 16:32], scalar1=-1.0, scalar2=None, op0=ALU.mult),
              r=["SS"], w=["SS"])

        gckv_bc = T("gckv_bc", [128, 256])
        s.dma("c0", lambda e: e.dma_start(out=gckv_bc[:], in_=g_ckv[0:1, :].partition_broadcast(128)), w=["gckv_bc"])
        normg_c = T("normg_c", [128, 8])
        gcq_c = T("gcq_c", [128, 3])
        gq2_c = T("gq2_c", [96, 2])
        gbn_c = T("gbn_c", [128, 1])
        lb_c = T("lb_c", [128, 2, 4])
        s.dma("c1", lambda e: e.dma_start(out=normg_c[:], in_=norm_g.rearrange("o (k p) -> p (o k)", p=128),
                                          allow_slow_non_contiguous=True), w=["normg_c"])
        s.dma("c2", lambda e: e.dma_start(out=gcq_c[:], in_=g_cq.rearrange("o (k p) -> p (o k)", p=128),
                                          allow_slow_non_contiguous=True), w=["gcq_c"])
        s.dma("c3", lambda e: e.dma_start(out=gq2_c[:, 0:1], in_=g_qn.rearrange("o p -> p o"),
                                          allow_slow_non_contiguous=True), w=["gq2_c"])
        s.dma("c4", lambda e: e.dma_start(out=gq2_c[:, 1:2], in_=g_kn.rearrange("o p -> p o"),
                                          allow_slow_non_contiguous=True), r=["gq2_c"], w=["gq2_c"])
        s.dma("c5", lambda e: e.dma_start(out=gbn_c[:], in_=g_bn.rearrange("o p -> p o"),
                                          allow_slow_non_contiguous=True), w=["gbn_c"])
        s.dma("c6", lambda e: e.dma_start(out=lb_c[:], in_=lb.rearrange("r (h p) -> p r h", p=128),
                                          allow_slow_non_contiguous=True), w=["lb_c"])
        gq2 = T("gq2", [96, 1])
        s.dve(lambda e: e.tensor_tensor(out=gq2[:], in0=gq2_c[:, 0:1], in1=gq2_c[:, 1:2], op=ALU.mult), r=["gq2_c"], w=["gq2"])
        hg_a = T("hg_a", [128, 4]); hg_nb = T("hg_nb", [128, 4]); hg_nnb = T("hg_nnb", [128, 4]); hg_t = T("hg_t", [128, 4])
        s.dve(lambda e: e.tensor_tensor(out=hg_t[:], in0=lb_c[:, 0, :], in1=lb_c[:, 1, :], op=ALU.subtract), r=["lb_c"], w=["hg_t"])
        s.act(lambda e: e.activation(out=hg_t[:], in_=hg_t[:], func=AF.Tanh, scale=0.5), r=["hg_t"], w=["hg_t"])
        s.dve(lambda e: e.tensor_scalar(out=hg_a[:], in0=hg_t[:], scalar1=0.25, scalar2=0.75, op0=ALU.mult, op1=ALU.add), r=["hg_t"], w=["hg_a"])
        s.dve(lambda e: e.tensor_scalar(out=hg_nb[:], in0=hg_t[:], scalar1=-0.25, scalar2=0.25, op0=ALU.mult, op1=ALU.add), r=["hg_t"], w=["hg_nb"])
        s.dve(lambda e: e.tensor_scalar(out=hg_nnb[:], in0=hg_t[:], scalar1=0.25, scalar2=-0.25, op0=ALU.mult, op1=ALU.add), r=["hg_t"], w=["hg_nnb"])

        xnT = T("xnT", [128, 8, NTOK], BF16)
        aA = Bump(0, 84)
        QT = aA([128, 8, NTOK], BF16)
        KT = aA([128, 8, NTOK], BF16)
        VA = aA([128, 17, 8, 65], BF16)
        Wukv = T("Wukv", [128, 2, 1024], BF16)
        s.pool(lambda e: e.memset(VA[:, :, :, 64:65], 1.0), w=["VA1"])

        a1 = Bump(84, 150)
        W1 = a1([128, 8, 672], BF16)
        Wuq = a1([128, 3, 768], BF16)
        wstf = [a1([128, 2048]) for i in range(2)]
        w_in_v = w_in.rearrange("(k p) c -> p k c", p=128)

        def load_w_in(dst, c_lo, c_hi, wst, chunk=256, engs=("pool",)):
            ci = 0
            res = ("W", c_lo)
            for c0 in range(c_lo, c_hi, chunk):
                cw = min(chunk, c_hi - c0)
                b = ci % 2
                stv = wst[b][:, 0:8 * cw].rearrange("p (k c) -> p k c", c=cw)
                s.dma("wst%d" % b, lambda e, stv=stv, c0=c0, cw=cw: e.dma_start(out=stv, in_=w_in_v[:, :, c0:c0 + cw]), w=["wst%d" % b])
                eng = engs[ci % len(engs)]
                dv = dst[:, :, c0 - c_lo:c0 - c_lo + cw]
                s.add(eng, lambda e, stv=stv, dv=dv, cw=cw: e.tensor_tensor(out=dv, in0=stv, in1=normg_c[:].unsqueeze(2).to_broadcast([128, 8, cw]), op=ALU.mult),
                      ["wst%d" % b, "normg_c"], [res])
                ci += 1
            return res

        rW1 = load_w_in(W1, 0, 672, wstf)
        for k3 in range(3):
            stv = wstf[1][:, 0:768]
            s.dma("wst1", lambda e, k3=k3, stv=stv: e.dma_start(out=stv, in_=w_uq[k3 * 128:(k3 + 1) * 128, :]), w=["wst1"])
            s.pool(lambda e, k3=k3, stv=stv: e.tensor_scalar(out=Wuq[:, k3, :], in0=stv, scalar1=gcq_c[:, k3:k3 + 1], scalar2=None, op0=ALU.mult),
                   r=["wst1", "gcq_c"], w=["Wuq"])
        stkv = wstf[0][:, 0:2048].rearrange("p (k c) -> p k c", c=1024)
        s.dma("wst0", lambda e, stkv=stkv: e.dma_start(out=stkv[:, :, 0:512], in_=w_uk.rearrange("(k p) c -> p k c", p=128)), w=["wst0"])
        s.dma("wst0b", lambda e, stkv=stkv: e.dma_start(out=stkv[:, :, 512:1024], in_=w_uv.rearrange("(k p) c -> p k c", p=128)), r=["wst0"], w=["wst0"])
        s.dve(lambda e, stkv=stkv: e.tensor_copy(out=Wukv[:], in_=stkv), r=["wst0"], w=["Wukv"])

        tiles = [(0, NMETA, meta[:, :], SEQ)]
        tiles += [(1 + i, 128, xp[i * 128:(i + 1) * 128, :], i * 128) for i in range(16)]
        tiles += [(17, NS, xs[:, :], TP)]
        xst = [a1([128, D]) for i in range(2)]
        xnb = [a1([128, D], BF16) for i in range(2)]
        junk = a1([128, D], BF16)
        junk2 = junk
        ssx = T("ssx", [128, NT])
        rsx = T("rsx", [128, NT])
        for n, (ti, R, src, cb) in enumerate(tiles):
            xb = n % 2
            nb = n % 2
            pb = n % 2
            s.dma("xst%d" % xb, lambda e, xb=xb, R=R, src=src: e.dma_start(out=xst[xb][:R, :], in_=src), w=["xst%d" % xb])
            s.act(lambda e, xb=xb, R=R, ti=ti: e.activation(out=junk[:R, :], in_=xst[xb][:R, :], func=AF.Square,
                                                           accum_out=ssx[:R, ti:ti + 1]),
                  r=["xst%d" % xb], w=["junk2", ("ssx", ti)])
            s.dve(lambda e, R=R, ti=ti: e.tensor_scalar(out=rsx[:R, ti:ti + 1], in0=ssx[:R, ti:ti + 1], scalar1=1.0 / D, scalar2=EPS,
                                                       op0=ALU.mult, op1=ALU.add), r=[("ssx", ti)], w=[("rsx", ti)])
            s.pool(lambda e, R=R, ti=ti: e.tensor_tensor(out=rsx[:R, ti:ti + 1], in0=rsx[:R, ti:ti + 1], in1=neghalf[:R, 0:1], op=ALU.pow),
                   r=[("rsx", ti), "neghalf"], w=[("rsx", ti)])
            s.dve(lambda e, xb=xb, nb=nb, R=R, ti=ti: e.tensor_scalar(out=xnb[nb][:R, :], in0=xst[xb][:R, :], scalar1=rsx[:R, ti:ti + 1],
                                                                     scalar2=None, op0=ALU.mult),
                  r=["xst%d" % xb, ("rsx", ti)], w=["xnb%d" % nb])
            psT = bkb[pb]
            for k in range(8):
                s.pe(lambda e, k=k, nb=nb, R=R, psT=psT: e.transpose(out=psT[:, k * 128:k * 128 + R], in_=xnb[nb][:R, k * 128:(k + 1) * 128],
                                                                   identity=ident[:R, :R]),
                     r=["xnb%d" % nb, "ident"], w=[bk(pb)])
            src_v = psT.rearrange("p (k r) -> p k r", r=128)[:, :, 0:R]
            if n % 2 == 0:
                s.act(lambda e, src_v=src_v, cb=cb, R=R: e.activation(out=xnT[:, :, cb:cb + R], in_=src_v, func=AF.Copy),
                      r=[bk(pb)], w=[("xnT", ti)])
            else:
                s.dve(lambda e, src_v=src_v, cb=cb, R=R: e.tensor_copy(out=xnT[:, :, cb:cb + R], in_=src_v),
                      r=[bk(pb)], w=[("xnT", ti)])

        ckn = [a1([128, 256]) for i in range(2)]
        cknb = [a1([128, 256], BF16) for i in range(2)]
        ckT = [a1([128, 2, 128], BF16) for i in range(2)]
        krr = [T("krr%d" % i, [128, 32]) for i in range(2)]
        rtmp = [T("rtmp%d" % i, [128, 2, 32]) for i in range(2)]
        st1 = T("st1", [128, NT, 4])
        ssk = T("ssk", [128, NT, 8])
        rsk = T("rsk", [128, NT, 8])
        kc = [a1([128, 8, 96], BF16)] * 2
        def s1b(n, ti, R, src, cb):
            b2 = n % 2
            pA, pB, pC, pD = 2, 3, 4, 5
            res_tile = ("xnT", ti)
            for k in range(8):
                s.pe(lambda e, k=k, cb=cb, R=R: e.matmul(banks[pA][:R, 0:288], lhsT=xnT[:, k, cb:cb + R], rhs=W1[:, k, 384:672],
                                                        start=(k == 0), stop=(k == 7)),
                     r=[res_tile, rW1], w=[bk(pA)])
            s.act(lambda e, R=R, ti=ti: e.activation(out=junk2[:R, 0:256], in_=banks[pA][:R, 0:256], func=AF.Square,
                                                    accum_out=st1[:R, ti, 0:1]), r=[bk(pA)], w=["junk2", ("st1", ti)])
            s.dve(lambda e, R=R, ti=ti: e.tensor_scalar(out=st1[:R, ti, 1:2], in0=st1[:R, ti, 0:1], scalar1=1.0 / 256, scalar2=EPS,
                                                       op0=ALU.mult, op1=ALU.add), r=[("st1", ti)], w=[("st1", ti)])
            s.pool(lambda e, R=R, ti=ti: e.tensor_tensor(out=st1[:R, ti, 1:2], in0=st1[:R, ti, 1:2], in1=neghalf[:R, 0:1], op=ALU.pow),
                   r=[("st1", ti), "neghalf"], w=[("st1", ti)])
            s.dve(lambda e, R=R, ti=ti, b2=b2: e.scalar_tensor_tensor(out=ckn[b2][:R, :], in0=banks[pA][:R, 0:256], scalar=st1[:R, ti, 1:2],
                                                                     in1=gckv_bc[:R, :], op0=ALU.mult, op1=ALU.mult),
                  r=[bk(pA), ("st1", ti), "gckv_bc"], w=["ckn%d" % b2])
            s.dve(lambda e, R=R, ti=ti, b2=b2: e.tensor_tensor(out=rtmp[b2][:R, 0, :], in0=banks[pA][:R, 256:288], in1=CC[:R, ti, :], op=ALU.mult),
                  r=[bk(pA), "CC"], w=["rtmp%d" % b2])
            s.dve(lambda e, R=R, ti=ti, b2=b2: e.tensor_tensor(out=rtmp[b2][:R, 1, 0:16], in0=banks[pA][:R, 272:288], in1=SS[:R, ti, 0:16], op=ALU.mult),
                  r=[bk(pA), "SS"], w=["rtmp%d" % b2])
            s.dve(lambda e, R=R, ti=ti, b2=b2: e.tensor_tensor(out=rtmp[b2][:R, 1, 16:32], in0=banks[pA][:R, 256:272], in1=SS[:R, ti, 16:32], op=ALU.mult),
                  r=[bk(pA), "SS", "rtmp%d" % b2], w=["rtmp%d" % b2])
            s.dve(lambda e, R=R, b2=b2: e.tensor_tensor(out=krr[b2][:R, :], in0=rtmp[b2][:R, 0, :], in1=rtmp[b2][:R, 1, :], op=ALU.add),
                  r=["rtmp%d" % b2], w=["krr%d" % b2])
            if ti == 0:
                dl, dk = lat_p[0:NMETA, :], kr_p[0:NMETA, :]
            elif ti == 17:
                dl, dk = lat_s[:, :], kr_s[:, :]
            else:
                r0 = NMETA + (ti - 1) * 128
                dl, dk = lat_p[r0:r0 + 128, :], kr_p[r0:r0 + 128, :]
            wl = ["lat_s_dram"] if ti == 17 else []
            s.dma("olat%d" % b2, lambda e, dl=dl, b2=b2, R=R: e.dma_start(out=dl, in_=ckn[b2][:R, :]), r=["ckn%d" % b2], w=wl)
            wl = ["kr_s_dram"] if ti == 17 else []
            s.dma("okr%d" % b2, lambda e, dk=dk, b2=b2, R=R: e.dma_start(out=dk, in_=krr[b2][:R, :]), r=["krr%d" % b2], w=wl)

        def s2b(n, ti, R, src, cb):
            b2 = n % 2
            pA, pB, pC, pD = 2, 3, 4, 5
            res_tile = ("xnT", ti)
            s.act(lambda e, R=R, b2=b2: e.activation(out=cknb[b2][:R, :], in_=ckn[b2][:R, :], func=AF.Copy),
                  r=["ckn%d" % b2], w=["cknb%d" % b2])
            psT = bkb[pD]
            for c in range(2):
                s.pe(lambda e, c=c, R=R, b2=b2, psT=psT: e.transpose(out=psT[:, c * 128:c * 128 + R], in_=cknb[b2][:R, c * 128:(c + 1) * 128],
                                                                   identity=ident[:R, :R]),
                     r=["cknb%d" % b2, "ident"], w=[bk(pD)])
            s.dve(lambda e, R=R, b2=b2, psT=psT: e.tensor_copy(out=ckT[b2][:, :, 0:R], in_=psT[:, 0:256].rearrange("p (c r) -> p c r", r=128)[:, :, 0:R]),
                  r=[bk(pD)], w=["ckT%d" % b2])
            for half, pbk in ((0, pB), (1, pC)):
                for c in range(2):
                    s.pe(lambda e, c=c, R=R, b2=b2, half=half, pbk=pbk: e.matmul(banks[pbk][:R, :], lhsT=ckT[b2][:, c, 0:R],
                                                                                rhs=Wukv[:, c, half * 512:(half + 1) * 512],
                                                                                start=(c == 0), stop=(c == 1)),
                         r=["ckT%d" % b2, "Wukv"], w=[bk(pbk)])
            if ti != 17:
                blk = 16 if ti == 0 else ti - 1
                s.act(lambda e, R=R, blk=blk: e.activation(out=VA[:R, blk, :, 0:64], in_=banks[pC][:R, :].rearrange("p (h d) -> p h d", d=64),
                                                          func=AF.Copy), r=[bk(pC)], w=[("VA", blk)])
            s.act(lambda e, R=R: e.activation(out=junk2[:R, 0:512], in_=banks[pB][:R, :], func=AF.Square), r=[bk(pB)], w=["junk2"])
            s.dve(lambda e, R=R, ti=ti: e.tensor_reduce(out=ssk[:R, ti, :], in_=junk2[:R, 0:512].rearrange("p (h d) -> p h d", d=64),
                                                       axis=AX.X, op=ALU.add), r=["junk2"], w=[("ssk", ti)])
            s.act(lambda e, R=R, ti=ti, b2=b2: e.activation(out=junk2[:R, 512:544], in_=krr[b2][:R, :], func=AF.Square,
                                                           accum_out=st1[:R, ti, 2:3]), r=["krr%d" % b2], w=["junk2", ("st1b", ti)])
            s.dve(lambda e, R=R, ti=ti: e.tensor_scalar(out=rsk[:R, ti, :], in0=ssk[:R, ti, :], scalar1=st1[:R, ti, 2:3], scalar2=1.0 / 96,
                                                       op0=ALU.add, op1=ALU.mult), r=[("ssk", ti), ("st1b", ti)], w=[("rsk", ti)])
            s.pool(lambda e, R=R, ti=ti: e.tensor_scalar(out=rsk[:R, ti, :], in0=rsk[:R, ti, :], scalar1=EPS, scalar2=None, op0=ALU.add),
                   r=[("rsk", ti)], w=[("rsk", ti)])
            s.pool(lambda e, R=R, ti=ti: e.tensor_tensor(out=rsk[:R, ti, :], in0=rsk[:R, ti, :], in1=neghalf[:R, 0:8], op=ALU.pow),
                   r=[("rsk", ti), "neghalf"], w=[("rsk", ti)])
            s.dve(lambda e, R=R, ti=ti, b2=b2: e.tensor_tensor(out=kc[b2][:R, :, 0:64], in0=banks[pB][:R, :].rearrange("p (h d) -> p h d", d=64),
                                                              in1=rsk[:R, ti, :].unsqueeze(2).to_broadcast([R, 8, 64]), op=ALU.mult),
                  r=[bk(pB), ("rsk", ti)], w=["kc"])
            s.dve(lambda e, R=R, ti=ti, b2=b2: e.tensor_tensor(out=kc[b2][:R, :, 64:96], in0=krr[b2][:R, :].unsqueeze(1).to_broadcast([R, 8, 32]),
                                                              in1=rsk[:R, ti, :].unsqueeze(2).to_broadcast([R, 8, 32]), op=ALU.mult),
                  r=["krr%d" % b2, ("rsk", ti), "kc"], w=["kc"])
            psK = bkb[6]
            for h in range(8):
                s.pe(lambda e, h=h, R=R, b2=b2, psK=psK: e.transpose(out=psK[0:96, h * 128:h * 128 + R], in_=kc[b2][:R, h, :], identity=ident[:R, :R]),
                     r=["kc", "ident"], w=[bk(6)])
            s.act(lambda e, R=R, cb=cb, psK=psK: e.activation(out=KT[0:96, :, cb:cb + R], in_=psK[0:96, :].rearrange("p (h r) -> p h r", r=128)[:, :, 0:R],
                                                             func=AF.Copy), r=[bk(6)], w=[("KT", ti)])

        for n in range(len(tiles) + 1):
            if n < len(tiles):
                s1b(n, *tiles[n])
            if n >= 1:
                s2b(n - 1, *tiles[n - 1])

        cqb = a1([128, 3, 512], BF16)
        cqsq = a1([128, 3, 512], BF16)
        qs = [a1([128, 8, 96])] * 2
        rq = a1([128, 2, 8, 32])
        qst = T("qst", [128, NT, 2])
        ssq = T("ssq", [128, NT, 8])
        rsq = T("rsq", [128, NT, 8])
        qnb = [a1([128, 8, 96], BF16)] * 2
        qgroups = [(512 * g, 512, [(1 + 4 * g + i, 128, 128 * i) for i in range(4)]) for g in range(4)]
        qgroups.append((TP, NS, [(17, NS, 0)]))
        for gi, (c0, n, gt) in enumerate(qgroups):
            rx = [("xnT", ti) for ti, _, _ in gt]
            for c in range(3):
                for k in range(8):
                    s.pe(lambda e, c=c, k=k, c0=c0, n=n: e.matmul(banks[c][:, 0:n], lhsT=W1[:, k, c * 128:(c + 1) * 128], rhs=xnT[:, k, c0:c0 + n],
                                                                 start=(k == 0), stop=(k == 7)), r=rx + [rW1], w=[bk(c)])
                s.act(lambda e, c=c, n=n: e.activation(out=cqb[:, c, 0:n], in_=banks[c][:, 0:n], func=AF.Copy), r=[bk(c)], w=["cqb"])
                s.act(lambda e, c=c, n=n: e.activation(out=cqsq[:, c, 0:n], in_=banks[c][:, 0:n], func=AF.Square), r=[bk(c)], w=["cqsq"])
            for tn, (ti, R, lo) in enumerate(gt):
                b2 = tn % 2
                cb = c0 + lo
                for c in range(3):
                    s.pe(lambda e, c=c, R=R, lo=lo: e.matmul(banks[7][:R, 0:1], lhsT=cqsq[:, c, lo:lo + R], rhs=ones_bf[:, 0:1],
                                                            start=(c == 0), stop=(c == 2)), r=["cqsq", "ones_bf"], w=[bk(7)])
                s.dve(lambda e, R=R, ti=ti: e.tensor_scalar(out=qst[:R, ti, 0:1], in0=banks[7][:R, 0:1], scalar1=1.0 / 384, scalar2=EPS,
                                                           op0=ALU.mult, op1=ALU.add), r=[bk(7)], w=[("qst", ti)])
                s.pool(lambda e, R=R, ti=ti: e.tensor_tensor(out=qst[:R, ti, 0:1], in0=qst[:R, ti, 0:1], in1=neghalf[:R, 0:1], op=ALU.pow),
                       r=[("qst", ti), "neghalf"], w=[("qst", ti)])
                for half, (pbk, w0, wn) in enumerate(((3, 0, 512), (4, 512, 256))):
                    for c in range(3):
                        s.pe(lambda e, c=c, R=R, lo=lo, pbk=pbk, w0=w0, wn=wn: e.matmul(banks[pbk][:R, 0:wn], lhsT=cqb[:, c, lo:lo + R],
                                                                                      rhs=Wuq[:, c, w0:w0 + wn], start=(c == 0), stop=(c == 2)),
                             r=["cqb", "Wuq"], w=[bk(pbk)])
                    qsf = qs[b2][:].rearrange("p h d -> p (h d)")
                    s.act(lambda e, R=R, ti=ti, pbk=pbk, w0=w0, wn=wn, qsf=qsf: e.activation(out=qsf[:R, w0:w0 + wn], in_=banks[pbk][:R, 0:wn],
                                                                                            func=AF.Copy, scale=qst[:R, ti, 0:1]),
                          r=[bk(pbk), ("qst", ti)], w=["qs"])
                qr = qs[b2][:, :, 64:96]
                s.dve(lambda e, R=R, ti=ti, qr=qr: e.tensor_tensor(out=rq[:R, 0], in0=qr[:R], in1=CC[:R, ti, :].unsqueeze(1).to_broadcast([R, 8, 32]), op=ALU.mult),
                      r=["qs", "CC"], w=["rq"])
                s.dve(lambda e, R=R, ti=ti, qr=qr: e.tensor_tensor(out=rq[:R, 1, :, 0:16], in0=qr[:R, :, 16:32],
                                                                  in1=SS[:R, ti, 0:16].unsqueeze(1).to_broadcast([R, 8, 16]), op=ALU.mult),
                      r=["qs", "SS", "rq"], w=["rq"])
                s.dve(lambda e, R=R, ti=ti, qr=qr: e.tensor_tensor(out=rq[:R, 1, :, 16:32], in0=qr[:R, :, 0:16],
                                                                  in1=SS[:R, ti, 16:32].unsqueeze(1).to_broadcast([R, 8, 16]), op=ALU.mult),
                      r=["qs", "SS", "rq"], w=["rq"])
                s.dve(lambda e, R=R, qr=qr: e.tensor_tensor(out=qr[:R], in0=rq[:R, 0], in1=rq[:R, 1], op=ALU.add), r=["rq"], w=["qs"])
                s.act(lambda e, R=R, b2=b2: e.activation(out=junk2[:R, 0:768], in_=qs[b2][:R].rearrange("p h d -> p (h d)"), func=AF.Square),
                      r=["qs"], w=["junk2"])
                s.dve(lambda e, R=R, ti=ti: e.tensor_reduce(out=ssq[:R, ti, :], in_=junk2[:R, 0:768].rearrange("p (h d) -> p h d", d=96),
                                                           axis=AX.X, op=ALU.add), r=["junk2"], w=[("ssq", ti)])
                s.dve(lambda e, R=R, ti=ti: e.tensor_scalar(out=rsq[:R, ti, :], in0=ssq[:R, ti, :], scalar1=1.0 / 96, scalar2=EPS,
                                                           op0=ALU.mult, op1=ALU.add), r=[("ssq", ti)], w=[("rsq", ti)])
                s.pool(lambda e, R=R, ti=ti: e.tensor_tensor(out=rsq[:R, ti, :], in0=rsq[:R, ti, :], in1=neghalf[:R, 0:8], op=ALU.pow),
                       r=[("rsq", ti), "neghalf"], w=[("rsq", ti)])
                s.dve(lambda e, R=R, ti=ti, b2=b2: e.tensor_tensor(out=qnb[b2][:R], in0=qs[b2][:R],
                                                                  in1=rsq[:R, ti, :].unsqueeze(2).to_broadcast([R, 8, 96]), op=ALU.mult),
                      r=["qs", ("rsq", ti)], w=["qnb"])
                psQ = bkb[5]
                for h in range(8):
                    s.pe(lambda e, h=h, R=R, b2=b2, psQ=psQ: e.transpose(out=psQ[0:96, h * 128:h * 128 + R], in_=qnb[b2][:R, h, :], identity=ident[:R, :R]),
                         r=["qnb", "ident"], w=[bk(5)])
                s.act(lambda e, R=R, cb=cb, psQ=psQ: e.activation(out=QT[0:96, :, cb:cb + R], in_=psQ[0:96, :].rearrange("p (h r) -> p h r", r=128)[:, :, 0:R],
                                                                 func=AF.Copy, scale=gq2[:, 0:1]), r=[bk(5), "gq2"], w=[("QT", ti)])

        def warm(bank_i, n=10):
            for _ in range(n):
                s.pe(lambda e: e.matmul(banks[bank_i][:, :], lhsT=ident[:, :], rhs=Wukv[:, 0, 0:512], start=True, stop=True), r=["ident", "Wukv"], w=[bk(bank_i)])

        barrier()
        aB = Bump(84, 117)
        oagT = aB([128, 4, NTOK], BF16)
        obgT = aB([128, 4, NTOK], BF16)
        a2 = Bump(117, 150)
        PT = [a2([128, 512], BF16) for i in range(4)]
        Osb = [a2([65, 512]) for i in range(2)]
        if stop_after >= 2:
            pti = 0
            hg = 0
            pend = []

            def finalize(ob, bb, osb, rs, h, g):
                s.act(lambda e: e.activation(out=osb[:, :], in_=banks[ob][0:65, :], func=AF.Copy), r=[bk(ob)], w=[rs])
                s.dve(lambda e: e.reciprocal(out=osb[64:65, :], in_=osb[64:65, :]), r=[rs], w=[rs])
                s.pe(lambda e: e.matmul(banks[bb][0:64, :], lhsT=selrow[:, :], rhs=osb[:, :], start=True, stop=True), r=[rs, "selrow"], w=[bk(bb)])
                po = 64 * (h % 2)
                s.dve(lambda e: e.tensor_tensor(out=oagT[po:po + 64, h // 2, 512 * g:512 * g + 512], in0=osb[0:64, :], in1=banks[bb][0:64, :], op=ALU.mult),
                      r=[rs, bk(bb)], w=[("oagT", g)])

            for g in range(4):
                for h in range(8):
                    ob = 4 + hg % 2
                    bb = 6 + hg % 2
                    osb = Osb[hg % 2]
                    rs = "Osb%d" % (hg % 2)
                    hg += 1
                    blocks = [(16, NMETA, SEQ, 0, False)] + [(j, 128, j * 128, max(0, j - 4 * g) * 128, j >= 4 * g) for j in range(4 * g + 4)]
                    for bi_, (blk, nk, kc0, qlo, diag) in enumerate(blocks):
                        sb = pti % 4
                        pb_ = pti % 4
                        pti += 1
                        qn = 512 - qlo
                        tk = 0 if blk == 16 else blk + 1
                        s.pe(lambda e, sb=sb, nk=nk, kc0=kc0, qlo=qlo, qn=qn, h=h, g=g: e.matmul(
                            banks[sb][:nk, 0:qn], lhsT=KT[0:96, h, kc0:kc0 + nk], rhs=QT[0:96, h, 512 * g + qlo:512 * g + 512], start=True, stop=True),
                            r=[("KT", tk)] + [("QT", 1 + 4 * g + i) for i in range(4)], w=[bk(sb)])
                        s.act(lambda e, sb=sb, nk=nk, qlo=qlo, qn=qn, pb_=pb_: e.activation(out=PT[pb_][:nk, qlo:512], in_=banks[sb][:nk, 0:qn], func=AF.Exp,
                                                                                          scale=SCALE, bias=sbias[:nk, 0:1]),
                              r=[bk(sb), "sbias"], w=["PT%d" % pb_])
                        if diag:
                            s.pool(lambda e, nk=nk, qlo=qlo, pb_=pb_: e.tensor_tensor(out=PT[pb_][:nk, qlo:qlo + 128], in0=PT[pb_][:nk, qlo:qlo + 128],
                                                                                    in1=tri[:nk, :], op=ALU.mult), r=["PT%d" % pb_, "tri"], w=["PT%d" % pb_])
                        if len(pend) >= 3:
                            pend.pop(0)()

                        def pv(ob=ob, bb=bb, osb=osb, rs=rs, nk=nk, blk=blk, qlo=qlo, pb_=pb_, h=h, g=g, first=(bi_ == 0), last=(bi_ == len(blocks) - 1)):
                            s.pe(lambda e: e.matmul(banks[ob][0:65, qlo:512], lhsT=VA[:nk, blk, h, :], rhs=PT[pb_][:nk, qlo:512], start=first, stop=last),
                                 r=["PT%d" % pb_, ("VA", blk), "VA1"], w=[bk(ob)])
                            if last:
                                finalize(ob, bb, osb, rs, h, g)
                        pend.append(pv)
            while pend:
                pend.pop(0)()

        if dbg and stop_after == 2:
            dbgo["d_QT"] = dout("d_QT", [96, 8 * NTOK], BF16)
            s.dma("dbg", lambda e: e.dma_start(out=dbgo["d_QT"][:, :], in_=QT[0:96].rearrange("p k t -> p (k t)")), r=[("QT", t) for t in range(NT)])
            dbgo["d_oa"] = dout("d_oa", [128, 4 * NTOK], BF16)
            s.dma("dbg", lambda e: e.dma_start(out=dbgo["d_oa"][:, :], in_=oagT[:].rearrange("p k t -> p (k t)")), r=[("oagT", t) for t in range(4)])

        barrier()
        if stop_after >= 3:
            a3 = Bump(33, 84)
            a3c = Bump(117, 150)
            WukT = a3([64, 8, 256], BF16)
            qabs = a3([128, 2, NS, 8], BF16)
            qrope = a3([32, NS, 8], BF16)
            NGRP = NPAGES // 4
            ptf = a3c([128, NS * NPAGES])
            pti32 = a3c([128, NS * NPAGES], I32)
            psel = a3c([128, NS * NGRP])
            gidx = a3c([128, NS * NGRP], I32)
            pmask = a3c([128, 4])
            qcol = a3c([128, 1])
            Cf = [a3([128, 4, 288]) for i in range(3)]
            Cb = [a3([128, 4, 296], BF16) for i in range(3)]
            CTs = [a3([128, 4, 384], BF16) for i in range(2)]
            sqj = a3([128, 2048], BF16)
            sqr = a3([128, 128], BF16)
            sst = [a3([128, 4, 32]) for i in range(2)]
            pbf = [a3([128, 4, 8], BF16) for i in range(2)]
            accs = a3([8, 260])
            olb = a3([8, 256], BF16)
            olT = a3([128, 2, 8], BF16)
            for h in range(8):
                for c in range(2):
                    s.pe(lambda e, h=h, c=c: e.transpose(out=bkb[0][0:64, (h * 2 + c) * 128:(h * 2 + c) * 128 + 128] if h < 4 else
                                                         bkb[1][0:64, ((h - 4) * 2 + c) * 128:((h - 4) * 2 + c) * 128 + 128],
                                                         in_=Wukv[:, c, h * 64:(h + 1) * 64], identity=ident[:, :]), r=["Wukv", "ident"], w=[bk(0 if h < 4 else 1)])
            s.dve(lambda e: e.tensor_copy(out=WukT[:, 0:4, :].rearrange("p h c -> p (h c)"), in_=bkb[0][0:64, :]), r=[bk(0)], w=["WukT"])
            s.dve(lambda e: e.tensor_copy(out=WukT[:, 4:8, :].rearrange("p h c -> p (h c)"), in_=bkb[1][0:64, :]), r=[bk(1)], w=["WukT"])
            for h in range(8):
                for c in range(2):
                    s.pe(lambda e, h=h, c=c: e.matmul(banks[2][:, (h * 2 + c) * 4:(h * 2 + c) * 4 + 4], lhsT=WukT[:, h, c * 128:(c + 1) * 128],
                                                     rhs=QT[0:64, h, TP:TP + NS], start=True, stop=True), r=["WukT", ("QT", 17)], w=[bk(2)])
            s.dve(lambda e: e.tensor_copy(out=qabs[:].rearrange("p c b h -> p h c b"), in_=banks[2][:, 0:64].rearrange("p (h c b) -> p h c b", c=2, b=NS)),
                  r=[bk(2)], w=["qabs"])
            s.dve(lambda e: e.tensor_copy(out=qrope[:].rearrange("p b h -> p h b"), in_=QT[64:96, :, TP:TP + NS]), r=[("QT", 17)], w=["qrope"])
            s.dma("c7", lambda e: e.dma_start(out=pti32[:], in_=pt.rearrange("(o b) n -> o (b n)", o=1).partition_broadcast(128)), w=["pti32"])
            s.dve(lambda e: e.tensor_copy(out=ptf[:], in_=pti32[:]), r=["pti32"], w=["ptf"])
            s.pool(lambda e: e.memset(pmask[:], 0.0), w=["pmask"])
            for sl in range(4):
                s.pool(lambda e, sl=sl: e.memset(pmask[32 * sl:32 * sl + 32, sl:sl + 1], 1.0), r=["pmask"], w=["pmask"])
                s.pool(lambda e, sl=sl: e.iota(qcol[32 * sl:32 * sl + 32, :], pattern=[[0, 1]], base=0, channel_multiplier=1,
                                               allow_small_or_imprecise_dtypes=True), r=["qcol"], w=["qcol"])
            s.dve(lambda e: e.tensor_tensor(out=ptf[:].rearrange("p (g s) -> p g s", s=4), in0=ptf[:].rearrange("p (g s) -> p g s", s=4),
                                            in1=pmask[:].unsqueeze(1).to_broadcast([128, NS * NGRP, 4]), op=ALU.mult), r=["ptf", "pmask"], w=["ptf"])
            s.dve(lambda e: e.tensor_reduce(out=psel[:], in_=ptf[:].rearrange("p (g s) -> p g s", s=4), axis=AX.X, op=ALU.add), r=["ptf"], w=["psel"])
            s.dve(lambda e: e.tensor_scalar(out=psel[:], in0=psel[:], scalar1=32.0, scalar2=qcol[:, 0:1], op0=ALU.mult, op1=ALU.add), r=["psel", "qcol"], w=["psel"])
            s.dve(lambda e: e.tensor_copy(out=gidx[:], in_=psel[:]), r=["psel"], w=["gidx"])
            for i in range(3):
                s.pool(lambda e, i=i: e.memset(Cb[i][:, :, 256:264], 1.0), w=[("Cb", i, 0), ("Cb", i, 1)])
            sqj2 = [sqj[:, 0:1024], sqj[:, 1024:2048]]
            sst3 = [a3([128, 2, 32]) for i in range(4)]
            numsb = [a3([128, 16]) for i in range(4)]
            pbf2 = [a3([128, 2, 8], BF16) for i in range(2)]
            CT2 = [a3([128, 2, 384], BF16) for i in range(2)]

            def st_A1(u):
                b, n, hf, R, NTL, gi_, cbj = u[:7]
                if hf == 0:
                    i = gi_ % 3
                    if n < NGRP:
                        col = b * NGRP + n
                        s.dma("gc%d" % i, lambda e: e.indirect_dma_start(out=Cf[i][:].rearrange("p t c -> p (t c)"), out_offset=None, in_=ccomb[:, :],
                              in_offset=bass.IndirectOffsetOnAxis(ap=gidx[:, col:col + 1], axis=0)), r=["gidx"], w=["Cf%d" % i], q="pool")
                    else:
                        s.dma("gs%d" % i, lambda e: e.dma_start(out=Cf[i][0:1, 0, 0:256], in_=lat_s[b:b + 1, :]), r=["lat_s_dram"], w=["Cf%d" % i])
                        s.dma("gr%d" % i, lambda e: e.dma_start(out=Cf[i][0:1, 0, 256:288], in_=kr_s[b:b + 1, :]), r=["kr_s_dram", "Cf%d" % i], w=["Cf%d" % i])
                i = gi_ % 3
                t0 = 2 * hf
                rcb = ("Cb", cbj, hf)
                s.pool(lambda e: e.tensor_copy(out=Cb[cbj][:R, t0:t0 + NTL, 0:256], in_=Cf[i][:R, t0:t0 + NTL, 0:256]), r=["Cf%d" % i], w=[rcb])
                s.pool(lambda e: e.tensor_copy(out=Cb[cbj][:R, t0:t0 + NTL, 264:296], in_=Cf[i][:R, t0:t0 + NTL, 256:288]), r=["Cf%d" % i, rcb], w=[rcb])
                uj = u[7] % 2
                u3 = u[7] % 4
                s.act(lambda e: e.activation(out=sqr[:R, 0:NTL * 32].rearrange("p (t c) -> p t c", c=32), in_=Cf[i][:R, t0:t0 + NTL, 256:288], func=AF.Square),
                      r=["Cf%d" % i], w=["sqr"])
                s.dve(lambda e: e.tensor_reduce(out=sst3[u3][:R, 0:NTL, 8:9], in_=sqr[:R, 0:NTL * 32].rearrange("p (t c) -> p t c", c=32), axis=AX.X, op=ALU.add),
                      r=["sqr"], w=["sst%d" % u3])
                for t in range(NTL):
                    to = t * 384
                    for c in range(2):
                        s.pe(lambda e, c=c, t=t, to=to: e.transpose(out=bkb[uj][:, to + c * 128:to + c * 128 + R], in_=Cb[cbj][:R, t0 + t, c * 128:(c + 1) * 128],
                                                                    identity=ident[:R, :R]), r=[rcb, "ident"], w=[bk(uj)])
                    s.pe(lambda e, t=t, to=to: e.transpose(out=bkb[uj][0:32, to + 256:to + 256 + R], in_=Cb[cbj][:R, t0 + t, 264:296], identity=ident[:R, :R]),
                         r=[rcb, "ident"], w=[bk(uj)])
                pv_ = bkb[uj][:, 0:384 * NTL].rearrange("p (t c) -> p t c", c=384)
                s.dve(lambda e: e.tensor_copy(out=CT2[uj][:, 0:NTL, 0:256].rearrange("p t (c r) -> p t c r", r=128)[:, :, :, 0:R],
                                              in_=pv_[:, :, 0:256].rearrange("p t (c r) -> p t c r", r=128)[:, :, :, 0:R]),
                      r=[bk(uj)], w=["CT2%d" % uj])
                s.dve(lambda e: e.tensor_copy(out=CT2[uj][0:32, 0:NTL, 256:256 + R], in_=pv_[0:32, :, 256:256 + R]),
                      r=[bk(uj), "CT2%d" % uj], w=["CT2%d" % uj])

            def st_A2(u):
                b, n, hf, R, NTL, gi_, cbj = u[:7]
                uj = u[7] % 2
                u3 = u[7] % 4
                kb0 = 4 + 2 * uj
                for t in range(NTL):
                    for c in range(2):
                        s.pe(lambda e, c=c, t=t: e.matmul(banks[kb0 + t][:R, :], lhsT=CT2[uj][:, t, c * 128:c * 128 + R], rhs=Wukv[:, c, 0:512],
                                                         start=(c == 0), stop=(c == 1)), r=["CT2%d" % uj, "Wukv"], w=[bk(kb0 + t)])
                for t in range(NTL):
                    nc0 = u3 * 16 + t * 8
                    for c in range(2):
                        s.pe(lambda e, c=c, t=t, nc0=nc0: e.matmul(banks[3][:R, nc0:nc0 + 8], lhsT=CT2[uj][:, t, c * 128:c * 128 + R], rhs=qabs[:, c, b, :],
                                                                  start=(c == 0), stop=False), r=["CT2%d" % uj, "qabs"], w=[bk(3)])
                    s.pe(lambda e, t=t, nc0=nc0: e.matmul(banks[3][:R, nc0:nc0 + 8], lhsT=CT2[uj][0:32, t, 256:256 + R], rhs=qrope[:, b, :],
                                                         start=False, stop=True), r=["CT2%d" % uj, "qrope"], w=[bk(3)])
                s.dve(lambda e: e.tensor_copy(out=numsb[u3][:R, 0:8 * NTL], in_=banks[3][:R, u3 * 16:u3 * 16 + 8 * NTL]), r=[bk(3)], w=[("numsb", u3)])
                s.act(lambda e: e.activation(out=sqj2[uj][:R, 0:512 * NTL], in_=psum_all[:R, 512 * kb0:512 * (kb0 + NTL)], func=AF.Square),
                      r=[bk(kb0 + t) for t in range(NTL)], w=["sqj%d" % uj])
                s.dve(lambda e: e.tensor_reduce(out=sst3[u3][:R, 0:NTL, 0:8], in_=sqj2[uj][:R, 0:512 * NTL].rearrange("p (t h d) -> p t h d", h=8, d=64),
                                                axis=AX.X, op=ALU.add), r=["sqj%d" % uj], w=["sst%d" % u3])
                sv_ = sst3[u3]
                s.dve(lambda e: e.tensor_tensor(out=sv_[:R, 0:NTL, 16:24], in0=sv_[:R, 0:NTL, 0:8], in1=sv_[:R, 0:NTL, 8:9].to_broadcast([R, NTL, 8]), op=ALU.add),
                      r=["sst%d" % u3], w=["sst%d" % u3])

            def st_B1(u):
                b, n, hf, R, NTL, gi_, cbj = u[:7]
                uj = u[7] % 2
                u3 = u[7] % 4
                rs_ = "sst%d" % u3
                sv = sst3[u3]
                s.act(lambda e: e.activation(out=sv[:R, 0:NTL, 16:24], in_=sv[:R, 0:NTL, 16:24], func=AF.Ln, bias=epsc[:R, 0:1], scale=1.0 / 96), r=[rs_, "epsc"], w=[rs_])
                s.act(lambda e: e.activation(out=sv[:R, 0:NTL, 16:24], in_=sv[:R, 0:NTL, 16:24], func=AF.Exp, scale=-0.5), r=[rs_], w=[rs_])
                s.dve(lambda e: e.tensor_tensor(out=sv[:R, 0:NTL, 24:32], in0=numsb[u3][:R, 0:8 * NTL].rearrange("p (t h) -> p t h", h=8),
                                                in1=sv[:R, 0:NTL, 16:24], op=ALU.mult), r=[("numsb", u3), rs_], w=[rs_])
                s.act(lambda e: e.activation(out=pbf2[uj][:R, 0:NTL, :], in_=sv[:R, 0:NTL, 24:32], func=AF.Exp, scale=SCALE, bias=sbias[:R, 0:1]),
                      r=[rs_, "sbias"], w=["pbf%d" % uj])

            def st_B2(u):
                b, n, hf, R, NTL, gi_, cbj = u[:7]
                uj = u[7] % 2
                t0 = 2 * hf
                for t in range(NTL):
                    s.pe(lambda e, t=t, first=(n == 0 and hf == 0 and t == 0), last=(n == NGRP): e.matmul(
                        banks[2][0:8, 0:257], lhsT=pbf2[uj][:R, t, :], rhs=Cb[cbj][:R, t0 + t, 0:257], start=first, stop=last, skip_group_check=True),
                        r=["pbf%d" % uj, ("Cb", cbj, hf)], w=[bk(2)])

            units = []
            gcount = 0
            for b in range(NS):
                for n in range(NGRP + 1):
                    for hf in range(2 if n < NGRP else 1):
                        R, NTL = (128, 2) if n < NGRP else (1, 1)
                        units.append((b, n, hf, R, NTL, gcount, gcount % 3, len(units)))
                    gcount += 1
            for b in range(NS):
                ub = [u for u in units if u[0] == b]
                for k in range(len(ub) + 3):
                    if 3 <= k:
                        st_B1(ub[k - 3])
                    if k < len(ub):
                        st_A1(ub[k])
                    if 1 <= k <= len(ub):
                        st_A2(ub[k - 1])
                    if 3 <= k:
                        st_B2(ub[k - 3])
                s.act(lambda e: e.activation(out=accs[:, 0:257], in_=banks[2][0:8, 0:257], func=AF.Copy), r=[bk(2)], w=["accs"])
                s.dve(lambda e: e.reciprocal(out=accs[:, 258:259], in_=accs[:, 256:257]), r=["accs"], w=["accs"])
                s.dve(lambda e: e.tensor_scalar(out=olb[:, :], in0=accs[:, 0:256], scalar1=accs[:, 258:259], scalar2=None, op0=ALU.mult), r=["accs"], w=["olb"])
                for c in range(2):
                    s.pe(lambda e, c=c: e.transpose(out=bkb[0][:, c * 8:c * 8 + 8], in_=olb[:, c * 128:(c + 1) * 128], identity=ident[0:8, 0:8]), r=["olb", "ident"], w=[bk(0)])
                s.dve(lambda e: e.tensor_copy(out=olT[:].rearrange("p c h -> p (c h)"), in_=bkb[0][:, 0:16]), r=[bk(0)], w=["olT"])
                for jj in range(4):
                    for c in range(2):
                        s.pe(lambda e, jj=jj, c=c: e.matmul(banks[1][:, jj * 8:jj * 8 + 8], lhsT=Wukv[:, c, 512 + jj * 128:512 + (jj + 1) * 128], rhs=olT[:, c, :],
                                                           start=(c == 0), stop=(c == 1)), r=["olT", "Wukv"], w=[bk(1)])
                for jj in range(4):
                    s.dve(lambda e, jj=jj, b=b: e.tensor_copy(out=oagT[0:64, jj, TP + b:TP + b + 1], in_=banks[1][0:64, jj * 8 + 2 * jj:jj * 8 + 2 * jj + 1]),
                          r=[bk(1)], w=[("oagT", 4)])
                    s.dve(lambda e, jj=jj, b=b: e.tensor_copy(out=oagT[64:128, jj, TP + b:TP + b + 1], in_=banks[1][64:128, jj * 8 + 2 * jj + 1:jj * 8 + 2 * jj + 2]),
                          r=[bk(1)], w=[("oagT", 4)])

        barrier()
        if stop_after >= 4:
            a4 = Bump(0, 84)
            a4c = Bump(117, 150)
            wstf = [a4c([128, 2048]) for i in range(2)]
            W4 = a4([128, 8, 2560], BF16)
            rW4 = load_w_in(W4, 672, 3232, wstf, engs=("pool", "dve"))
            OBQ, OBF, OBI, OGA, OGB = 0, 512, 1024, 1536, 2048
            qS = a4([128, 4, 512], BF16); fS = a4([128, 4, 512]); kS = a4([128, 4, 512]); thb = a4([128, 512])
            vbf = [a4([128, 512], BF16) for i in range(2)]
            sgb = [a4([128, 512], BF16) for i in range(2)]
            PA = a4([128, 4, 128]); rPA = a4([128, 4, 128]); rk4 = a4([128, 4, 128])
            Pt4s = [a4([128, 4, 2]) for i in range(2)]
            qe4 = a4([128, 4, 128], BF16); ke4 = a4([128, 4, 128], BF16); qPt4 = a4([128, 4, 128], BF16); kd24 = a4([128, 4, 128], BF16)
            qPB4 = a4([128, 4, 64], BF16); kdA4 = a4([128, 4, 64], BF16); am4 = a4([128, 4, 128], BF16); kdt4 = a4([128, 4, 128], BF16)
            S32 = a4([128, 4, 128]); Sbf = a4([128, 4, 128], BF16)
            zer = a4([128, 64])
            hst = a4([128, 8]); otmp = a4([128, 512], BF16)
            s.pool(lambda e: e.memset(zer[:], 0.0), w=["zer"])
            s.pool(lambda e: e.memset(S32[:], 0.0), w=[("S32", h) for h in range(4)])
            s.pool(lambda e: e.memset(Sbf[:], 0.0), w=[("Sbf", h) for h in range(4)])
            Sold = a4c([128, NS, 4, 128])
            vsb = a4c([NS, 512])
            onesel = a4c([NS, NS, 128])
            qsel = a4c([128, 4, NS, NS])
            Snew = [a4c([128, 128]) for i in range(2)]
            stmp = a4c([128, 128])
            junk4 = a4c([128, 512], BF16)
            sga = a4c([128, 512], BF16)
            obg = a4c([128, 512], BF16)
            for b in range(NS):
                s.dma("so%d" % b, lambda e, b=b: e.dma_start(out=Sold[:, b], in_=st_in[b].rearrange("h k v -> k h v")), w=[("Sold", b)])
                s.dve(lambda e, b=b: e.tensor_copy(out=onesel[:, b, :], in_=identf[0:NS, b:b + 1].to_broadcast([NS, 128])), r=["identf"], w=["onesel"])
            s.pool(lambda e: e.memset(qsel[:], 0.0), w=["qsel"])

            hgroups = [(SEQ, NMETA, [(0, NMETA, 0)], "meta")]
            hgroups += [(512 * g, 512, [(1 + 4 * g + i, 128, 128 * i) for i in range(4)], "x") for g in range(4)]
            hgroups.append((TP, NS, [(17, NS, 0)], "sample"))
            first_chunk = True
            tn = 0
            for gi, (c0, n, gt, kind) in enumerate(hgroups):
                rx = [("xnT", ti) for ti, _, _ in gt]
                for h in range(4):
                    for (off, pb_, which) in ((OBQ, 0, "q"), (OBF, 1, "f")):
                        for k in range(8):
                            s.pe(lambda e, k=k, h=h, off=off, pb_=pb_, c0=c0, n=n: e.matmul(banks[pb_][:, 0:n], lhsT=W4[:, k, off + h * 128:off + (h + 1) * 128],
                                                                                          rhs=xnT[:, k, c0:c0 + n], start=(k == 0), stop=(k == 7)),
                                 r=rx + [rW4], w=[bk(pb_)])
                        if which == "q":
                            s.act(lambda e, h=h, n=n: e.activation(out=qS[:, h, 0:n], in_=banks[0][:, 0:n], func=AF.Silu), r=[bk(0)], w=["qS"])
                        else:
                            s.act(lambda e, n=n: e.activation(out=thb[:, 0:n], in_=banks[1][:, 0:n], func=AF.Tanh, scale=0.5), r=[bk(1)], w=["thb"])
                            s.dve(lambda e, h=h, n=n: e.tensor_scalar(out=fS[:, h, 0:n], in0=thb[:, 0:n], scalar1=hg_nb[:, h:h + 1], scalar2=hg_a[:, h:h + 1],
                                                                     op0=ALU.mult, op1=ALU.add), r=["thb", "hg_nb", "hg_a"], w=["fS"])
                            s.dve(lambda e, h=h, n=n: e.tensor_scalar(out=kS[:, h, 0:n], in0=thb[:, 0:n], scalar1=hg_nnb[:, h:h + 1], scalar2=hg_nb[:, h:h + 1],
                                                                     op0=ALU.mult, op1=ALU.add), r=["thb", "hg_nb", "hg_nnb"], w=["kS"])
                for (ti, R, lo) in gt:
                    b2 = tn % 2
                    tn += 1
                    cb = c0 + lo
                    for (off, pb_) in ((OBI, 2), (OGB, 3)):
                        for k in range(8):
                            s.pe(lambda e, k=k, off=off, pb_=pb_, cb=cb, R=R: e.matmul(banks[pb_][:R, :], lhsT=xnT[:, k, cb:cb + R], rhs=W4[:, k, off:off + 512],
                                                                                     start=(k == 0), stop=(k == 7)), r=[("xnT", ti), rW4], w=[bk(pb_)])
                    if kind != "sample":
                        s.act(lambda e, R=R, b2=b2: e.activation(out=vbf[b2][:R, :], in_=banks[2][:R, :], func=AF.Copy), r=[bk(2)], w=["vbf%d" % b2])
                    else:
                        s.act(lambda e, R=R: e.activation(out=vsb[:R, :], in_=banks[2][:R, :], func=AF.Copy), r=[bk(2)], w=["vsb"])
                    if kind != "meta":
                        s.act(lambda e, R=R, b2=b2: e.activation(out=sgb[b2][:R, :], in_=banks[3][:R, :], func=AF.Silu), r=[bk(3)], w=["sgb%d" % b2])
                    if kind != "sample":
                        chunks = [(0, R)] if kind == "meta" else [(0, 64), (64, 64)]
                        tpar = tn % 2
                        Pt4 = Pt4s[tpar]
                        rPt = "Pt4_%d" % tpar
                        fv = fS[:, :, lo:lo + R]; qv = qS[:, :, lo:lo + R]; kv = kS[:, :, lo:lo + R]
                        for h in range(4):
                            for (cs, L) in chunks:
                                s.dve(lambda e, cs=cs, L=L, h=h, lo=lo: e.tensor_tensor_scan(out=PA[:, h, cs:cs + L], data0=fS[:, h, lo + cs:lo + cs + L], data1=zer[:, 0:L],
                                                                                          initial=1.0, op0=ALU.mult, op1=ALU.add), r=["fS", "zer"], w=["PA"])
                        s.dve(lambda e, R=R: e.reciprocal(out=rPA[:, :, 0:R], in_=PA[:, :, 0:R]), r=["PA"], w=["rPA"])
                        if kind == "meta":
                            s.dve(lambda e, R=R, Pt4=Pt4: e.tensor_copy(out=Pt4[:, :, 0:1], in_=PA[:, :, R - 1:R]), r=["PA"], w=[rPt])
                            s.dve(lambda e, R=R, kv=kv: e.tensor_tensor(out=rk4[:, :, 0:R], in0=rPA[:, :, 0:R], in1=kv, op=ALU.mult), r=["rPA", "kS"], w=["rk4"])
                            s.dve(lambda e, R=R, Pt4=Pt4: e.tensor_tensor(out=kd24[:, :, 0:R], in0=rk4[:, :, 0:R], in1=Pt4[:, :, 0:1].to_broadcast([128, 4, R]), op=ALU.mult),
                                  r=["rk4", rPt], w=["kd24"])
                        else:
                            c4 = lambda ap: ap.rearrange("p h (c t) -> p h c t", t=64)
                            s.dve(lambda e, Pt4=Pt4: e.tensor_tensor(out=Pt4[:, :, 0:1], in0=PA[:, :, 63:64], in1=PA[:, :, 127:128], op=ALU.mult), r=["PA"], w=[rPt])
                            s.dve(lambda e, Pt4=Pt4: e.tensor_copy(out=Pt4[:, :, 1:2], in_=PA[:, :, 127:128]), r=["PA", rPt], w=[rPt])
                            s.dve(lambda e, qv=qv: e.tensor_tensor(out=qPt4[:, :, :], in0=PA[:, :, :], in1=qv, op=ALU.mult), r=["PA", "qS"], w=["qPt4"])
                            s.dve(lambda e: e.tensor_tensor(out=c4(qe4[:, :, :]), in0=c4(qPt4[:, :, :]), in1=c4(rPA[:, :, :])[:, :, :, 31:32].to_broadcast([128, 4, 2, 64]), op=ALU.mult),
                                  r=["qPt4", "rPA"], w=["qe4"])
                            s.dve(lambda e: e.tensor_copy(out=qPB4[:, :, :], in_=qPt4[:, :, 64:128]), r=["qPt4"], w=["qPB4"])
                            s.dve(lambda e: e.tensor_tensor(out=qPt4[:, :, 64:128], in0=qPB4[:, :, :], in1=PA[:, :, 63:64].to_broadcast([128, 4, 64]), op=ALU.mult),
                                  r=["qPB4", "PA", "qe4"], w=["qPt4"])
                            s.dve(lambda e, kv=kv: e.tensor_tensor(out=rk4[:, :, :], in0=rPA[:, :, :], in1=kv, op=ALU.mult), r=["rPA", "kS"], w=["rk4"])
                            s.dve(lambda e: e.tensor_tensor(out=c4(ke4[:, :, :]), in0=c4(rk4[:, :, :]), in1=c4(PA[:, :, :])[:, :, :, 31:32].to_broadcast([128, 4, 2, 64]), op=ALU.mult),
                                  r=["rk4", "PA"], w=["ke4"])
                            s.dve(lambda e: e.tensor_tensor(out=kdA4[:, :, :], in0=rk4[:, :, 0:64], in1=PA[:, :, 63:64].to_broadcast([128, 4, 64]), op=ALU.mult),
                                  r=["rk4", "PA"], w=["kdA4"])
                            s.dve(lambda e, Pt4=Pt4: e.tensor_tensor(out=c4(kd24[:, :, :]), in0=c4(rk4[:, :, :]), in1=Pt4[:, :, 0:2].unsqueeze(3).to_broadcast([128, 4, 2, 64]), op=ALU.mult),
                                  r=["rk4", rPt], w=["kd24"])
                        for h in range(4):
                            if kind != "meta":
                                s.pe(lambda e, h=h: e.matmul(banks[4][:, h * 128:(h + 1) * 128], lhsT=ke4[:, h, :], rhs=qe4[:, h, :], start=True, stop=True),
                                     r=["ke4", "qe4"], w=[bk(4)])
                                s.pe(lambda e, h=h: e.matmul(banks[4][0:64, h * 128 + 64:(h + 1) * 128], lhsT=kdA4[:, h, :], rhs=qPB4[:, h, :], start=True, stop=True,
                                                                   skip_group_check=True), r=["kdA4", "qPB4"], w=[bk(4)])
                                s.dve(lambda e, h=h: e.tensor_tensor(out=am4[:, h, :], in0=banks[4][:, h * 128:(h + 1) * 128], in1=tri[:, :], op=ALU.mult),
                                      r=[bk(4), "tri"], w=["am%d" % h])
                            s.pe(lambda e, h=h, R=R: e.transpose(out=bkb[5][:R, h * 128:(h + 1) * 128], in_=kd24[:, h, 0:R], identity=ident[:, :]),
                                 r=["kd24", "ident"], w=[bk(5)])
                            s.act(lambda e, h=h, R=R: e.activation(out=kdt4[:R, h, :], in_=bkb[5][:R, h * 128:(h + 1) * 128], func=AF.Copy), r=[bk(5)], w=["kdt%d" % h])
                        if kind != "meta":
                            for h in range(4):
                                s.pe(lambda e, h=h, b2=b2: e.matmul(banks[6][:, h * 128:(h + 1) * 128], lhsT=am4[:, h, :], rhs=vbf[b2][:, h * 128:(h + 1) * 128],
                                                                          start=True, stop=False), r=["am%d" % h, "vbf%d" % b2], w=[bk(6)])
                                s.pe(lambda e, h=h: e.matmul(banks[6][:, h * 128:(h + 1) * 128], lhsT=qPt4[:, h, :], rhs=Sbf[:, h, :], start=False, stop=True),
                                     r=["qPt4", ("Sbf", h)], w=[bk(6)])
                        for h in range(4):
                            s.pe(lambda e, h=h, b2=b2, R=R: e.matmul(banks[7][:, h * 128:(h + 1) * 128], lhsT=kdt4[:R, h, :], rhs=vbf[b2][:R, h * 128:(h + 1) * 128],
                                                                           start=True, stop=True), r=["kdt%d" % h, "vbf%d" % b2], w=[bk(7)])
                            s.dve(lambda e, h=h, Pt4=Pt4: e.scalar_tensor_tensor(out=S32[:, h, :], in0=S32[:, h, :], scalar=Pt4[:, h, 0:1], in1=banks[7][:, h * 128:(h + 1) * 128],
                                                                              op0=ALU.mult, op1=ALU.add), r=[("S32", h), rPt, bk(7)], w=[("S32", h)])
                            s.act(lambda e, h=h: e.activation(out=Sbf[:, h, :], in_=S32[:, h, :], func=AF.Copy), r=[("S32", h)], w=[("Sbf", h)])
                    else:
                        for b in range(NS):
                            s.dve(lambda e, b=b: e.tensor_copy(out=qsel[:, :, b, b], in_=qS[:, :, b]), r=["qS", "qsel"], w=["qsel"])
                        for b in range(NS):
                            s.pe(lambda e, b=b: e.matmul(banks[4][:, :], lhsT=onesel[:, b, :], rhs=vsb[:, :], start=True, stop=True), r=["onesel", "vsb"], w=[bk(4)])
                            for h in range(4):
                                sn = Snew[(b * 4 + h) % 2]
                                rs = "Snew%d" % ((b * 4 + h) % 2)
                                s.dve(lambda e, b=b, h=h: e.tensor_scalar(out=stmp[:, :], in0=Sold[:, b, h, :], scalar1=fS[:, h, b:b + 1], scalar2=None, op0=ALU.mult),
                                      r=[("Sold", b), "fS"], w=["stmp"])
                                s.dve(lambda e, b=b, h=h, sn=sn: e.scalar_tensor_tensor(out=sn[:, :], in0=banks[4][:, h * 128:(h + 1) * 128], scalar=kS[:, h, b:b + 1], in1=stmp[:, :],
                                                                                       op0=ALU.mult, op1=ALU.add), r=[bk(4)] + ["kS", "stmp"], w=[rs])
                                s.dma("hs%d" % ((b * 4 + h) % 2), lambda e, b=b, h=h, sn=sn: e.dma_start(out=hg_s[b, h], in_=sn[:, :]), r=[rs])
                                s.pe(lambda e, b=b, h=h, sn=sn: e.matmul(banks[6][0:NS, h * 128:(h + 1) * 128], lhsT=qsel[:, h, b, :], rhs=sn[:, :],
                                                                        start=(b == 0 and h == 0), stop=(b == NS - 1 and h == 3), skip_group_check=True), r=["qsel", rs], w=[bk(6)])
                    if kind == "meta":
                        continue
                    s.act(lambda e, R=R: e.activation(out=junk4[:R, :], in_=banks[6][:R, :], func=AF.Square), r=[bk(6)], w=["junk4"])
                    s.dve(lambda e, R=R: e.tensor_reduce(out=hst[:R, 0:4], in_=junk4[:R, :].rearrange("p (h d) -> p h d", d=128), axis=AX.X, op=ALU.add), r=["junk4"], w=["hst"])
                    s.dve(lambda e, R=R: e.tensor_scalar(out=hst[:R, 4:8], in0=hst[:R, 0:4], scalar1=1.0 / 128, scalar2=EPS, op0=ALU.mult, op1=ALU.add), r=["hst"], w=["hst"])
                    s.pool(lambda e, R=R: e.tensor_tensor(out=hst[:R, 4:8], in0=hst[:R, 4:8], in1=neghalf[:R, 0:4], op=ALU.pow), r=["hst", "neghalf"], w=["hst"])
                    s.dve(lambda e, R=R: e.tensor_tensor(out=otmp[:R, :].rearrange("p (h d) -> p h d", d=128), in0=banks[6][:R, :].rearrange("p (h d) -> p h d", d=128),
                                                        in1=hst[:R, 4:8].unsqueeze(2).to_broadcast([R, 4, 128]), op=ALU.mult), r=[bk(6), "hst"], w=["otmp"])
                    s.dve(lambda e, R=R, b2=b2: e.tensor_tensor(out=obg[:R, :], in0=otmp[:R, :], in1=sgb[b2][:R, :], op=ALU.mult), r=["otmp", "sgb%d" % b2], w=["obg"])
                    for c in range(4):
                        s.pe(lambda e, R=R, c=c: e.transpose(out=bkb[5][:, c * 128:c * 128 + R], in_=obg[:R, c * 128:(c + 1) * 128], identity=ident[:R, :R]), r=["obg", "ident"], w=[bk(5)])
                    s.act(lambda e, R=R, cb=cb: e.activation(out=obgT[:, :, cb:cb + R], in_=bkb[5][:, 0:512].rearrange("p (c r) -> p c r", r=128)[:, :, 0:R], func=AF.Copy),
                          r=[bk(5)], w=[("obgT", gi)])
                if kind != "meta":
                    for c in range(4):
                        for k in range(8):
                            s.pe(lambda e, k=k, c=c, c0=c0, n=n: e.matmul(banks[c % 2][:, 0:n], lhsT=W4[:, k, OGA + c * 128:OGA + (c + 1) * 128], rhs=xnT[:, k, c0:c0 + n],
                                                                         start=(k == 0), stop=(k == 7)), r=rx + [rW4], w=[bk(c % 2)])
                        s.act(lambda e, c=c, n=n: e.activation(out=sga[:, 0:n], in_=banks[c % 2][:, 0:n], func=AF.Silu), r=[bk(c % 2)], w=["sga"])
                        s.dve(lambda e, c=c, n=n, c0=c0: e.tensor_tensor(out=oagT[:, c, c0:c0 + n], in0=oagT[:, c, c0:c0 + n], in1=sga[:, 0:n], op=ALU.mult),
                              r=["sga", ("oagT", gi - 1)], w=[("oagT", gi - 1)])
            s.dma("hp", lambda e: e.dma_start(out=hg_p.rearrange("h k v -> k h v"), in_=S32[:, :, :]), r=[("S32", h) for h in range(4)])

        barrier()
        if stop_after >= 5:
            a5 = Bump(0, 84)
            a5c = Bump(117, 150)
            wstf = [a5c([128, 2048]) for i in range(2)]
            W5 = a5([128, 8, 2048], BF16)
            rW5 = load_w_in(W5, 3232, 5280, wstf, engs=("pool", "dve"))
            Woa = a5([128, 4, 1024], BF16); Wob = a5([128, 4, 1024], BF16); Wo = a5([128, 8, 1024], BF16)
            zT = a5([128, 8, 512], BF16)
            tha = a5([128, 512]); thm = a5([128, 512]); t1 = a5([128, 512])
            xin = [a5c([128, 1024]) for i in range(2)]
            yout = [a5c([128, 1024]) for i in range(2)]
            wi = 0
            for (src_w, dst_w, nk, kind) in ((w_oa, Woa, 4, "plain"), (w_ob, Wob, 4, "gbn"), (w_o, Wo, 8, "half")):
                for k in range(nk):
                    for hf in range(2):
                        b = wi % 2
                        wi += 1
                        stv = wstf[b][:, 0:512]
                        s.dma("wst%d" % b, lambda e, stv=stv, src_w=src_w, k=k, hf=hf: e.dma_start(out=stv, in_=src_w[k * 128:(k + 1) * 128, hf * 512:(hf + 1) * 512]), w=["wst%d" % b])
                        dstv = dst_w[:, k, hf * 512:(hf + 1) * 512]
                        rsw = ("Wm", id(dst_w))
                        sc = None if kind == "plain" else (gbn_c[:, 0:1] if kind == "gbn" else 0.5)
                        eng = ("pool", "act", "dve")[wi % 3]
                        if eng == "act":
                            if sc is None:
                                s.act(lambda e, stv=stv, dstv=dstv: e.activation(out=dstv, in_=stv, func=AF.Copy), r=["wst%d" % b], w=[rsw])
                            else:
                                s.act(lambda e, stv=stv, dstv=dstv, sc=sc: e.activation(out=dstv, in_=stv, func=AF.Copy, scale=sc), r=["wst%d" % b, "gbn_c"], w=[rsw])
                        else:
                            if sc is None:
                                s.add(eng, lambda e, stv=stv, dstv=dstv: e.tensor_copy(out=dstv, in_=stv), ["wst%d" % b], [rsw])
                            else:
                                s.add(eng, lambda e, stv=stv, dstv=dstv, sc=sc: e.tensor_scalar(out=dstv, in0=stv, scalar1=sc, scalar2=None, op0=ALU.mult),
                                      ["wst%d" % b, "gbn_c"], [rsw])
            rWoa, rWob, rWo = ("Wm", id(Woa)), ("Wm", id(Wob)), ("Wm", id(Wo))
            mgroups = [(512 * g, 512, [(1 + 4 * g + i, 128, 128 * i) for i in range(4)], g, g + 1) for g in range(4)]
            mgroups.append((TP, NS, [(17, NS, 0)], 4, 5))
            tn = 0
            for (c0, n, gt, og, hgi) in mgroups:
                rx = [("xnT", ti) for ti, _, _ in gt]
                for c in range(8):
                    bs = 4 * (c % 2)
                    for j in range(4):
                        s.pe(lambda e, c=c, j=j, c0=c0, n=n, bs=bs: e.matmul(banks[bs][:, 0:n], lhsT=Woa[:, j, c * 128:(c + 1) * 128], rhs=oagT[:, j, c0:c0 + n], start=(j == 0), stop=(j == 3)),
                             r=[rWoa, ("oagT", og)], w=[bk(bs)])
                    for j in range(4):
                        s.pe(lambda e, c=c, j=j, c0=c0, n=n, bs=bs: e.matmul(banks[bs + 1][:, 0:n], lhsT=Wob[:, j, c * 128:(c + 1) * 128], rhs=obgT[:, j, c0:c0 + n], start=(j == 0), stop=(j == 3)),
                             r=[rWob, ("obgT", hgi)], w=[bk(bs + 1)])
                    for (off, pb_) in ((0, bs + 2), (1024, bs + 3)):
                        for k in range(8):
                            s.pe(lambda e, c=c, k=k, off=off, pb_=pb_, c0=c0, n=n: e.matmul(banks[pb_][:, 0:n], lhsT=W5[:, k, off + c * 128:off + (c + 1) * 128], rhs=xnT[:, k, c0:c0 + n],
                                                                                          start=(k == 0), stop=(k == 7)), r=rx + [rW5], w=[bk(pb_)])
                    s.act(lambda e, n=n, bs=bs: e.activation(out=tha[:, 0:n], in_=banks[bs + 2][:, 0:n], func=AF.Tanh, scale=0.5), r=[bk(bs + 2)], w=["tha"])
                    s.act(lambda e, n=n, bs=bs: e.activation(out=thm[:, 0:n], in_=banks[bs + 3][:, 0:n], func=AF.Tanh, scale=0.5), r=[bk(bs + 3)], w=["thm"])
                    s.dve(lambda e, n=n, bs=bs: e.scalar_tensor_tensor(out=t1[:, 0:n], in0=tha[:, 0:n], scalar=1.0, in1=banks[bs][:, 0:n], op0=ALU.add, op1=ALU.mult), r=["tha", bk(bs)], w=["t1"])
                    s.dve(lambda e, n=n, bs=bs: e.scalar_tensor_tensor(out=thm[:, 0:n], in0=thm[:, 0:n], scalar=1.0, in1=banks[bs + 1][:, 0:n], op0=ALU.add, op1=ALU.mult), r=["thm", bk(bs + 1)], w=["thm"])
                    s.dve(lambda e, c=c, n=n: e.tensor_tensor(out=zT[:, c, 0:n], in0=t1[:, 0:n], in1=thm[:, 0:n], op=ALU.add), r=["t1", "thm"], w=["zT"])
                for (ti, R, lo) in gt:
                    b2 = tn % 2
                    tn += 1
                    srcx = xs[:, :] if ti == 17 else xp[(ti - 1) * 128:ti * 128, :]
                    dsty = y_s[:, :] if ti == 17 else y_p[(ti - 1) * 128:ti * 128, :]
                    s.dma("xin%d" % b2, lambda e, b2=b2, R=R, srcx=srcx: e.dma_start(out=xin[b2][:R, :], in_=srcx), w=["xin%d" % b2])
                    for hf in range(2):
                        pb_ = 4 + hf
                        for c in range(8):
                            s.pe(lambda e, c=c, hf=hf, pb_=pb_, R=R, lo=lo: e.matmul(banks[pb_][:R, :], lhsT=zT[:, c, lo:lo + R], rhs=Wo[:, c, hf * 512:(hf + 1) * 512],
                                                                                   start=(c == 0), stop=(c == 7)), r=["zT", rWo], w=[bk(pb_)])
                        s.dve(lambda e, hf=hf, pb_=pb_, R=R, b2=b2: e.tensor_tensor(out=yout[b2][:R, hf * 512:(hf + 1) * 512], in0=banks[pb_][:R, :], in1=xin[b2][:R, hf * 512:(hf + 1) * 512],
                                                                                  op=ALU.add), r=[bk(pb_), "xin%d" % b2], w=["yout%d" % b2])
                    s.dma("yo%d" % b2, lambda e, b2=b2, R=R, dsty=dsty: e.dma_start(out=dsty, in_=yout[b2][:R, :]), r=["yout%d" % b2])

        s.emit(st)
    return nc


_NC = {}


def _get_nc(dbg=False):
    if dbg not in _NC:
        _NC[dbg] = build(dbg)
    return _NC[dbg]


def make_in_maps(x_prompt, x_sample, cache_latent, cache_krope, state_hgrn, page_table, meta_tokens,
                 norm_g, w_in, g_cq, w_uq, g_ckv, w_uk, w_uv, g_qn, g_kn, lb_logits, g_bn, w_oa, w_ob, w_o):
    f = lambda a: np.ascontiguousarray(np.asarray(a, dtype=np.float32))
    ccomb = np.concatenate([f(cache_latent)[0].reshape(NPOOL * 128, 256), f(cache_krope)[0].reshape(NPOOL * 128, 32)],
                           axis=1).reshape(NPOOL * 32, 4 * 288)
    shared = dict(meta=f(meta_tokens), ccomb=ccomb, norm_g=f(norm_g), w_in=f(w_in)[0], g_cq=f(g_cq),
                  w_uq=f(w_uq)[0], g_ckv=f(g_ckv), w_uk=f(w_uk)[0].reshape(256, 512), w_uv=f(w_uv)[0].reshape(256, 512),
                  g_qn=f(g_qn), g_kn=f(g_kn), lb=f(lb_logits), g_bn=f(g_bn), w_oa=f(w_oa)[0], w_ob=f(w_ob)[0], w_o=f(w_o)[0])
    xpf = f(x_prompt); xsf = f(x_sample); stf = f(state_hgrn)
    ptab = np.ascontiguousarray(np.asarray(page_table, dtype=np.int32))
    maps = []
    for c in range(NCORES):
        m = dict(shared)
        m["xp"] = xpf[c]
        m["xs"] = xsf[NS * c:NS * (c + 1), 0]
        m["st_in"] = stf[0, NS * c:NS * (c + 1)]
        m["pt"] = ptab[NS * c:NS * (c + 1)]
        maps.append(m)
    return maps


def kernel(**inputs):
    nc = _get_nc(False)
    maps = make_in_maps(**inputs)
    res = run_bass_kernel_spmd(nc, maps, core_ids=list(range(NCORES)))
    r = res.results
    cat = lambda k: np.stack([np.asarray(x[k]) for x in r], axis=0)
    y_p = cat("y_p")
    y_s = np.concatenate([np.asarray(x["y_s"]) for x in r], axis=0)[:, None, :]
    lat_p = cat("lat_p")[None]
    kr_p = cat("kr_p")[None]
    hg_p = cat("hg_p")[None]
    lat_s = np.concatenate([np.asarray(x["lat_s"]) for x in r], axis=0)[None, :, None, :]
    kr_s = np.concatenate([np.asarray(x["kr_s"]) for x in r], axis=0)[None, :, None, :]
    hg_s = np.concatenate([np.asarray(x["hg_s"]) for x in r], axis=0)[None]
    return (y_p, y_s, lat_p, kr_p, hg_p, lat_s, kr_s, hg_s)
```

```python
import contextlib
import math
import numpy as np
import concourse.bass as bass
import concourse.mybir as mybir
from concourse.bass_utils import run_bass_kernel_spmd

F32 = mybir.dt.float32
BF16 = mybir.dt.bfloat16
I32 = mybir.dt.int32
AF = mybir.ActivationFunctionType
ALU = mybir.AluOpType
AX = mybir.AxisListType

NCORES = 8
D = 1024
SEQ = 2048
NMETA = 16
TP = SEQ + NMETA
NS = 4
NTOK = TP + NS
INC = 5280
EPS = 1e-6
NPOOL = 5120
NPAGES = 128
SCALE = 96 ** -0.5
SBIAS = -(96 ** 0.5)
COMPUTE = ("pe", "act", "dve", "pool")
DEBUG = False


class Sched:
    def __init__(self, nc):
        self.nc = nc
        self.ops = []
        self.lastw = {}
        self.readers = {}
        self.chan_last = {}
        self.chan_count = {}
        self.pending = {}

    def barrier(self, fns):
        ids = [self.add(e, fns[e], (), [("bar", e)]) for e in COMPUTE]
        allc = set(ids) | set(self.chan_last.values())
        for e in COMPUTE + ("sp",):
            self.pending.setdefault(e, set()).update(allc)

    def add(self, eng, fn, reads=(), writes=(), dma=None):
        idx = len(self.ops)
        deps = set(self.pending.pop(eng, ()))
        for r in reads:
            w = self.lastw.get(r)
            if w is not None:
                deps.add(w)
        for w_ in writes:
            w = self.lastw.get(w_)
            if w is not None:
                deps.add(w)
            rd = self.readers.get(w_)
            if rd:
                deps.update(rd.values())
        if dma is not None:
            if dma in self.chan_last:
                deps.add(self.chan_last[dma])
            self.chan_count[dma] = self.chan_count.get(dma, 0) + 1
        op = dict(eng=eng, fn=fn, deps=deps, dma=dma, idx=idx,
                  cnt=self.chan_count.get(dma, 0) if dma is not None else 0)
        self.ops.append(op)
        for r in reads:
            d = self.readers.setdefault(r, {})
            d[eng if dma is None else ("dma", dma)] = idx
        for w_ in writes:
            self.lastw[w_] = idx
            self.readers[w_] = {}
        if dma is not None:
            self.chan_last[dma] = idx
        return idx

    def pe(self, fn, r=(), w=()):
        return self.add("pe", fn, r, w)

    def act(self, fn, r=(), w=()):
        return self.add("act", fn, r, w)

    def dve(self, fn, r=(), w=()):
        return self.add("dve", fn, r, w)

    def pool(self, fn, r=(), w=()):
        return self.add("pool", fn, r, w)

    def dma(self, chan, fn, r=(), w=(), q="sp"):
        return self.add(q, fn, r, w, dma=chan)

    def emit(self, stack):
        nc = self.nc
        ops = self.ops
        waited = {}
        signal = set()
        for op in ops:
            e = op["eng"]
            waits = []
            for d in sorted(op["deps"]):
                p = ops[d]
                if p["dma"] is not None:
                    key = ("dma", p["dma"])
                    if waited.get((e, key), 0) >= p["cnt"]:
                        continue
                    waited[(e, key)] = p["cnt"]
                    waits.append(("dma", p["dma"], p["cnt"]))
                else:
                    pe_ = p["eng"]
                    if pe_ == "pe" and e == "pe" and op["dma"] is None:
                        continue
                    key = ("eng", pe_)
                    if waited.get((e, key), -1) >= d:
                        continue
                    waited[(e, key)] = d
                    signal.add(d)
                    waits.append(("eng", pe_, d))
            op["waits"] = waits
        cnt = {e: 0 for e in COMPUTE}
        for op in ops:
            if op["idx"] in signal:
                cnt[op["eng"]] += 1
                op["ticket"] = cnt[op["eng"]]
        esem = {e: stack.enter_context(nc.semaphore("s_" + e)) for e in COMPUTE}
        csem = {c: stack.enter_context(nc.semaphore("c_%d" % i))
                for i, c in enumerate(self.chan_count)}
        per = {e: [] for e in COMPUTE + ("sp",)}
        for op in ops:
            per[op["eng"]].append(op)
        chan_count = self.chan_count

        def run(engh, lst, final=False):
            for op in lst:
                for w in op["waits"]:
                    if w[0] == "dma":
                        engh.wait_ge(csem[w[1]], 16 * w[2])
                    else:
                        engh.wait_ge(esem[w[1]], ops[w[2]]["ticket"])
                inst = op["fn"](engh)
                if op["dma"] is not None:
                    inst.then_inc(csem[op["dma"]], 16)
                elif op["idx"] in signal:
                    inst.then_inc(esem[op["eng"]], 1)
            if final:
                for c, n in chan_count.items():
                    engh.wait_ge(csem[c], 16 * n)

        block = stack.enter_context(nc.Block())

        @block.sync
        def _(e):
            run(e, per["sp"], final=True)

        @block.tensor
        def _(e):
            run(e, per["pe"])

        @block.scalar
        def _(e):
            run(e, per["act"])

        @block.vector
        def _(e):
            run(e, per["dve"])

        @block.gpsimd
        def _(e):
            run(e, per["pool"])


def _inv_freq():
    j = np.arange(16, dtype=np.float32)
    return (np.float32(10000.0) ** (-j / np.float32(16))).astype(np.float32)


def build(dbg=False, stop_after=99):
    nc = bass.Bass("TRN2", target_bir_lowering=False)
    din = lambda n, s, d=F32: nc.dram_tensor(n, s, d, kind="ExternalInput").ap()
    dout = lambda n, s, d=F32: nc.dram_tensor(n, s, d, kind="ExternalOutput").ap()
    xp = din("xp", [SEQ, D]); xs = din("xs", [NS, D]); meta = din("meta", [NMETA, D])
    ccomb = din("ccomb", [NPOOL * 32, 4 * 288])
    st_in = din("st_in", [NS, 4, 128, 128]); pt = din("pt", [NS, NPAGES], I32)
    norm_g = din("norm_g", [1, D]); w_in = din("w_in", [D, INC]); g_cq = din("g_cq", [1, 384])
    w_uq = din("w_uq", [384, 768]); g_ckv = din("g_ckv", [1, 256]); w_uk = din("w_uk", [256, 512])
    w_uv = din("w_uv", [256, 512]); g_qn = din("g_qn", [1, 96]); g_kn = din("g_kn", [1, 96])
    lb = din("lb", [2, 512]); g_bn = din("g_bn", [1, 128]); w_oa = din("w_oa", [512, D])
    w_ob = din("w_ob", [512, D]); w_o = din("w_o", [D, D])
    y_p = dout("y_p", [SEQ, D]); y_s = dout("y_s", [NS, D]); lat_p = dout("lat_p", [TP, 256])
    kr_p = dout("kr_p", [TP, 32]); hg_p = dout("hg_p", [4, 128, 128]); lat_s = dout("lat_s", [NS, 256])
    kr_s = dout("kr_s", [NS, 32]); hg_s = dout("hg_s", [NS, 4, 128, 128])
    dbgo = {}

    with contextlib.ExitStack() as st:
        T = lambda n, s, d=F32: st.enter_context(nc.sbuf_tensor(n, s, d))
        s = Sched(nc)
        psum_all = st.enter_context(nc.psum_tensor("psum_all", [128, 4096], F32))
        banks = [psum_all[:, 512 * i:512 * (i + 1)] for i in range(8)]
        bkb = [b_.bitcast(BF16) for b_ in banks]

        def bk(i):
            return "bank%d" % i

        AKB = 150
        arena = T("arena", [128, AKB * 256])

        class Bump:
            def __init__(self, lo_kb, hi_kb):
                self.off = int(lo_kb * 256); self.hi = int(hi_kb * 256)

            def __call__(self, shape, dt=F32):
                free = list(shape[1:])
                n = 1
                for d_ in free:
                    n *= d_
                words = (n + 1) // 2 if dt == BF16 else n
                words = (words + 7) // 8 * 8
                v = arena[0:shape[0], self.off:self.off + words]
                self.off += words
                assert self.off <= self.hi, (self.off, self.hi)
                if dt == BF16:
                    v = v.bitcast(BF16)[:, 0:n]
                elif dt == I32:
                    v = v.bitcast(I32)[:, 0:n]
                else:
                    v = v[:, 0:n]
                if len(free) > 1:
                    names = " ".join("a%d" % i for i in range(len(free)))
                    v = v.rearrange("p (%s) -> p %s" % (names, names), **{"a%d" % i: free[i] for i in range(1, len(free))})
                return v

        scr_act = T("scr_act", [1, 8]); scr_dve = T("scr_dve", [1, 8]); scr_pool = T("scr_pool", [1, 8])
        scr_bf = T("scr_bf", [1, 8], BF16)

        def barrier():
            s.pe(lambda e: e.matmul(banks[7][0:1, 0:8], lhsT=scr_bf[0:1, 0:1], rhs=scr_bf[0:1, 0:8], start=True, stop=True), r=["scr_bf"], w=[bk(7), bk(4), bk(5)])
            s.barrier({
                "pe": lambda e: e.matmul(banks[7][0:1, 0:8], lhsT=scr_bf[0:1, 0:1], rhs=scr_bf[0:1, 0:8], start=True, stop=True),
                "act": lambda e: e.activation(out=scr_act[:], in_=scr_act[:], func=AF.Copy),
                "dve": lambda e: e.memset(scr_dve[:], 0.0),
                "pool": lambda e: e.memset(scr_pool[:], 0.0),
            })
        s.pool(lambda e: e.memset(scr_bf[:], 0.0), w=["scr_bf"])
        s.pool(lambda e: e.memset(scr_act[:], 0.0), w=["scr_act"])

        ident = T("ident", [128, 128], BF16)
        identf = T("identf", [128, 128], F32)
        tri = T("tri", [128, 128], BF16)
        btri = T("btri", [128, 128], BF16)
        for tt, tn_ in ((ident, "ident"), (identf, "identf")):
            s.pool(lambda e, tt=tt: e.memset(tt[:], 0.0), w=[tn_])
            s.pool(lambda e, tt=tt: e.affine_select(out=tt[:], in_=tt[:], pattern=[[1, 128]], compare_op=ALU.not_equal,
                                                    fill=1.0, base=0, channel_multiplier=-1), r=[tn_], w=[tn_])
        s.pool(lambda e: e.memset(tri[:], 1.0), w=["tri"])
        s.pool(lambda e: e.affine_select(out=tri[:], in_=tri[:], pattern=[[1, 128]], compare_op=ALU.is_ge,
                                         fill=0.0, base=0, channel_multiplier=-1), r=["tri"], w=["tri"])
        s.pool(lambda e: e.tensor_copy(out=btri[:], in_=tri[:]), r=["tri"], w=["btri"])
        s.pool(lambda e: e.memset(btri[0:64, 64:128], 0.0), r=["btri"], w=["btri"])
        neghalf = T("neghalf", [128, 16])
        s.pool(lambda e: e.memset(neghalf[:], -0.5), w=["neghalf"])
        ones_bf = T("ones_bf", [128, 64], BF16)
        s.pool(lambda e: e.memset(ones_bf[:], 1.0), w=["ones_bf"])
        sbias = T("sbias", [128, 1])
        s.pool(lambda e: e.memset(sbias[:], SBIAS), w=["sbias"])
        epsc = T("epsc", [128, 1])
        s.pool(lambda e: e.memset(epsc[:], EPS), w=["epsc"])
        selrow = T("selrow", [65, 64])
        s.pool(lambda e: e.memset(selrow[:], 0.0), w=["selrow"])
        s.pool(lambda e: e.memset(selrow[64:65, :], 1.0), r=["selrow"], w=["selrow"])

        NT = 18
        pos = T("pos", [128, NT])
        ang = T("ang", [128, 2, NT, 16])
        invf = T("invf", [128, 16])
        kq = T("kq", [128, 2 * NT * 16])
        kqi = T("kqi", [128, 2 * NT * 16], I32)
        CC = T("CC", [128, NT, 32])
        SS = T("SS", [128, NT, 32])
        s.pool(lambda e: e.iota(pos[:], pattern=[[128, NT]], base=NMETA - 128, channel_multiplier=1,
                                allow_small_or_imprecise_dtypes=True), w=["pos"])
        s.pool(lambda e: e.iota(pos[:, 0:1], pattern=[[0, 1]], base=0, channel_multiplier=1,
                                allow_small_or_imprecise_dtypes=True), r=["pos"], w=["pos"])
        s.pool(lambda e: e.memset(pos[:, NT - 1:NT], 16384.0), r=["pos"], w=["pos"])
        for j, v in enumerate(_inv_freq()):
            s.pool(lambda e, j=j, v=float(v): e.memset(invf[:, j:j + 1], v), r=["invf"], w=["invf"])
        s.dve(lambda e: e.tensor_tensor(out=ang[:, 0], in0=pos[:].unsqueeze(2).to_broadcast([128, NT, 16]),
                                        in1=invf[:].unsqueeze(1).to_broadcast([128, NT, 16]), op=ALU.mult),
              r=["pos", "invf"], w=["ang"])
        s.dve(lambda e: e.tensor_scalar(out=ang[:, 1], in0=ang[:, 0], scalar1=math.pi / 2, scalar2=None, op0=ALU.add),
              r=["ang"], w=["ang"])
        angf = ang[:].rearrange("p a t j -> p (a t j)")
        s.dve(lambda e: e.tensor_scalar(out=kq[:], in0=angf, scalar1=1.0 / (2 * math.pi), scalar2=None, op0=ALU.mult),
              r=["ang"], w=["kq"])
        s.dve(lambda e: e.tensor_copy(out=kqi[:], in_=kq[:]), r=["kq"], w=["kqi"])
        s.dve(lambda e: e.tensor_copy(out=kq[:], in_=kqi[:]), r=["kqi"], w=["kq"])
        C1 = 6.28125
        C2 = 2 * math.pi - C1
        s.dve(lambda e: e.scalar_tensor_tensor(out=angf, in0=kq[:], scalar=-C1, in1=angf, op0=ALU.mult, op1=ALU.add),
              r=["kq", "ang"], w=["ang"])
        s.dve(lambda e: e.scalar_tensor_tensor(out=angf, in0=kq[:], scalar=-C2, in1=angf, op0=ALU.mult, op1=ALU.add),
              r=["kq", "ang"], w=["ang"])
        s.dve(lambda e: e.tensor_scalar(out=kq[:], in0=angf, scalar1=math.pi, scalar2=-2 * math.pi, op0=ALU.is_gt, op1=ALU.mult),
              r=["ang"], w=["kq"])
        s.dve(lambda e: e.tensor_tensor(out=angf, in0=angf, in1=kq[:], op=ALU.add), r=["ang", "kq"], w=["ang"])
        s.dve(lambda e: e.tensor_scalar(out=kq[:], in0=angf, scalar1=-math.pi, scalar2=2 * math.pi, op0=ALU.is_lt, op1=ALU.mult),
              r=["ang"], w=["kq"])
        s.dve(lambda e: e.tensor_tensor(out=angf, in0=angf, in1=kq[:], op=ALU.add), r=["ang", "kq"], w=["ang"])
        s.dve(lambda e: e.tensor_scalar(out=angf, in0=angf, scalar1=-math.pi, scalar2=math.pi, op0=ALU.max, op1=ALU.min),
              r=["ang"], w=["ang"])
        s.act(lambda e: e.activation(out=SS[:, :, 16:32], in_=ang[:, 0], func=AF.Sin), r=["ang"], w=["SS"])
        s.act(lambda e: e.activation(out=CC[:, :, 0:16], in_=ang[:, 1], func=AF.Sin), r=["ang"], w=["CC"])
        s.dve(lambda e: e.tensor_copy(out=CC[:, :, 16:32], in_=CC[:, :, 0:16]), r=["CC"], w=["CC"])
        s.dve(lambda e: e.tensor_scalar(out=SS[:, :, 0:16], in0=SS[:, :, 16:32], scalar1=-1.0, scalar2=None, op0=ALU.mult),
              r=["SS"], w=["SS"])

        gckv_bc = T("gckv_bc", [128, 256])
        s.dma("c0", lambda e: e.dma_start(out=gckv_bc[:], in_=g_ckv[0:1, :].partition_broadcast(128)), w=["gckv_bc"])
        normg_c = T("normg_c", [128, 8])
        gcq_c = T("gcq_c", [128, 3])
        gq2_c = T("gq2_c", [96, 2])
        gbn_c = T("gbn_c", [128, 1])
        lb_c = T("lb_c", [128, 2, 4])
        s.dma("c1", lambda e: e.dma_start(out=normg_c[:], in_=norm_g.rearrange("o (k p) -> p (o k)", p=128),
                                          allow_slow_non_contiguous=True), w=["normg_c"])
        s.dma("c2", lambda e: e.dma_start(out=gcq_c[:], in_=g_cq.rearrange("o (k p) -> p (o k)", p=128),
                                          allow_slow_non_contiguous=True), w=["gcq_c"])
        s.dma("c3", lambda e: e.dma_start(out=gq2_c[:, 0:1], in_=g_qn.rearrange("o p -> p o"),
                                          allow_slow_non_contiguous=True), w=["gq2_c"])
        s.dma("c4", lambda e: e.dma_start(out=gq2_c[:, 1:2], in_=g_kn.rearrange("o p -> p o"),
                                          allow_slow_non_contiguous=True), r=["gq2_c"], w=["gq2_c"])
        s.dma("c5", lambda e: e.dma_start(out=gbn_c[:], in_=g_bn.rearrange("o p -> p o"),
                                          allow_slow_non_contiguous=True), w=["gbn_c"])
        s.dma("c6", lambda e: e.dma_start(out=lb_c[:], in_=lb.rearrange("r (h p) -> p r h", p=128),
                                          allow_slow_non_contiguous=True), w=["lb_c"])
        gq2 = T("gq2", [96, 1])
        s.dve(lambda e: e.tensor_tensor(out=gq2[:], in0=gq2_c[:, 0:1], in1=gq2_c[:, 1:2], op=ALU.mult), r=["gq2_c"], w=["gq2"])
        hg_a = T("hg_a", [128, 4]); hg_nb = T("hg_nb", [128, 4]); hg_nnb = T("hg_nnb", [128, 4]); hg_t = T("hg_t", [128, 4])
        s.dve(lambda e: e.tensor_tensor(out=hg_t[:], in0=lb_c[:, 0, :], in1=lb_c[:, 1, :], op=ALU.subtract), r=["lb_c"], w=["hg_t"])
        s.act(lambda e: e.activation(out=hg_t[:], in_=hg_t[:], func=AF.Tanh, scale=0.5), r=["hg_t"], w=["hg_t"])
        s.dve(lambda e: e.tensor_scalar(out=hg_a[:], in0=hg_t[:], scalar1=0.25, scalar2=0.75, op0=ALU.mult, op1=ALU.add), r=["hg_t"], w=["hg_a"])
        s.dve(lambda e: e.tensor_scalar(out=hg_nb[:], in0=hg_t[:], scalar1=-0.25, scalar2=0.25, op0=ALU.mult, op1=ALU.add), r=["hg_t"], w=["hg_nb"])
        s.dve(lambda e: e.tensor_scalar(out=hg_nnb[:], in0=hg_t[:], scalar1=0.25, scalar2=-0.25, op0=ALU.mult, op1=ALU.add), r=["hg_t"], w=["hg_nnb"])

        xnT = T("xnT", [128, 8, NTOK], BF16)
        aA = Bump(0, 84)
        QT = aA([128, 8, NTOK], BF16)
        KT = aA([128, 8, NTOK], BF16)
        VA = aA([128, 17, 8, 65], BF16)
        Wukv = T("Wukv", [128, 2, 1024], BF16)
        s.pool(lambda e: e.memset(VA[:, :, :, 64:65], 1.0), w=["VA1"])

        a1 = Bump(84, 150)
        W1 = a1([128, 8, 672], BF16)
        Wuq = a1([128, 3, 768], BF16)
        wstf = [a1([128, 2048]) for i in range(2)]
        w_in_v = w_in.rearrange("(k p) c -> p k c", p=128)

        def load_w_in(dst, c_lo, c_hi, wst, chunk=256, engs=("pool",)):
            ci = 0
            res = ("W", c_lo)
            for c0 in range(c_lo, c_hi, chunk):
                cw = min(chunk, c_hi - c0)
                b = ci % 2
                stv = wst[b][:, 0:8 * cw].rearrange("p (k c) -> p k c", c=cw)
                s.dma("wst%d" % b, lambda e, stv=stv, c0=c0, cw=cw: e.dma_start(out=stv, in_=w_in_v[:, :, c0:c0 + cw]), w=["wst%d" % b])
                eng = engs[ci % len(engs)]
                dv = dst[:, :, c0 - c_lo:c0 - c_lo + cw]
                s.add(eng, lambda e, stv=stv, dv=dv, cw=cw: e.tensor_tensor(out=dv, in0=stv, in1=normg_c[:].unsqueeze(2).to_broadcast([128, 8, cw]), op=ALU.mult),
                      ["wst%d" % b, "normg_c"], [res])
                ci += 1
            return res

        rW1 = load_w_in(W1, 0, 672, wstf)
        for k3 in range(3):
            stv = wstf[1][:, 0:768]
            s.dma("wst1", lambda e, k3=k3, stv=stv: e.dma_start(out=stv, in_=w_uq[k3 * 128:(k3 + 1) * 128, :]), w=["wst1"])
            s.pool(lambda e, k3=k3, stv=stv: e.tensor_scalar(out=Wuq[:, k3, :], in0=stv, scalar1=gcq_c[:, k3:k3 + 1], scalar2=None, op0=ALU.mult),
                   r=["wst1", "gcq_c"], w=["Wuq"])
        stkv = wstf[0][:, 0:2048].rearrange("p (k c) -> p k c", c=1024)
        s.dma("wst0", lambda e, stkv=stkv: e.dma_start(out=stkv[:, :, 0:512], in_=w_uk.rearrange("(k p) c -> p k c", p=128)), w=["wst0"])
        s.dma("wst0b", lambda e, stkv=stkv: e.dma_start(out=stkv[:, :, 512:1024], in_=w_uv.rearrange("(k p) c -> p k c", p=128)), r=["wst0"], w=["wst0"])
        s.dve(lambda e, stkv=stkv: e.tensor_copy(out=Wukv[:], in_=stkv), r=["wst0"], w=["Wukv"])

        tiles = [(0, NMETA, meta[:, :], SEQ)]
        tiles += [(1 + i, 128, xp[i * 128:(i + 1) * 128, :], i * 128) for i in range(16)]
        tiles += [(17, NS, xs[:, :], TP)]
        xst = [a1([128, D]) for i in range(2)]
        xnb = [a1([128, D], BF16) for i in range(2)]
        junk = a1([128, D], BF16)
        junk2 = junk
        ssx = T("ssx", [128, NT])
        rsx = T("rsx", [128, NT])
        for n, (ti, R, src, cb) in enumerate(tiles):
            xb = n % 2
            nb = n % 2
            pb = n % 2
            s.dma("xst%d" % xb, lambda e, xb=xb, R=R, src=src: e.dma_start(out=xst[xb][:R, :], in_=src), w=["xst%d" % xb])
            s.act(lambda e, xb=xb, R=R, ti=ti: e.activation(out=junk[:R, :], in_=xst[xb][:R, :], func=AF.Square,
                                                           accum_out=ssx[:R, ti:ti + 1]),
                  r=["xst%d" % xb], w=["junk2", ("ssx", ti)])
            s.dve(lambda e, R=R, ti=ti: e.tensor_scalar(out=rsx[:R, ti:ti + 1], in0=ssx[:R, ti:ti + 1], scalar1=1.0 / D, scalar2=EPS,
                                                       op0=ALU.mult, op1=ALU.add), r=[("ssx", ti)], w=[("rsx", ti)])
            s.act(lambda e, R=R, ti=ti: e.activation(out=rsx[:R, ti:ti + 1], in_=rsx[:R, ti:ti + 1], func=AF.Sqrt), r=[("rsx", ti)], w=[("rsx", ti)])
            s.dve(lambda e, R=R, ti=ti: e.reciprocal(out=rsx[:R, ti:ti + 1], in_=rsx[:R, ti:ti + 1]), r=[("rsx", ti)], w=[("rsx", ti)])
            s.dve(lambda e, xb=xb, nb=nb, R=R, ti=ti: e.tensor_scalar(out=xnb[nb][:R, :], in0=xst[xb][:R, :], scalar1=rsx[:R, ti:ti + 1],
                                                                     scalar2=None, op0=ALU.mult),
                  r=["xst%d" % xb, ("rsx", ti)], w=["xnb%d" % nb])
            psT = bkb[pb]
            for k in range(8):
                s.pe(lambda e, k=k, nb=nb, R=R, psT=psT: e.transpose(out=psT[:, k * 128:k * 128 + R], in_=xnb[nb][:R, k * 128:(k + 1) * 128],
                                                                   identity=ident[:R, :R]),
                     r=["xnb%d" % nb, "ident"], w=[bk(pb)])
            src_v = psT.rearrange("p (k r) -> p k r", r=128)[:, :, 0:R]
            if n % 2 == 0:
                s.act(lambda e, src_v=src_v, cb=cb, R=R: e.activation(out=xnT[:, :, cb:cb + R], in_=src_v, func=AF.Copy),
                      r=[bk(pb)], w=[("xnT", ti)])
            else:
                s.dve(lambda e, src_v=src_v, cb=cb, R=R: e.tensor_copy(out=xnT[:, :, cb:cb + R], in_=src_v),
                      r=[bk(pb)], w=[("xnT", ti)])

        ckn = [a1([128, 256]) for i in range(2)]
        cknb = [a1([128, 256], BF16) for i in range(2)]
        ckT = [a1([128, 2, 128], BF16) for i in range(2)]
        krr = [T("krr%d" % i, [128, 32]) for i in range(2)]
        rtmp = [T("rtmp%d" % i, [128, 2, 32]) for i in range(2)]
        st1 = T("st1", [128, NT, 4])
        ssk = T("ssk", [128, NT, 8])
        rsk = T("rsk", [128, NT, 8])
        kc = [a1([128, 8, 96], BF16)] * 2
        def s1b(n, ti, R, src, cb):
            b2 = n % 2
            pA, pB, pC, pD = 2, 3, 4, 5
            res_tile = ("xnT", ti)
            for k in range(8):
                s.pe(lambda e, k=k, cb=cb, R=R: e.matmul(banks[pA][:R, 0:288], lhsT=xnT[:, k, cb:cb + R], rhs=W1[:, k, 384:672],
                                                        start=(k == 0), stop=(k == 7)),
                     r=[res_tile, rW1], w=[bk(pA)])
            s.act(lambda e, R=R, ti=ti: e.activation(out=junk2[:R, 0:256], in_=banks[pA][:R, 0:256], func=AF.Square,
                                                    accum_out=st1[:R, ti, 0:1]), r=[bk(pA)], w=["junk2", ("st1", ti)])
            s.dve(lambda e, R=R, ti=ti: e.tensor_scalar(out=st1[:R, ti, 1:2], in0=st1[:R, ti, 0:1], scalar1=1.0 / 256, scalar2=EPS,
                                                       op0=ALU.mult, op1=ALU.add), r=[("st1", ti)], w=[("st1", ti)])
            s.act(lambda e, R=R, ti=ti: e.activation(out=st1[:R, ti, 1:2], in_=st1[:R, ti, 1:2], func=AF.Sqrt), r=[("st1", ti)], w=[("st1", ti)])
            s.dve(lambda e, R=R, ti=ti: e.reciprocal(out=st1[:R, ti, 1:2], in_=st1[:R, ti, 1:2]), r=[("st1", ti)], w=[("st1", ti)])
            s.dve(lambda e, R=R, ti=ti, b2=b2: e.scalar_tensor_tensor(out=ckn[b2][:R, :], in0=banks[pA][:R, 0:256], scalar=st1[:R, ti, 1:2],
                                                                     in1=gckv_bc[:R, :], op0=ALU.mult, op1=ALU.mult),
                  r=[bk(pA), ("st1", ti), "gckv_bc"], w=["ckn%d" % b2])
            s.dve(lambda e, R=R, ti=ti, b2=b2: e.tensor_tensor(out=rtmp[b2][:R, 0, :], in0=banks[pA][:R, 256:288], in1=CC[:R, ti, :], op=ALU.mult),
                  r=[bk(pA), "CC"], w=["rtmp%d" % b2])
            s.dve(lambda e, R=R, ti=ti, b2=b2: e.tensor_tensor(out=rtmp[b2][:R, 1, 0:16], in0=banks[pA][:R, 272:288], in1=SS[:R, ti, 0:16], op=ALU.mult),
                  r=[bk(pA), "SS"], w=["rtmp%d" % b2])
            s.dve(lambda e, R=R, ti=ti, b2=b2: e.tensor_tensor(out=rtmp[b2][:R, 1, 16:32], in0=banks[pA][:R, 256:272], in1=SS[:R, ti, 16:32], op=ALU.mult),
                  r=[bk(pA), "SS", "rtmp%d" % b2], w=["rtmp%d" % b2])
            s.dve(lambda e, R=R, b2=b2: e.tensor_tensor(out=krr[b2][:R, :], in0=rtmp[b2][:R, 0, :], in1=rtmp[b2][:R, 1, :], op=ALU.add),
                  r=["rtmp%d" % b2], w=["krr%d" % b2])
            if ti == 0:
                dl, dk = lat_p[0:NMETA, :], kr_p[0:NMETA, :]
            elif ti == 17:
                dl, dk = lat_s[:, :], kr_s[:, :]
            else:
                r0 = NMETA + (ti - 1) * 128
                dl, dk = lat_p[r0:r0 + 128, :], kr_p[r0:r0 + 128, :]
            wl = ["lat_s_dram"] if ti == 17 else []
            s.dma("olat%d" % b2, lambda e, dl=dl, b2=b2, R=R: e.dma_start(out=dl, in_=ckn[b2][:R, :]), r=["ckn%d" % b2], w=wl)
            wl = ["kr_s_dram"] if ti == 17 else []
            s.dma("okr%d" % b2, lambda e, dk=dk, b2=b2, R=R: e.dma_start(out=dk, in_=krr[b2][:R, :]), r=["krr%d" % b2], w=wl)

        def s2b(n, ti, R, src, cb):
            b2 = n % 2
            pA, pB, pC, pD = 2, 3, 4, 5
            res_tile = ("xnT", ti)
            s.act(lambda e, R=R, b2=b2: e.activation(out=cknb[b2][:R, :], in_=ckn[b2][:R, :], func=AF.Copy),
                  r=["ckn%d" % b2], w=["cknb%d" % b2])
            psT = bkb[pD]
            for c in range(2):
                s.pe(lambda e, c=c, R=R, b2=b2, psT=psT: e.transpose(out=psT[:, c * 128:c * 128 + R], in_=cknb[b2][:R, c * 128:(c + 1) * 128],
                                                                   identity=ident[:R, :R]),
                     r=["cknb%d" % b2, "ident"], w=[bk(pD)])
            s.dve(lambda e, R=R, b2=b2, psT=psT: e.tensor_copy(out=ckT[b2][:, :, 0:R], in_=psT[:, 0:256].rearrange("p (c r) -> p c r", r=128)[:, :, 0:R]),
                  r=[bk(pD)], w=["ckT%d" % b2])
            for half, pbk in ((0, pB), (1, pC)):
                for c in range(2):
                    s.pe(lambda e, c=c, R=R, b2=b2, half=half, pbk=pbk: e.matmul(banks[pbk][:R, :], lhsT=ckT[b2][:, c, 0:R],
                                                                                rhs=Wukv[:, c, half * 512:(half + 1) * 512],
                                                                                start=(c == 0), stop=(c == 1)),
                         r=["ckT%d" % b2, "Wukv"], w=[bk(pbk)])
            if ti != 17:
                blk = 16 if ti == 0 else ti - 1
                s.act(lambda e, R=R, blk=blk: e.activation(out=VA[:R, blk, :, 0:64], in_=banks[pC][:R, :].rearrange("p (h d) -> p h d", d=64),
                                                          func=AF.Copy), r=[bk(pC)], w=[("VA", blk)])
            s.act(lambda e, R=R: e.activation(out=junk2[:R, 0:512], in_=banks[pB][:R, :], func=AF.Square), r=[bk(pB)], w=["junk2"])
            s.dve(lambda e, R=R, ti=ti: e.tensor_reduce(out=ssk[:R, ti, :], in_=junk2[:R, 0:512].rearrange("p (h d) -> p h d", d=64),
                                                       axis=AX.X, op=ALU.add), r=["junk2"], w=[("ssk", ti)])
            s.act(lambda e, R=R, ti=ti, b2=b2: e.activation(out=junk2[:R, 512:544], in_=krr[b2][:R, :], func=AF.Square,
                                                           accum_out=st1[:R, ti, 2:3]), r=["krr%d" % b2], w=["junk2", ("st1b", ti)])
            s.dve(lambda e, R=R, ti=ti: e.tensor_scalar(out=rsk[:R, ti, :], in0=ssk[:R, ti, :], scalar1=st1[:R, ti, 2:3], scalar2=1.0 / 96,
                                                       op0=ALU.add, op1=ALU.mult), r=[("ssk", ti), ("st1b", ti)], w=[("rsk", ti)])
            s.pool(lambda e, R=R, ti=ti: e.tensor_scalar(out=rsk[:R, ti, :], in0=rsk[:R, ti, :], scalar1=EPS, scalar2=None, op0=ALU.add),
                   r=[("rsk", ti)], w=[("rsk", ti)])
            s.act(lambda e, R=R, ti=ti: e.activation(out=rsk[:R, ti, :], in_=rsk[:R, ti, :], func=AF.Sqrt), r=[("rsk", ti)], w=[("rsk", ti)])
            s.dve(lambda e, R=R, ti=ti: e.reciprocal(out=rsk[:R, ti, :], in_=rsk[:R, ti, :]), r=[("rsk", ti)], w=[("rsk", ti)])
            s.dve(lambda e, R=R, ti=ti, b2=b2: e.tensor_tensor(out=kc[b2][:R, :, 0:64], in0=banks[pB][:R, :].rearrange("p (h d) -> p h d", d=64),
                                                              in1=rsk[:R, ti, :].unsqueeze(2).to_broadcast([R, 8, 64]), op=ALU.mult),
                  r=[bk(pB), ("rsk", ti)], w=["kc"])
            s.dve(lambda e, R=R, ti=ti, b2=b2: e.tensor_tensor(out=kc[b2][:R, :, 64:96], in0=krr[b2][:R, :].unsqueeze(1).to_broadcast([R, 8, 32]),
                                                              in1=rsk[:R, ti, :].unsqueeze(2).to_broadcast([R, 8, 32]), op=ALU.mult),
                  r=["krr%d" % b2, ("rsk", ti), "kc"], w=["kc"])
            psK = bkb[6]
            for h in range(8):
                s.pe(lambda e, h=h, R=R, b2=b2, psK=psK: e.transpose(out=psK[0:96, h * 128:h * 128 + R], in_=kc[b2][:R, h, :], identity=ident[:R, :R]),
                     r=["kc", "ident"], w=[bk(6)])
            s.act(lambda e, R=R, cb=cb, psK=psK: e.activation(out=KT[0:96, :, cb:cb + R], in_=psK[0:96, :].rearrange("p (h r) -> p h r", r=128)[:, :, 0:R],
                                                             func=AF.Copy), r=[bk(6)], w=[("KT", ti)])

        for n in range(len(tiles) + 1):
            if n < len(tiles):
                s1b(n, *tiles[n])
            if n >= 1:
                s2b(n - 1, *tiles[n - 1])

        cqb = a1([128, 3, 512], BF16)
        cqsq = a1([128, 3, 512], BF16)
        qs = [a1([128, 8, 96])] * 2
        rq = a1([128, 2, 8, 32])
        qst = T("qst", [128, NT, 2])
        ssq = T("ssq", [128, NT, 8])
        rsq = T("rsq", [128, NT, 8])
        qnb = [a1([128, 8, 96], BF16)] * 2
        qgroups = [(512 * g, 512, [(1 + 4 * g + i, 128, 128 * i) for i in range(4)]) for g in range(4)]
        qgroups.append((TP, NS, [(17, NS, 0)]))
        for gi, (c0, n, gt) in enumerate(qgroups):
            rx = [("xnT", ti) for ti, _, _ in gt]
            for c in range(3):
                for k in range(8):
                    s.pe(lambda e, c=c, k=k, c0=c0, n=n: e.matmul(banks[c][:, 0:n], lhsT=W1[:, k, c * 128:(c + 1) * 128], rhs=xnT[:, k, c0:c0 + n],
                                                                 start=(k == 0), stop=(k == 7)), r=rx + [rW1], w=[bk(c)])
                s.act(lambda e, c=c, n=n: e.activation(out=cqb[:, c, 0:n], in_=banks[c][:, 0:n], func=AF.Copy), r=[bk(c)], w=["cqb"])
                s.act(lambda e, c=c, n=n: e.activation(out=cqsq[:, c, 0:n], in_=banks[c][:, 0:n], func=AF.Square), r=[bk(c)], w=["cqsq"])
            for tn, (ti, R, lo) in enumerate(gt):
                b2 = tn % 2
                cb = c0 + lo
                for c in range(3):
                    s.pe(lambda e, c=c, R=R, lo=lo: e.matmul(banks[7][:R, 0:1], lhsT=cqsq[:, c, lo:lo + R], rhs=ones_bf[:, 0:1],
                                                            start=(c == 0), stop=(c == 2)), r=["cqsq", "ones_bf"], w=[bk(7)])
                s.dve(lambda e, R=R, ti=ti: e.tensor_scalar(out=qst[:R, ti, 0:1], in0=banks[7][:R, 0:1], scalar1=1.0 / 384, scalar2=EPS,
                                                           op0=ALU.mult, op1=ALU.add), r=[bk(7)], w=[("qst", ti)])
                s.act(lambda e, R=R, ti=ti: e.activation(out=qst[:R, ti, 0:1], in_=qst[:R, ti, 0:1], func=AF.Sqrt), r=[("qst", ti)], w=[("qst", ti)])
                s.dve(lambda e, R=R, ti=ti: e.reciprocal(out=qst[:R, ti, 0:1], in_=qst[:R, ti, 0:1]), r=[("qst", ti)], w=[("qst", ti)])
                for half, (pbk, w0, wn) in enumerate(((3, 0, 512), (4, 512, 256))):
                    for c in range(3):
                        s.pe(lambda e, c=c, R=R, lo=lo, pbk=pbk, w0=w0, wn=wn: e.matmul(banks[pbk][:R, 0:wn], lhsT=cqb[:, c, lo:lo + R],
                                                                                      rhs=Wuq[:, c, w0:w0 + wn], start=(c == 0), stop=(c == 2)),
                             r=["cqb", "Wuq"], w=[bk(pbk)])
                    qsf = qs[b2][:].rearrange("p h d -> p (h d)")
                    s.act(lambda e, R=R, ti=ti, pbk=pbk, w0=w0, wn=wn, qsf=qsf: e.activation(out=qsf[:R, w0:w0 + wn], in_=banks[pbk][:R, 0:wn],
                                                                                            func=AF.Copy, scale=qst[:R, ti, 0:1]),
                          r=[bk(pbk), ("qst", ti)], w=["qs"])
                qr = qs[b2][:, :, 64:96]
                s.dve(lambda e, R=R, ti=ti, qr=qr: e.tensor_tensor(out=rq[:R, 0], in0=qr[:R], in1=CC[:R, ti, :].unsqueeze(1).to_broadcast([R, 8, 32]), op=ALU.mult),
                      r=["qs", "CC"], w=["rq"])
                s.dve(lambda e, R=R, ti=ti, qr=qr: e.tensor_tensor(out=rq[:R, 1, :, 0:16], in0=qr[:R, :, 16:32],
                                                                  in1=SS[:R, ti, 0:16].unsqueeze(1).to_broadcast([R, 8, 16]), op=ALU.mult),
                      r=["qs", "SS", "rq"], w=["rq"])
                s.dve(lambda e, R=R, ti=ti, qr=qr: e.tensor_tensor(out=rq[:R, 1, :, 16:32], in0=qr[:R, :, 0:16],
                                                                  in1=SS[:R, ti, 16:32].unsqueeze(1).to_broadcast([R, 8, 16]), op=ALU.mult),
                      r=["qs", "SS", "rq"], w=["rq"])
                s.dve(lambda e, R=R, qr=qr: e.tensor_tensor(out=qr[:R], in0=rq[:R, 0], in1=rq[:R, 1], op=ALU.add), r=["rq"], w=["qs"])
                s.act(lambda e, R=R, b2=b2: e.activation(out=junk2[:R, 0:768], in_=qs[b2][:R].rearrange("p h d -> p (h d)"), func=AF.Square),
                      r=["qs"], w=["junk2"])
                s.dve(lambda e, R=R, ti=ti: e.tensor_reduce(out=ssq[:R, ti, :], in_=junk2[:R, 0:768].rearrange("p (h d) -> p h d", d=96),
                                                           axis=AX.X, op=ALU.add), r=["junk2"], w=[("ssq", ti)])
                s.dve(lambda e, R=R, ti=ti: e.tensor_scalar(out=rsq[:R, ti, :], in0=ssq[:R, ti, :], scalar1=1.0 / 96, scalar2=EPS,
                                                           op0=ALU.mult, op1=ALU.add), r=[("ssq", ti)], w=[("rsq", ti)])
                s.act(lambda e, R=R, ti=ti: e.activation(out=rsq[:R, ti, :], in_=rsq[:R, ti, :], func=AF.Sqrt), r=[("rsq", ti)], w=[("rsq", ti)])
                s.dve(lambda e, R=R, ti=ti: e.reciprocal(out=rsq[:R, ti, :], in_=rsq[:R, ti, :]), r=[("rsq", ti)], w=[("rsq", ti)])
                s.dve(lambda e, R=R, ti=ti, b2=b2: e.tensor_tensor(out=qnb[b2][:R], in0=qs[b2][:R],
                                                                  in1=rsq[:R, ti, :].unsqueeze(2).to_broadcast([R, 8, 96]), op=ALU.mult),
                      r=["qs", ("rsq", ti)], w=["qnb"])
                psQ = bkb[5]
                for h in range(8):
                    s.pe(lambda e, h=h, R=R, b2=b2, psQ=psQ: e.transpose(out=psQ[0:96, h * 128:h * 128 + R], in_=qnb[b2][:R, h, :], identity=ident[:R, :R]),
                         r=["qnb", "ident"], w=[bk(5)])
                s.act(lambda e, R=R, cb=cb, psQ=psQ: e.activation(out=QT[0:96, :, cb:cb + R], in_=psQ[0:96, :].rearrange("p (h r) -> p h r", r=128)[:, :, 0:R],
                                                                 func=AF.Copy, scale=gq2[:, 0:1]), r=[bk(5), "gq2"], w=[("QT", ti)])

        def warm(bank_i, n=10):
            for _ in range(n):
                s.pe(lambda e: e.matmul(banks[bank_i][:, :], lhsT=ident[:, :], rhs=Wukv[:, 0, 0:512], start=True, stop=True), r=["ident", "Wukv"], w=[bk(bank_i)])

        barrier()
        aB = Bump(84, 117)
        oagT = aB([128, 4, NTOK], BF16)
        obgT = aB([128, 4, NTOK], BF16)
        a2 = Bump(117, 150)
        PT = [a2([128, 512], BF16) for i in range(4)]
        Osb = [a2([65, 512]) for i in range(2)]
        if stop_after >= 2:
            pti = 0
            hg = 0
            pend = []

            def finalize(ob, bb, osb, rs, h, g):
                s.act(lambda e: e.activation(out=osb[:, :], in_=banks[ob][0:65, :], func=AF.Copy), r=[bk(ob)], w=[rs])
                s.dve(lambda e: e.reciprocal(out=osb[64:65, :], in_=osb[64:65, :]), r=[rs], w=[rs])
                s.pe(lambda e: e.matmul(banks[bb][0:64, :], lhsT=selrow[:, :], rhs=osb[:, :], start=True, stop=True), r=[rs, "selrow"], w=[bk(bb)])
                po = 64 * (h % 2)
                s.dve(lambda e: e.tensor_tensor(out=oagT[po:po + 64, h // 2, 512 * g:512 * g + 512], in0=osb[0:64, :], in1=banks[bb][0:64, :], op=ALU.mult),
                      r=[rs, bk(bb)], w=[("oagT", g)])

            for g in range(4):
                for h in range(8):
                    ob = 4 + hg % 2
                    bb = 6 + hg % 2
                    osb = Osb[hg % 2]
                    rs = "Osb%d" % (hg % 2)
                    hg += 1
                    blocks = [(16, NMETA, SEQ, 0, False)] + [(j, 128, j * 128, max(0, j - 4 * g) * 128, j >= 4 * g) for j in range(4 * g + 4)]
                    for bi_, (blk, nk, kc0, qlo, diag) in enumerate(blocks):
                        sb = pti % 4
                        pb_ = pti % 4
                        pti += 1
                        qn = 512 - qlo
                        tk = 0 if blk == 16 else blk + 1
                        s.pe(lambda e, sb=sb, nk=nk, kc0=kc0, qlo=qlo, qn=qn, h=h, g=g: e.matmul(
                            banks[sb][:nk, 0:qn], lhsT=KT[0:96, h, kc0:kc0 + nk], rhs=QT[0:96, h, 512 * g + qlo:512 * g + 512], start=True, stop=True),
                            r=[("KT", tk)] + [("QT", 1 + 4 * g + i) for i in range(4)], w=[bk(sb)])
                        s.act(lambda e, sb=sb, nk=nk, qlo=qlo, qn=qn, pb_=pb_: e.activation(out=PT[pb_][:nk, qlo:512], in_=banks[sb][:nk, 0:qn], func=AF.Exp,
                                                                                          scale=SCALE, bias=sbias[:nk, 0:1]),
                              r=[bk(sb), "sbias"], w=["PT%d" % pb_])
                        if diag:
                            s.pool(lambda e, nk=nk, qlo=qlo, pb_=pb_: e.tensor_tensor(out=PT[pb_][:nk, qlo:qlo + 128], in0=PT[pb_][:nk, qlo:qlo + 128],
                                                                                    in1=tri[:nk, :], op=ALU.mult), r=["PT%d" % pb_, "tri"], w=["PT%d" % pb_])
                        if len(pend) >= 3:
                            pend.pop(0)()

                        def pv(ob=ob, bb=bb, osb=osb, rs=rs, nk=nk, blk=blk, qlo=qlo, pb_=pb_, h=h, g=g, first=(bi_ == 0), last=(bi_ == len(blocks) - 1)):
                            s.pe(lambda e: e.matmul(banks[ob][0:65, qlo:512], lhsT=VA[:nk, blk, h, :], rhs=PT[pb_][:nk, qlo:512], start=first, stop=last),
                                 r=["PT%d" % pb_, ("VA", blk), "VA1"], w=[bk(ob)])
                            if last:
                                finalize(ob, bb, osb, rs, h, g)
                        pend.append(pv)
            while pend:
                pend.pop(0)()

        if dbg and stop_after == 2:
            dbgo["d_QT"] = dout("d_QT", [96, 8 * NTOK], BF16)
            s.dma("dbg", lambda e: e.dma_start(out=dbgo["d_QT"][:, :], in_=QT[0:96].rearrange("p k t -> p (k t)")), r=[("QT", t) for t in range(NT)])
            dbgo["d_oa"] = dout("d_oa", [128, 4 * NTOK], BF16)
            s.dma("dbg", lambda e: e.dma_start(out=dbgo["d_oa"][:, :], in_=oagT[:].rearrange("p k t -> p (k t)")), r=[("oagT", t) for t in range(4)])

        barrier()
        if stop_after >= 3:
            a3 = Bump(33, 84)
            a3c = Bump(117, 150)
            WukT = a3([64, 8, 256], BF16)
            qabs = a3([128, 2, NS, 8], BF16)
            qrope = a3([32, NS, 8], BF16)
            NGRP = NPAGES // 4
            ptf = a3c([128, NS * NPAGES])
            pti32 = a3c([128, NS * NPAGES], I32)
            psel = a3c([128, NS * NGRP])
            gidx = a3c([128, NS * NGRP], I32)
            pmask = a3c([128, 4])
            qcol = a3c([128, 1])
            Cf = [a3([128, 4, 288]) for i in range(3)]
            Cb = [a3([128, 4, 296], BF16) for i in range(3)]
            CTs = [a3([128, 4, 384], BF16) for i in range(2)]
            sqj = a3([128, 2048], BF16)
            sqr = a3([128, 128], BF16)
            sst = [a3([128, 4, 32]) for i in range(2)]
            pbf = [a3([128, 4, 8], BF16) for i in range(2)]
            accs = a3([8, 260])
            olb = a3([8, 256], BF16)
            olT = a3([128, 2, 8], BF16)
            for h in range(8):
                for c in range(2):
                    s.pe(lambda e, h=h, c=c: e.transpose(out=bkb[0][0:64, (h * 2 + c) * 128:(h * 2 + c) * 128 + 128] if h < 4 else
                                                         bkb[1][0:64, ((h - 4) * 2 + c) * 128:((h - 4) * 2 + c) * 128 + 128],
                                                         in_=Wukv[:, c, h * 64:(h + 1) * 64], identity=ident[:, :]), r=["Wukv", "ident"], w=[bk(0 if h < 4 else 1)])
            s.dve(lambda e: e.tensor_copy(out=WukT[:, 0:4, :].rearrange("p h c -> p (h c)"), in_=bkb[0][0:64, :]), r=[bk(0)], w=["WukT"])
            s.dve(lambda e: e.tensor_copy(out=WukT[:, 4:8, :].rearrange("p h c -> p (h c)"), in_=bkb[1][0:64, :]), r=[bk(1)], w=["WukT"])
            for h in range(8):
                for c in range(2):
                    s.pe(lambda e, h=h, c=c: e.matmul(banks[2][:, (h * 2 + c) * 4:(h * 2 + c) * 4 + 4], lhsT=WukT[:, h, c * 128:(c + 1) * 128],
                                                     rhs=QT[0:64, h, TP:TP + NS], start=True, stop=True), r=["WukT", ("QT", 17)], w=[bk(2)])
            s.dve(lambda e: e.tensor_copy(out=qabs[:].rearrange("p c b h -> p h c b"), in_=banks[2][:, 0:64].rearrange("p (h c b) -> p h c b", c=2, b=NS)),
                  r=[bk(2)], w=["qabs"])
            s.dve(lambda e: e.tensor_copy(out=qrope[:].rearrange("p b h -> p h b"), in_=QT[64:96, :, TP:TP + NS]), r=[("QT", 17)], w=["qrope"])
            s.dma("c7", lambda e: e.dma_start(out=pti32[:], in_=pt.rearrange("(o b) n -> o (b n)", o=1).partition_broadcast(128)), w=["pti32"])
            s.dve(lambda e: e.tensor_copy(out=ptf[:], in_=pti32[:]), r=["pti32"], w=["ptf"])
            s.pool(lambda e: e.memset(pmask[:], 0.0), w=["pmask"])
            for sl in range(4):
                s.pool(lambda e, sl=sl: e.memset(pmask[32 * sl:32 * sl + 32, sl:sl + 1], 1.0), r=["pmask"], w=["pmask"])
                s.pool(lambda e, sl=sl: e.iota(qcol[32 * sl:32 * sl + 32, :], pattern=[[0, 1]], base=0, channel_multiplier=1,
                                               allow_small_or_imprecise_dtypes=True), r=["qcol"], w=["qcol"])
            s.dve(lambda e: e.tensor_tensor(out=ptf[:].rearrange("p (g s) -> p g s", s=4), in0=ptf[:].rearrange("p (g s) -> p g s", s=4),
                                            in1=pmask[:].unsqueeze(1).to_broadcast([128, NS * NGRP, 4]), op=ALU.mult), r=["ptf", "pmask"], w=["ptf"])
            s.dve(lambda e: e.tensor_reduce(out=psel[:], in_=ptf[:].rearrange("p (g s) -> p g s", s=4), axis=AX.X, op=ALU.add), r=["ptf"], w=["psel"])
            s.dve(lambda e: e.tensor_scalar(out=psel[:], in0=psel[:], scalar1=32.0, scalar2=qcol[:, 0:1], op0=ALU.mult, op1=ALU.add), r=["psel", "qcol"], w=["psel"])
            s.dve(lambda e: e.tensor_copy(out=gidx[:], in_=psel[:]), r=["psel"], w=["gidx"])
            for i in range(3):
                s.pool(lambda e, i=i: e.memset(Cb[i][:, :, 256:264], 1.0), w=[("Cb", i, 0), ("Cb", i, 1)])
            sqj2 = [sqj[:, 0:1024], sqj[:, 1024:2048]]
            sst3 = [a3([128, 2, 32]) for i in range(4)]
            numsb = [a3([128, 16]) for i in range(4)]
            pbf2 = [a3([128, 2, 8], BF16) for i in range(2)]
            CT2 = [a3([128, 2, 384], BF16) for i in range(2)]

            def st_A1(u):
                b, n, hf, R, NTL, gi_, cbj = u[:7]
                if hf == 0:
                    i = gi_ % 3
                    if n < NGRP:
                        col = b * NGRP + n
                        s.dma("gc%d" % i, lambda e: e.indirect_dma_start(out=Cf[i][:].rearrange("p t c -> p (t c)"), out_offset=None, in_=ccomb[:, :],
                              in_offset=bass.IndirectOffsetOnAxis(ap=gidx[:, col:col + 1], axis=0)), r=["gidx"], w=["Cf%d" % i], q="pool")
                    else:
                        s.dma("gs%d" % i, lambda e: e.dma_start(out=Cf[i][0:1, 0, 0:256], in_=lat_s[b:b + 1, :]), r=["lat_s_dram"], w=["Cf%d" % i])
                        s.dma("gr%d" % i, lambda e: e.dma_start(out=Cf[i][0:1, 0, 256:288], in_=kr_s[b:b + 1, :]), r=["kr_s_dram", "Cf%d" % i], w=["Cf%d" % i])
                i = gi_ % 3
                t0 = 2 * hf
                rcb = ("Cb", cbj, hf)
                s.pool(lambda e: e.tensor_copy(out=Cb[cbj][:R, t0:t0 + NTL, 0:256], in_=Cf[i][:R, t0:t0 + NTL, 0:256]), r=["Cf%d" % i], w=[rcb])
                s.pool(lambda e: e.tensor_copy(out=Cb[cbj][:R, t0:t0 + NTL, 264:296], in_=Cf[i][:R, t0:t0 + NTL, 256:288]), r=["Cf%d" % i, rcb], w=[rcb])
                uj = u[7] % 2
                u3 = u[7] % 4
                s.act(lambda e: e.activation(out=sqr[:R, 0:NTL * 32].rearrange("p (t c) -> p t c", c=32), in_=Cf[i][:R, t0:t0 + NTL, 256:288], func=AF.Square),
                      r=["Cf%d" % i], w=["sqr"])
                s.dve(lambda e: e.tensor_reduce(out=sst3[u3][:R, 0:NTL, 8:9], in_=sqr[:R, 0:NTL * 32].rearrange("p (t c) -> p t c", c=32), axis=AX.X, op=ALU.add),
                      r=["sqr"], w=["sst%d" % u3])
                for t in range(NTL):
                    to = t * 384
                    for c in range(2):
                        s.pe(lambda e, c=c, t=t, to=to: e.transpose(out=bkb[uj][:, to + c * 128:to + c * 128 + R], in_=Cb[cbj][:R, t0 + t, c * 128:(c + 1) * 128],
                                                                    identity=ident[:R, :R]), r=[rcb, "ident"], w=[bk(uj)])
                    s.pe(lambda e, t=t, to=to: e.transpose(out=bkb[uj][0:32, to + 256:to + 256 + R], in_=Cb[cbj][:R, t0 + t, 264:296], identity=ident[:R, :R]),
                         r=[rcb, "ident"], w=[bk(uj)])
                pv_ = bkb[uj][:, 0:384 * NTL].rearrange("p (t c) -> p t c", c=384)
                s.dve(lambda e: e.tensor_copy(out=CT2[uj][:, 0:NTL, 0:256].rearrange("p t (c r) -> p t c r", r=128)[:, :, :, 0:R],
                                              in_=pv_[:, :, 0:256].rearrange("p t (c r) -> p t c r", r=128)[:, :, :, 0:R]),
                      r=[bk(uj)], w=["CT2%d" % uj])
                s.dve(lambda e: e.tensor_copy(out=CT2[uj][0:32, 0:NTL, 256:256 + R], in_=pv_[0:32, :, 256:256 + R]),
                      r=[bk(uj), "CT2%d" % uj], w=["CT2%d" % uj])

            def st_A2(u):
                b, n, hf, R, NTL, gi_, cbj = u[:7]
                uj = u[7] % 2
                u3 = u[7] % 4
                kb0 = 4 + 2 * uj
                for t in range(NTL):
                    for c in range(2):
                        s.pe(lambda e, c=c, t=t: e.matmul(banks[kb0 + t][:R, :], lhsT=CT2[uj][:, t, c * 128:c * 128 + R], rhs=Wukv[:, c, 0:512],
                                                         start=(c == 0), stop=(c == 1)), r=["CT2%d" % uj, "Wukv"], w=[bk(kb0 + t)])
                for t in range(NTL):
                    nc0 = u3 * 16 + t * 8
                    for c in range(2):
                        s.pe(lambda e, c=c, t=t, nc0=nc0: e.matmul(banks[3][:R, nc0:nc0 + 8], lhsT=CT2[uj][:, t, c * 128:c * 128 + R], rhs=qabs[:, c, b, :],
                                                                  start=(c == 0), stop=False), r=["CT2%d" % uj, "qabs"], w=[bk(3)])
                    s.pe(lambda e, t=t, nc0=nc0: e.matmul(banks[3][:R, nc0:nc0 + 8], lhsT=CT2[uj][0:32, t, 256:256 + R], rhs=qrope[:, b, :],
                                                         start=False, stop=True), r=["CT2%d" % uj, "qrope"], w=[bk(3)])
                s.dve(lambda e: e.tensor_copy(out=numsb[u3][:R, 0:8 * NTL], in_=banks[3][:R, u3 * 16:u3 * 16 + 8 * NTL]), r=[bk(3)], w=[("numsb", u3)])
                s.act(lambda e: e.activation(out=sqj2[uj][:R, 0:512 * NTL], in_=psum_all[:R, 512 * kb0:512 * (kb0 + NTL)], func=AF.Square),
                      r=[bk(kb0 + t) for t in range(NTL)], w=["sqj%d" % uj])
                s.dve(lambda e: e.tensor_reduce(out=sst3[u3][:R, 0:NTL, 0:8], in_=sqj2[uj][:R, 0:512 * NTL].rearrange("p (t h d) -> p t h d", h=8, d=64),
                                                axis=AX.X, op=ALU.add), r=["sqj%d" % uj], w=["sst%d" % u3])
                sv_ = sst3[u3]
                s.dve(lambda e: e.tensor_tensor(out=sv_[:R, 0:NTL, 16:24], in0=sv_[:R, 0:NTL, 0:8], in1=sv_[:R, 0:NTL, 8:9].to_broadcast([R, NTL, 8]), op=ALU.add),
                      r=["sst%d" % u3], w=["sst%d" % u3])

            def st_B1(u):
                b, n, hf, R, NTL, gi_, cbj = u[:7]
                uj = u[7] % 2
                u3 = u[7] % 4
                rs_ = "sst%d" % u3
                sv = sst3[u3]
                s.act(lambda e: e.activation(out=sv[:R, 0:NTL, 16:24], in_=sv[:R, 0:NTL, 16:24], func=AF.Ln, bias=epsc[:R, 0:1], scale=1.0 / 96), r=[rs_, "epsc"], w=[rs_])
                s.act(lambda e: e.activation(out=sv[:R, 0:NTL, 16:24], in_=sv[:R, 0:NTL, 16:24], func=AF.Exp, scale=-0.5), r=[rs_], w=[rs_])
                s.dve(lambda e: e.tensor_tensor(out=sv[:R, 0:NTL, 24:32], in0=numsb[u3][:R, 0:8 * NTL].rearrange("p (t h) -> p t h", h=8),
                                                in1=sv[:R, 0:NTL, 16:24], op=ALU.mult), r=[("numsb", u3), rs_], w=[rs_])
                s.act(lambda e: e.activation(out=pbf2[uj][:R, 0:NTL, :], in_=sv[:R, 0:NTL, 24:32], func=AF.Exp, scale=SCALE, bias=sbias[:R, 0:1]),
                      r=[rs_, "sbias"], w=["pbf%d" % uj])

            def st_B2(u):
                b, n, hf, R, NTL, gi_, cbj = u[:7]
                uj = u[7] % 2
                t0 = 2 * hf
                for t in range(NTL):
                    s.pe(lambda e, t=t, first=(n == 0 and hf == 0 and t == 0), last=(n == NGRP): e.matmul(
                        banks[2][0:8, 0:257], lhsT=pbf2[uj][:R, t, :], rhs=Cb[cbj][:R, t0 + t, 0:257], start=first, stop=last, skip_group_check=True),
                        r=["pbf%d" % uj, ("Cb", cbj, hf)], w=[bk(2)])

            units = []
            gcount = 0
            for b in range(NS):
                for n in range(NGRP + 1):
                    for hf in range(2 if n < NGRP else 1):
                        R, NTL = (128, 2) if n < NGRP else (1, 1)
                        units.append((b, n, hf, R, NTL, gcount, gcount % 3, len(units)))
                    gcount += 1
            for b in range(NS):
                ub = [u for u in units if u[0] == b]
                for k in range(len(ub) + 3):
                    if 3 <= k:
                        st_B1(ub[k - 3])
                    if k < len(ub):
                        st_A1(ub[k])
                    if 1 <= k <= len(ub):
                        st_A2(ub[k - 1])
                    if 3 <= k:
                        st_B2(ub[k - 3])
                s.act(lambda e: e.activation(out=accs[:, 0:257], in_=banks[2][0:8, 0:257], func=AF.Copy), r=[bk(2)], w=["accs"])
                s.dve(lambda e: e.reciprocal(out=accs[:, 258:259], in_=accs[:, 256:257]), r=["accs"], w=["accs"])
                s.dve(lambda e: e.tensor_scalar(out=olb[:, :], in0=accs[:, 0:256], scalar1=accs[:, 258:259], scalar2=None, op0=ALU.mult), r=["accs"], w=["olb"])
                for c in range(2):
                    s.pe(lambda e, c=c: e.transpose(out=bkb[0][:, c * 8:c * 8 + 8], in_=olb[:, c * 128:(c + 1) * 128], identity=ident[0:8, 0:8]), r=["olb", "ident"], w=[bk(0)])
                s.dve(lambda e: e.tensor_copy(out=olT[:].rearrange("p c h -> p (c h)"), in_=bkb[0][:, 0:16]), r=[bk(0)], w=["olT"])
                for jj in range(4):
                    for c in range(2):
                        s.pe(lambda e, jj=jj, c=c: e.matmul(banks[1][:, jj * 8:jj * 8 + 8], lhsT=Wukv[:, c, 512 + jj * 128:512 + (jj + 1) * 128], rhs=olT[:, c, :],
                                                           start=(c == 0), stop=(c == 1)), r=["olT", "Wukv"], w=[bk(1)])
                for jj in range(4):
                    s.dve(lambda e, jj=jj, b=b: e.tensor_copy(out=oagT[0:64, jj, TP + b:TP + b + 1], in_=banks[1][0:64, jj * 8 + 2 * jj:jj * 8 + 2 * jj + 1]),
                          r=[bk(1)], w=[("oagT", 4)])
                    s.dve(lambda e, jj=jj, b=b: e.tensor_copy(out=oagT[64:128, jj, TP + b:TP + b + 1], in_=banks[1][64:128, jj * 8 + 2 * jj + 1:jj * 8 + 2 * jj + 2]),
                          r=[bk(1)], w=[("oagT", 4)])

        barrier()
        if stop_after >= 4:
            a4 = Bump(0, 84)
            a4c = Bump(117, 150)
            wstf = [a4c([128, 2048]) for i in range(2)]
            W4 = a4([128, 8, 2560], BF16)
            rW4 = load_w_in(W4, 672, 3232, wstf, engs=("pool", "dve"))
            OBQ, OBF, OBI, OGA, OGB = 0, 512, 1024, 1536, 2048
            qS = a4([128, 4, 512], BF16); fS = a4([128, 4, 512]); kS = a4([128, 4, 512]); thb = a4([128, 512])
            vbf = [a4([128, 512], BF16) for i in range(2)]
            sgb = [a4([128, 512], BF16) for i in range(2)]
            PA = a4([128, 4, 128]); rPA = a4([128, 4, 128]); rk4 = a4([128, 4, 128])
            Pt4s = [a4([128, 4, 2]) for i in range(2)]
            qe4 = a4([128, 4, 128], BF16); ke4 = a4([128, 4, 128], BF16); qPt4 = a4([128, 4, 128], BF16); kd24 = a4([128, 4, 128], BF16)
            qPB4 = a4([128, 4, 64], BF16); kdA4 = a4([128, 4, 64], BF16); am4 = a4([128, 4, 128], BF16); kdt4 = a4([128, 4, 128], BF16)
            S32 = a4([128, 4, 128]); Sbf = a4([128, 4, 128], BF16)
            zer = a4([128, 64])
            hst = a4([128, 8]); otmp = a4([128, 512], BF16)
            s.pool(lambda e: e.memset(zer[:], 0.0), w=["zer"])
            s.pool(lambda e: e.memset(S32[:], 0.0), w=[("S32", h) for h in range(4)])
            s.pool(lambda e: e.memset(Sbf[:], 0.0), w=[("Sbf", h) for h in range(4)])
            Sold = a4c([128, NS, 4, 128])
            vsb = a4c([NS, 512])
            onesel = a4c([NS, NS, 128])
            qsel = a4c([128, 4, NS, NS])
            Snew = [a4c([128, 128]) for i in range(2)]
            stmp = a4c([128, 128])
            junk4 = a4c([128, 512], BF16)
            sga = a4c([128, 512], BF16)
            obg = a4c([128, 512], BF16)
            for b in range(NS):
                s.dma("so%d" % b, lambda e, b=b: e.dma_start(out=Sold[:, b], in_=st_in[b].rearrange("h k v -> k h v")), w=[("Sold", b)])
                s.dve(lambda e, b=b: e.tensor_copy(out=onesel[:, b, :], in_=identf[0:NS, b:b + 1].to_broadcast([NS, 128])), r=["identf"], w=["onesel"])
            s.pool(lambda e: e.memset(qsel[:], 0.0), w=["qsel"])

            hgroups = [(SEQ, NMETA, [(0, NMETA, 0)], "meta")]
            hgroups += [(512 * g, 512, [(1 + 4 * g + i, 128, 128 * i) for i in range(4)], "x") for g in range(4)]
            hgroups.append((TP, NS, [(17, NS, 0)], "sample"))
            first_chunk = True
            tn = 0
            for gi, (c0, n, gt, kind) in enumerate(hgroups):
                rx = [("xnT", ti) for ti, _, _ in gt]
                for h in range(4):
                    for (off, pb_, which) in ((OBQ, 0, "q"), (OBF, 1, "f")):
                        for k in range(8):
                            s.pe(lambda e, k=k, h=h, off=off, pb_=pb_, c0=c0, n=n: e.matmul(banks[pb_][:, 0:n], lhsT=W4[:, k, off + h * 128:off + (h + 1) * 128],
                                                                                          rhs=xnT[:, k, c0:c0 + n], start=(k == 0), stop=(k == 7)),
                                 r=rx + [rW4], w=[bk(pb_)])
                        if which == "q":
                            s.act(lambda e, h=h, n=n: e.activation(out=qS[:, h, 0:n], in_=banks[0][:, 0:n], func=AF.Silu), r=[bk(0)], w=["qS"])
                        else:
                            s.act(lambda e, n=n: e.activation(out=thb[:, 0:n], in_=banks[1][:, 0:n], func=AF.Tanh, scale=0.5), r=[bk(1)], w=["thb"])
                            s.dve(lambda e, h=h, n=n: e.tensor_scalar(out=fS[:, h, 0:n], in0=thb[:, 0:n], scalar1=hg_nb[:, h:h + 1], scalar2=hg_a[:, h:h + 1],
                                                                     op0=ALU.mult, op1=ALU.add), r=["thb", "hg_nb", "hg_a"], w=["fS"])
                            s.dve(lambda e, h=h, n=n: e.tensor_scalar(out=kS[:, h, 0:n], in0=thb[:, 0:n], scalar1=hg_nnb[:, h:h + 1], scalar2=hg_nb[:, h:h + 1],
                                                                     op0=ALU.mult, op1=ALU.add), r=["thb", "hg_nb", "hg_nnb"], w=["kS"])
                for (ti, R, lo) in gt:
                    b2 = tn % 2
                    tn += 1
                    cb = c0 + lo
                    for (off, pb_) in ((OBI, 2), (OGB, 3)):
                        for k in range(8):
                            s.pe(lambda e, k=k, off=off, pb_=pb_, cb=cb, R=R: e.matmul(banks[pb_][:R, :], lhsT=xnT[:, k, cb:cb + R], rhs=W4[:, k, off:off + 512],
                                                                                     start=(k == 0), stop=(k == 7)), r=[("xnT", ti), rW4], w=[bk(pb_)])
                    if kind != "sample":
                        s.act(lambda e, R=R, b2=b2: e.activation(out=vbf[b2][:R, :], in_=banks[2][:R, :], func=AF.Copy), r=[bk(2)], w=["vbf%d" % b2])
                    else:
                        s.act(lambda e, R=R: e.activation(out=vsb[:R, :], in_=banks[2][:R, :], func=AF.Copy), r=[bk(2)], w=["vsb"])
                    if kind != "meta":
                        s.act(lambda e, R=R, b2=b2: e.activation(out=sgb[b2][:R, :], in_=banks[3][:R, :], func=AF.Silu), r=[bk(3)], w=["sgb%d" % b2])
                    if kind != "sample":
                        chunks = [(0, R)] if kind == "meta" else [(0, 64), (64, 64)]
                        tpar = tn % 2
                        Pt4 = Pt4s[tpar]
                        rPt = "Pt4_%d" % tpar
                        fv = fS[:, :, lo:lo + R]; qv = qS[:, :, lo:lo + R]; kv = kS[:, :, lo:lo + R]
                        for h in range(4):
                            for (cs, L) in chunks:
                                s.dve(lambda e, cs=cs, L=L, h=h, lo=lo: e.tensor_tensor_scan(out=PA[:, h, cs:cs + L], data0=fS[:, h, lo + cs:lo + cs + L], data1=zer[:, 0:L],
                                                                                          initial=1.0, op0=ALU.mult, op1=ALU.add), r=["fS", "zer"], w=["PA"])
                        s.dve(lambda e, R=R: e.reciprocal(out=rPA[:, :, 0:R], in_=PA[:, :, 0:R]), r=["PA"], w=["rPA"])
                        if kind == "meta":
                            s.dve(lambda e, R=R, Pt4=Pt4: e.tensor_copy(out=Pt4[:, :, 0:1], in_=PA[:, :, R - 1:R]), r=["PA"], w=[rPt])
                            s.dve(lambda e, R=R, kv=kv: e.tensor_tensor(out=rk4[:, :, 0:R], in0=rPA[:, :, 0:R], in1=kv, op=ALU.mult), r=["rPA", "kS"], w=["rk4"])
                            s.dve(lambda e, R=R, Pt4=Pt4: e.tensor_tensor(out=kd24[:, :, 0:R], in0=rk4[:, :, 0:R], in1=Pt4[:, :, 0:1].to_broadcast([128, 4, R]), op=ALU.mult),
                                  r=["rk4", rPt], w=["kd24"])
                        else:
                            c4 = lambda ap: ap.rearrange("p h (c t) -> p h c t", t=64)
                            s.dve(lambda e, Pt4=Pt4: e.tensor_tensor(out=Pt4[:, :, 0:1], in0=PA[:, :, 63:64], in1=PA[:, :, 127:128], op=ALU.mult), r=["PA"], w=[rPt])
                            s.dve(lambda e, Pt4=Pt4: e.tensor_copy(out=Pt4[:, :, 1:2], in_=PA[:, :, 127:128]), r=["PA", rPt], w=[rPt])
                            s.dve(lambda e, qv=qv: e.tensor_tensor(out=qPt4[:, :, :], in0=PA[:, :, :], in1=qv, op=ALU.mult), r=["PA", "qS"], w=["qPt4"])
                            s.dve(lambda e: e.tensor_tensor(out=c4(qe4[:, :, :]), in0=c4(qPt4[:, :, :]), in1=c4(rPA[:, :, :])[:, :, :, 31:32].to_broadcast([128, 4, 2, 64]), op=ALU.mult),
                                  r=["qPt4", "rPA"], w=["qe4"])
                            s.dve(lambda e: e.tensor_copy(out=qPB4[:, :, :], in_=qPt4[:, :, 64:128]), r=["qPt4"], w=["qPB4"])
                            s.dve(lambda e: e.tensor_tensor(out=qPt4[:, :, 64:128], in0=qPB4[:, :, :], in1=PA[:, :, 63:64].to_broadcast([128, 4, 64]), op=ALU.mult),
                                  r=["qPB4", "PA", "qe4"], w=["qPt4"])
                            s.dve(lambda e, kv=kv: e.tensor_tensor(out=rk4[:, :, :], in0=rPA[:, :, :], in1=kv, op=ALU.mult), r=["rPA", "kS"], w=["rk4"])
                            s.dve(lambda e: e.tensor_tensor(out=c4(ke4[:, :, :]), in0=c4(rk4[:, :, :]), in1=c4(PA[:, :, :])[:, :, :, 31:32].to_broadcast([128, 4, 2, 64]), op=ALU.mult),
                                  r=["rk4", "PA"], w=["ke4"])
                            s.dve(lambda e: e.tensor_tensor(out=kdA4[:, :, :], in0=rk4[:, :, 0:64], in1=PA[:, :, 63:64].to_broadcast([128, 4, 64]), op=ALU.mult),
                                  r=["rk4", "PA"], w=["kdA4"])
                            s.dve(lambda e, Pt4=Pt4: e.tensor_tensor(out=c4(kd24[:, :, :]), in0=c4(rk4[:, :, :]), in1=Pt4[:, :, 0:2].unsqueeze(3).to_broadcast([128, 4, 2, 64]), op=ALU.mult),
                                  r=["rk4", rPt], w=["kd24"])
                        for h in range(4):
                            if kind != "meta":
                                s.pe(lambda e, h=h: e.matmul(banks[4][:, h * 128:(h + 1) * 128], lhsT=ke4[:, h, :], rhs=qe4[:, h, :], start=True, stop=True),
                                     r=["ke4", "qe4"], w=[bk(4)])
                                s.pe(lambda e, h=h: e.matmul(banks[4][0:64, h * 128 + 64:(h + 1) * 128], lhsT=kdA4[:, h, :], rhs=qPB4[:, h, :], start=True, stop=True,
                                                                   skip_group_check=True), r=["kdA4", "qPB4"], w=[bk(4)])
                                s.dve(lambda e, h=h: e.tensor_tensor(out=am4[:, h, :], in0=banks[4][:, h * 128:(h + 1) * 128], in1=tri[:, :], op=ALU.mult),
                                      r=[bk(4), "tri"], w=["am%d" % h])
                            s.pe(lambda e, h=h, R=R: e.transpose(out=bkb[5][:R, h * 128:(h + 1) * 128], in_=kd24[:, h, 0:R], identity=ident[:, :]),
                                 r=["kd24", "ident"], w=[bk(5)])
                            s.act(lambda e, h=h, R=R: e.activation(out=kdt4[:R, h, :], in_=bkb[5][:R, h * 128:(h + 1) * 128], func=AF.Copy), r=[bk(5)], w=["kdt%d" % h])
                        if kind != "meta":
                            for h in range(4):
                                s.pe(lambda e, h=h, b2=b2: e.matmul(banks[6][:, h * 128:(h + 1) * 128], lhsT=am4[:, h, :], rhs=vbf[b2][:, h * 128:(h + 1) * 128],
                                                                          start=True, stop=False), r=["am%d" % h, "vbf%d" % b2], w=[bk(6)])
                                s.pe(lambda e, h=h: e.matmul(banks[6][:, h * 128:(h + 1) * 128], lhsT=qPt4[:, h, :], rhs=Sbf[:, h, :], start=False, stop=True),
                                     r=["qPt4", ("Sbf", h)], w=[bk(6)])
                        for h in range(4):
                            s.pe(lambda e, h=h, b2=b2, R=R: e.matmul(banks[7][:, h * 128:(h + 1) * 128], lhsT=kdt4[:R, h, :], rhs=vbf[b2][:R, h * 128:(h + 1) * 128],
                                                                           start=True, stop=True), r=["kdt%d" % h, "vbf%d" % b2], w=[bk(7)])
                            s.dve(lambda e, h=h, Pt4=Pt4: e.scalar_tensor_tensor(out=S32[:, h, :], in0=S32[:, h, :], scalar=Pt4[:, h, 0:1], in1=banks[7][:, h * 128:(h + 1) * 128],
                                                                              op0=ALU.mult, op1=ALU.add), r=[("S32", h), rPt, bk(7)], w=[("S32", h)])
                            s.act(lambda e, h=h: e.activation(out=Sbf[:, h, :], in_=S32[:, h, :], func=AF.Copy), r=[("S32", h)], w=[("Sbf", h)])
                    else:
                        for b in range(NS):
                            s.dve(lambda e, b=b: e.tensor_copy(out=qsel[:, :, b, b], in_=qS[:, :, b]), r=["qS", "qsel"], w=["qsel"])
                        for b in range(NS):
                            s.pe(lambda e, b=b: e.matmul(banks[4][:, :], lhsT=onesel[:, b, :], rhs=vsb[:, :], start=True, stop=True), r=["onesel", "vsb"], w=[bk(4)])
                            for h in range(4):
                                sn = Snew[(b * 4 + h) % 2]
                                rs = "Snew%d" % ((b * 4 + h) % 2)
                                s.dve(lambda e, b=b, h=h: e.tensor_scalar(out=stmp[:, :], in0=Sold[:, b, h, :], scalar1=fS[:, h, b:b + 1], scalar2=None, op0=ALU.mult),
                                      r=[("Sold", b), "fS"], w=["stmp"])
                                s.dve(lambda e, b=b, h=h, sn=sn: e.scalar_tensor_tensor(out=sn[:, :], in0=banks[4][:, h * 128:(h + 1) * 128], scalar=kS[:, h, b:b + 1], in1=stmp[:, :],
                                                                                       op0=ALU.mult, op1=ALU.add), r=[bk(4)] + ["kS", "stmp"], w=[rs])
                                s.dma("hs%d" % ((b * 4 + h) % 2), lambda e, b=b, h=h, sn=sn: e.dma_start(out=hg_s[b, h], in_=sn[:, :]), r=[rs])
                                s.pe(lambda e, b=b, h=h, sn=sn: e.matmul(banks[6][0:NS, h * 128:(h + 1) * 128], lhsT=qsel[:, h, b, :], rhs=sn[:, :],
                                                                        start=(b == 0 and h == 0), stop=(b == NS - 1 and h == 3), skip_group_check=True), r=["qsel", rs], w=[bk(6)])
                    if kind == "meta":
                        continue
                    s.act(lambda e, R=R: e.activation(out=junk4[:R, :], in_=banks[6][:R, :], func=AF.Square), r=[bk(6)], w=["junk4"])
                    s.dve(lambda e, R=R: e.tensor_reduce(out=hst[:R, 0:4], in_=junk4[:R, :].rearrange("p (h d) -> p h d", d=128), axis=AX.X, op=ALU.add), r=["junk4"], w=["hst"])
                    s.dve(lambda e, R=R: e.tensor_scalar(out=hst[:R, 4:8], in0=hst[:R, 0:4], scalar1=1.0 / 128, scalar2=EPS, op0=ALU.mult, op1=ALU.add), r=["hst"], w=["hst"])
                    s.pool(lambda e, R=R: e.tensor_tensor(out=hst[:R, 4:8], in0=hst[:R, 4:8], in1=neghalf[:R, 0:4], op=ALU.pow), r=["hst", "neghalf"], w=["hst"])
                    s.dve(lambda e, R=R: e.tensor_tensor(out=otmp[:R, :].rearrange("p (h d) -> p h d", d=128), in0=banks[6][:R, :].rearrange("p (h d) -> p h d", d=128),
                                                        in1=hst[:R, 4:8].unsqueeze(2).to_broadcast([R, 4, 128]), op=ALU.mult), r=[bk(6), "hst"], w=["otmp"])
                    s.dve(lambda e, R=R, b2=b2: e.tensor_tensor(out=obg[:R, :], in0=otmp[:R, :], in1=sgb[b2][:R, :], op=ALU.mult), r=["otmp", "sgb%d" % b2], w=["obg"])
                    for c in range(4):
                        s.pe(lambda e, R=R, c=c: e.transpose(out=bkb[5][:, c * 128:c * 128 + R], in_=obg[:R, c * 128:(c + 1) * 128], identity=ident[:R, :R]), r=["obg", "ident"], w=[bk(5)])
                    s.act(lambda e, R=R, cb=cb: e.activation(out=obgT[:, :, cb:cb + R], in_=bkb[5][:, 0:512].rearrange("p (c r) -> p c r", r=128)[:, :, 0:R], func=AF.Copy),
                          r=[bk(5)], w=[("obgT", gi)])
                if kind != "meta":
                    for c in range(4):
                        for k in range(8):
                            s.pe(lambda e, k=k, c=c, c0=c0, n=n: e.matmul(banks[c % 2][:, 0:n], lhsT=W4[:, k, OGA + c * 128:OGA + (c + 1) * 128], rhs=xnT[:, k, c0:c0 + n],
                                                                         start=(k == 0), stop=(k == 7)), r=rx + [rW4], w=[bk(c % 2)])
                        s.act(lambda e, c=c, n=n: e.activation(out=sga[:, 0:n], in_=banks[c % 2][:, 0:n], func=AF.Silu), r=[bk(c % 2)], w=["sga"])
                        s.dve(lambda e, c=c, n=n, c0=c0: e.tensor_tensor(out=oagT[:, c, c0:c0 + n], in0=oagT[:, c, c0:c0 + n], in1=sga[:, 0:n], op=ALU.mult),
                              r=["sga", ("oagT", gi - 1)], w=[("oagT", gi - 1)])
            s.dma("hp", lambda e: e.dma_start(out=hg_p.rearrange("h k v -> k h v"), in_=S32[:, :, :]), r=[("S32", h) for h in range(4)])

        barrier()
        if stop_after >= 5:
            a5 = Bump(0, 84)
            a5c = Bump(117, 150)
            wstf = [a5c([128, 2048]) for i in range(2)]
            W5 = a5([128, 8, 2048], BF16)
            rW5 = load_w_in(W5, 3232, 5280, wstf, engs=("pool", "dve"))
            Woa = a5([128, 4, 1024], BF16); Wob = a5([128, 4, 1024], BF16); Wo = a5([128, 8, 1024], BF16)
            zT = a5([128, 8, 512], BF16)
            tha = a5([128, 512]); thm = a5([128, 512]); t1 = a5([128, 512])
            xin = [a5c([128, 1024]) for i in range(2)]
            yout = [a5c([128, 1024]) for i in range(2)]
            wi = 0
            for (src_w, dst_w, nk, kind) in ((w_oa, Woa, 4, "plain"), (w_ob, Wob, 4, "gbn"), (w_o, Wo, 8, "half")):
                for k in range(nk):
                    for hf in range(2):
                        b = wi % 2
                        wi += 1
                        stv = wstf[b][:, 0:512]
                        s.dma("wst%d" % b, lambda e, stv=stv, src_w=src_w, k=k, hf=hf: e.dma_start(out=stv, in_=src_w[k * 128:(k + 1) * 128, hf * 512:(hf + 1) * 512]), w=["wst%d" % b])
                        dstv = dst_w[:, k, hf * 512:(hf + 1) * 512]
                        rsw = ("Wm", id(dst_w))
                        sc = None if kind == "plain" else (gbn_c[:, 0:1] if kind == "gbn" else 0.5)
                        eng = ("pool", "act", "dve")[wi % 3]
                        if eng == "act":
                            if sc is None:
                                s.act(lambda e, stv=stv, dstv=dstv: e.activation(out=dstv, in_=stv, func=AF.Copy), r=["wst%d" % b], w=[rsw])
                            else:
                                s.act(lambda e, stv=stv, dstv=dstv, sc=sc: e.activation(out=dstv, in_=stv, func=AF.Copy, scale=sc), r=["wst%d" % b, "gbn_c"], w=[rsw])
                        else:
                            if sc is None:
                                s.add(eng, lambda e, stv=stv, dstv=dstv: e.tensor_copy(out=dstv, in_=stv), ["wst%d" % b], [rsw])
                            else:
                                s.add(eng, lambda e, stv=stv, dstv=dstv, sc=sc: e.tensor_scalar(out=dstv, in0=stv, scalar1=sc, scalar2=None, op0=ALU.mult),
                                      ["wst%d" % b, "gbn_c"], [rsw])
            rWoa, rWob, rWo = ("Wm", id(Woa)), ("Wm", id(Wob)), ("Wm", id(Wo))
            mgroups = [(512 * g, 512, [(1 + 4 * g + i, 128, 128 * i) for i in range(4)], g, g + 1) for g in range(4)]
            mgroups.append((TP, NS, [(17, NS, 0)], 4, 5))
            tn = 0
            for (c0, n, gt, og, hgi) in mgroups:
                rx = [("xnT", ti) for ti, _, _ in gt]
                for c in range(8):
                    bs = 4 * (c % 2)
                    for j in range(4):
                        s.pe(lambda e, c=c, j=j, c0=c0, n=n, bs=bs: e.matmul(banks[bs][:, 0:n], lhsT=Woa[:, j, c * 128:(c + 1) * 128], rhs=oagT[:, j, c0:c0 + n], start=(j == 0), stop=(j == 3)),
                             r=[rWoa, ("oagT", og)], w=[bk(bs)])
                    for j in range(4):
                        s.pe(lambda e, c=c, j=j, c0=c0, n=n, bs=bs: e.matmul(banks[bs + 1][:, 0:n], lhsT=Wob[:, j, c * 128:(c + 1) * 128], rhs=obgT[:, j, c0:c0 + n], start=(j == 0), stop=(j == 3)),
                             r=[rWob, ("obgT", hgi)], w=[bk(bs + 1)])
                    for (off, pb_) in ((0, bs + 2), (1024, bs + 3)):
                        for k in range(8):
                            s.pe(lambda e, c=c, k=k, off=off, pb_=pb_, c0=c0, n=n: e.matmul(banks[pb_][:, 0:n], lhsT=W5[:, k, off + c * 128:off + (c + 1) * 128], rhs=xnT[:, k, c0:c0 + n],
                                                                                          start=(k == 0), stop=(k == 7)), r=rx + [rW5], w=[bk(pb_)])
                    s.act(lambda e, n=n, bs=bs: e.activation(out=tha[:, 0:n], in_=banks[bs + 2][:, 0:n], func=AF.Tanh, scale=0.5), r=[bk(bs + 2)], w=["tha"])
                    s.act(lambda e, n=n, bs=bs: e.activation(out=thm[:, 0:n], in_=banks[bs + 3][:, 0:n], func=AF.Tanh, scale=0.5), r=[bk(bs + 3)], w=["thm"])
                    s.dve(lambda e, n=n, bs=bs: e.scalar_tensor_tensor(out=t1[:, 0:n], in0=tha[:, 0:n], scalar=1.0, in1=banks[bs][:, 0:n], op0=ALU.add, op1=ALU.mult), r=["tha", bk(bs)], w=["t1"])
                    s.dve(lambda e, n=n, bs=bs: e.scalar_tensor_tensor(out=thm[:, 0:n], in0=thm[:, 0:n], scalar=1.0, in1=banks[bs + 1][:, 0:n], op0=ALU.add, op1=ALU.mult), r=["thm", bk(bs + 1)], w=["thm"])
                    s.dve(lambda e, c=c, n=n: e.tensor_tensor(out=zT[:, c, 0:n], in0=t1[:, 0:n], in1=thm[:, 0:n], op=ALU.add), r=["t1", "thm"], w=["zT"])
                for (ti, R, lo) in gt:
                    b2 = tn % 2
                    tn += 1
                    srcx = xs[:, :] if ti == 17 else xp[(ti - 1) * 128:ti * 128, :]
                    dsty = y_s[:, :] if ti == 17 else y_p[(ti - 1) * 128:ti * 128, :]
                    s.dma("xin%d" % b2, lambda e, b2=b2, R=R, srcx=srcx: e.dma_start(out=xin[b2][:R, :], in_=srcx), w=["xin%d" % b2])
                    for hf in range(2):
                        pb_ = 4 + hf
                        for c in range(8):
                            s.pe(lambda e, c=c, hf=hf, pb_=pb_, R=R, lo=lo: e.matmul(banks[pb_][:R, :], lhsT=zT[:, c, lo:lo + R], rhs=Wo[:, c, hf * 512:(hf + 1) * 512],
                                                                                   start=(c == 0), stop=(c == 7)), r=["zT", rWo], w=[bk(pb_)])
                        s.dve(lambda e, hf=hf, pb_=pb_, R=R, b2=b2: e.tensor_tensor(out=yout[b2][:R, hf * 512:(hf + 1) * 512], in0=banks[pb_][:R, :], in1=xin[b2][:R, hf * 512:(hf + 1) * 512],
                                                                                  op=ALU.add), r=[bk(pb_), "xin%d" % b2], w=["yout%d" % b2])
                    s.dma("yo%d" % b2, lambda e, b2=b2, R=R, dsty=dsty: e.dma_start(out=dsty, in_=yout[b2][:R, :]), r=["yout%d" % b2])

        s.emit(st)
    return nc


_NC = {}


def _get_nc(dbg=False):
    if dbg not in _NC:
        _NC[dbg] = build(dbg)
    return _NC[dbg]


def make_in_maps(x_prompt, x_sample, cache_latent, cache_krope, state_hgrn, page_table, meta_tokens,
                 norm_g, w_in, g_cq, w_uq, g_ckv, w_uk, w_uv, g_qn, g_kn, lb_logits, g_bn, w_oa, w_ob, w_o):
    f = lambda a: np.ascontiguousarray(np.asarray(a, dtype=np.float32))
    ccomb = np.concatenate([f(cache_latent)[0].reshape(NPOOL * 128, 256), f(cache_krope)[0].reshape(NPOOL * 128, 32)],
                           axis=1).reshape(NPOOL * 32, 4 * 288)
    shared = dict(meta=f(meta_tokens), ccomb=ccomb, norm_g=f(norm_g), w_in=f(w_in)[0], g_cq=f(g_cq),
                  w_uq=f(w_uq)[0], g_ckv=f(g_ckv), w_uk=f(w_uk)[0].reshape(256, 512), w_uv=f(w_uv)[0].reshape(256, 512),
                  g_qn=f(g_qn), g_kn=f(g_kn), lb=f(lb_logits), g_bn=f(g_bn), w_oa=f(w_oa)[0], w_ob=f(w_ob)[0], w_o=f(w_o)[0])
    xpf = f(x_prompt); xsf = f(x_sample); stf = f(state_hgrn)
    ptab = np.ascontiguousarray(np.asarray(page_table, dtype=np.int32))
    maps = []
    for c in range(NCORES):
        m = dict(shared)
        m["xp"] = xpf[c]
        m["xs"] = xsf[NS * c:NS * (c + 1), 0]
        m["st_in"] = stf[0, NS * c:NS * (c + 1)]
        m["pt"] = ptab[NS * c:NS * (c + 1)]
        maps.append(m)
    return maps


def kernel(**inputs):
    nc = _get_nc(False)
    maps = make_in_maps(**inputs)
    res = run_bass_kernel_spmd(nc, maps, core_ids=list(range(NCORES)))
    r = res.results
    cat = lambda k: np.stack([np.asarray(x[k]) for x in r], axis=0)
    y_p = cat("y_p")
    y_s = np.concatenate([np.asarray(x["y_s"]) for x in r], axis=0)[:, None, :]
    lat_p = cat("lat_p")[None]
    kr_p = cat("kr_p")[None]
    hg_p = cat("hg_p")[None]
    lat_s = np.concatenate([np.asarray(x["lat_s"]) for x in r], axis=0)[None, :, None, :]
    kr_s = np.concatenate([np.asarray(x["kr_s"]) for x in r], axis=0)[None, :, None, :]
    hg_s = np.concatenate([np.asarray(x["hg_s"]) for x in r], axis=0)[None]
    return (y_p, y_s, lat_p, kr_p, hg_p, lat_s, kr_s, hg_s)
```

```python
import contextlib
import math
import numpy as np
import concourse.bass as bass
import concourse.mybir as mybir
from concourse.bass_utils import run_bass_kernel_spmd

F32 = mybir.dt.float32
BF16 = mybir.dt.bfloat16
I32 = mybir.dt.int32
AF = mybir.ActivationFunctionType
ALU = mybir.AluOpType
AX = mybir.AxisListType

NCORES = 8
D = 1024
SEQ = 2048
NMETA = 16
TP = SEQ + NMETA
NS = 4
NTOK = TP + NS
INC = 5280
EPS = 1e-6
NPOOL = 5120
NPAGES = 128
SCALE = 96 ** -0.5
SBIAS = -(96 ** 0.5)
COMPUTE = ("pe", "act", "dve", "pool")
DEBUG = False


class Sched:
    def __init__(self, nc):
        self.nc = nc
        self.ops = []
        self.lastw = {}
        self.readers = {}
        self.chan_last = {}
        self.chan_count = {}
        self.pending = {}

    def barrier(self, fns):
        ids = [self.add(e, fns[e], (), [("bar", e)]) for e in COMPUTE]
        allc = set(ids) | set(self.chan_last.values())
        for e in COMPUTE + ("sp",):
            self.pending.setdefault(e, set()).update(allc)

    def add(self, eng, fn, reads=(), writes=(), dma=None):
        idx = len(self.ops)
        deps = set(self.pending.pop(eng, ()))
        for r in reads:
            w = self.lastw.get(r)
            if w is not None:
                deps.add(w)
        for w_ in writes:
            w = self.lastw.get(w_)
            if w is not None:
                deps.add(w)
            rd = self.readers.get(w_)
            if rd:
                deps.update(rd.values())
        if dma is not None:
            if dma in self.chan_last:
                deps.add(self.chan_last[dma])
            self.chan_count[dma] = self.chan_count.get(dma, 0) + 1
        op = dict(eng=eng, fn=fn, deps=deps, dma=dma, idx=idx,
                  cnt=self.chan_count.get(dma, 0) if dma is not None else 0)
        self.ops.append(op)
        for r in reads:
            d = self.readers.setdefault(r, {})
            d[eng if dma is None else ("dma", dma)] = idx
        for w_ in writes:
            self.lastw[w_] = idx
            self.readers[w_] = {}
        if dma is not None:
            self.chan_last[dma] = idx
        return idx

    def pe(self, fn, r=(), w=()):
        return self.add("pe", fn, r, w)

    def act(self, fn, r=(), w=()):
        return self.add("act", fn, r, w)

    def dve(self, fn, r=(), w=()):
        return self.add("dve", fn, r, w)

    def pool(self, fn, r=(), w=()):
        return self.add("pool", fn, r, w)

    def dma(self, chan, fn, r=(), w=(), q="sp"):
        return self.add(q, fn, r, w, dma=chan)

    def emit(self, stack):
        nc = self.nc
        ops = self.ops
        waited = {}
        signal = set()
        for op in ops:
            e = op["eng"]
            waits = []
            for d in sorted(op["deps"]):
                p = ops[d]
                if p["dma"] is not None:
                    key = ("dma", p["dma"])
                    if waited.get((e, key), 0) >= p["cnt"]:
                        continue
                    waited[(e, key)] = p["cnt"]
                    waits.append(("dma", p["dma"], p["cnt"]))
                else:
                    pe_ = p["eng"]
                    if pe_ == "pe" and e == "pe" and op["dma"] is None:
                        continue
                    key = ("eng", pe_)
                    if waited.get((e, key), -1) >= d:
                        continue
                    waited[(e, key)] = d
                    signal.add(d)
                    waits.append(("eng", pe_, d))
            op["waits"] = waits
        cnt = {e: 0 for e in COMPUTE}
        for op in ops:
            if op["idx"] in signal:
                cnt[op["eng"]] += 1
                op["ticket"] = cnt[op["eng"]]
        esem = {e: stack.enter_context(nc.semaphore("s_" + e)) for e in COMPUTE}
        csem = {c: stack.enter_context(nc.semaphore("c_%d" % i))
                for i, c in enumerate(self.chan_count)}
        per = {e: [] for e in COMPUTE + ("sp",)}
        for op in ops:
            per[op["eng"]].append(op)
        chan_count = self.chan_count

        def run(engh, lst, final=False):
            for op in lst:
                for w in op["waits"]:
                    if w[0] == "dma":
                        engh.wait_ge(csem[w[1]], 16 * w[2])
                    else:
                        engh.wait_ge(esem[w[1]], ops[w[2]]["ticket"])
                inst = op["fn"](engh)
                if op["dma"] is not None:
                    inst.then_inc(csem[op["dma"]], 16)
                elif op["idx"] in signal:
                    inst.then_inc(esem[op["eng"]], 1)
            if final:
                for c, n in chan_count.items():
                    engh.wait_ge(csem[c], 16 * n)

        block = stack.enter_context(nc.Block())

        @block.sync
        def _(e):
            run(e, per["sp"], final=True)

        @block.tensor
        def _(e):
            run(e, per["pe"])

        @block.scalar
        def _(e):
            run(e, per["act"])

        @block.vector
        def _(e):
            run(e, per["dve"])

        @block.gpsimd
        def _(e):
            run(e, per["pool"])


def _inv_freq():
    j = np.arange(16, dtype=np.float32)
    return (np.float32(10000.0) ** (-j / np.float32(16))).astype(np.float32)


def build(dbg=False, stop_after=99):
    nc = bass.Bass("TRN2", target_bir_lowering=False)
    din = lambda n, s, d=F32: nc.dram_tensor(n, s, d, kind="ExternalInput").ap()
    dout = lambda n, s, d=F32: nc.dram_tensor(n, s, d, kind="ExternalOutput").ap()
    xp = din("xp", [SEQ, D]); xs = din("xs", [NS, D]); meta = din("meta", [NMETA, D])
    ccomb = din("ccomb", [NPOOL * 32, 4 * 288])
    st_in = din("st_in", [NS, 4, 128, 128]); pt = din("pt", [NS, NPAGES], I32)
    norm_g = din("norm_g", [1, D]); w_in = din("w_in", [D, INC]); g_cq = din("g_cq", [1, 384])
    w_uq = din("w_uq", [384, 768]); g_ckv = din("g_ckv", [1, 256]); w_uk = din("w_uk", [256, 512])
    w_uv = din("w_uv", [256, 512]); g_qn = din("g_qn", [1, 96]); g_kn = din("g_kn", [1, 96])
    lb = din("lb", [2, 512]); g_bn = din("g_bn", [1, 128]); w_oa = din("w_oa", [512, D])
    w_ob = din("w_ob", [512, D]); w_o = din("w_o", [D, D])
    y_p = dout("y_p", [SEQ, D]); y_s = dout("y_s", [NS, D]); lat_p = dout("lat_p", [TP, 256])
    kr_p = dout("kr_p", [TP, 32]); hg_p = dout("hg_p", [4, 128, 128]); lat_s = dout("lat_s", [NS, 256])
    kr_s = dout("kr_s", [NS, 32]); hg_s = dout("hg_s", [NS, 4, 128, 128])
    dbgo = {}

    with contextlib.ExitStack() as st:
        T = lambda n, s, d=F32: st.enter_context(nc.sbuf_tensor(n, s, d))
        s = Sched(nc)
        psum_all = st.enter_context(nc.psum_tensor("psum_all", [128, 4096], F32))
        banks = [psum_all[:, 512 * i:512 * (i + 1)] for i in range(8)]
        bkb = [b_.bitcast(BF16) for b_ in banks]

        def bk(i):
            return "bank%d" % i

        AKB = 150
        arena = T("arena", [128, AKB * 256])

        class Bump:
            def __init__(self, lo_kb, hi_kb):
                self.off = int(lo_kb * 256); self.hi = int(hi_kb * 256)

            def __call__(self, shape, dt=F32):
                free = list(shape[1:])
                n = 1
                for d_ in free:
                    n *= d_
                words = (n + 1) // 2 if dt == BF16 else n
                words = (words + 7) // 8 * 8
                v = arena[0:shape[0], self.off:self.off + words]
                self.off += words
                assert self.off <= self.hi, (self.off, self.hi)
                if dt == BF16:
                    v = v.bitcast(BF16)[:, 0:n]
                elif dt == I32:
                    v = v.bitcast(I32)[:, 0:n]
                else:
                    v = v[:, 0:n]
                if len(free) > 1:
                    names = " ".join("a%d" % i for i in range(len(free)))
                    v = v.rearrange("p (%s) -> p %s" % (names, names), **{"a%d" % i: free[i] for i in range(1, len(free))})
                return v

        scr_act = T("scr_act", [1, 8]); scr_dve = T("scr_dve", [1, 8]); scr_pool = T("scr_pool", [1, 8])
        scr_bf = T("scr_bf", [1, 8], BF16)

        def barrier():
            s.pe(lambda e: e.matmul(banks[7][0:1, 0:8], lhsT=scr_bf[0:1, 0:1], rhs=scr_bf[0:1, 0:8], start=True, stop=True), r=["scr_bf"], w=[bk(7), bk(4), bk(5)])
            s.barrier({
                "pe": lambda e: e.matmul(banks[7][0:1, 0:8], lhsT=scr_bf[0:1, 0:1], rhs=scr_bf[0:1, 0:8], start=True, stop=True),
                "act": lambda e: e.activation(out=scr_act[:], in_=scr_act[:], func=AF.Copy),
                "dve": lambda e: e.memset(scr_dve[:], 0.0),
                "pool": lambda e: e.memset(scr_pool[:], 0.0),
            })
        s.pool(lambda e: e.memset(scr_bf[:], 0.0), w=["scr_bf"])
        s.pool(lambda e: e.memset(scr_act[:], 0.0), w=["scr_act"])

        ident = T("ident", [128, 128], BF16)
        identf = T("identf", [128, 128], F32)
        tri = T("tri", [128, 128], BF16)
        btri = T("btri", [128, 128], BF16)
        for tt, tn_ in ((ident, "ident"), (identf, "identf")):
            s.pool(lambda e, tt=tt: e.memset(tt[:], 0.0), w=[tn_])
            s.pool(lambda e, tt=tt: e.affine_select(out=tt[:], in_=tt[:], pattern=[[1, 128]], compare_op=ALU.not_equal,
                                                    fill=1.0, base=0, channel_multiplier=-1), r=[tn_], w=[tn_])
        s.pool(lambda e: e.memset(tri[:], 1.0), w=["tri"])
        s.pool(lambda e: e.affine_select(out=tri[:], in_=tri[:], pattern=[[1, 128]], compare_op=ALU.is_ge,
                                         fill=0.0, base=0, channel_multiplier=-1), r=["tri"], w=["tri"])
        s.pool(lambda e: e.tensor_copy(out=btri[:], in_=tri[:]), r=["tri"], w=["btri"])
        s.pool(lambda e: e.memset(btri[0:64, 64:128], 0.0), r=["btri"], w=["btri"])
        neghalf = T("neghalf", [128, 16])
        s.pool(lambda e: e.memset(neghalf[:], -0.5), w=["neghalf"])
        ones_bf = T("ones_bf", [128, 64], BF16)
        s.pool(lambda e: e.memset(ones_bf[:], 1.0), w=["ones_bf"])
        sbias = T("sbias", [128, 1])
        s.pool(lambda e: e.memset(sbias[:], SBIAS), w=["sbias"])
        epsc = T("epsc", [128, 1])
        s.pool(lambda e: e.memset(epsc[:], EPS), w=["epsc"])
        selrow = T("selrow", [65, 64])
        s.pool(lambda e: e.memset(selrow[:], 0.0), w=["selrow"])
        s.pool(lambda e: e.memset(selrow[64:65, :], 1.0), r=["selrow"], w=["selrow"])

        NT = 18
        pos = T("pos", [128, NT])
        ang = T("ang", [128, 2, NT, 16])
        invf = T("invf", [128, 16])
        kq = T("kq", [128, 2 * NT * 16])
        kqi = T("kqi", [128, 2 * NT * 16], I32)
        CC = T("CC", [128, NT, 32])
        SS = T("SS", [128, NT, 32])
        s.pool(lambda e: e.iota(pos[:], pattern=[[128, NT]], base=NMETA - 128, channel_multiplier=1,
                                allow_small_or_imprecise_dtypes=True), w=["pos"])
        s.pool(lambda e: e.iota(pos[:, 0:1], pattern=[[0, 1]], base=0, channel_multiplier=1,
                                allow_small_or_imprecise_dtypes=True), r=["pos"], w=["pos"])
        s.pool(lambda e: e.memset(pos[:, NT - 1:NT], 16384.0), r=["pos"], w=["pos"])
        for j, v in enumerate(_inv_freq()):
            s.pool(lambda e, j=j, v=float(v): e.memset(invf[:, j:j + 1], v), r=["invf"], w=["invf"])
        s.dve(lambda e: e.tensor_tensor(out=ang[:, 0], in0=pos[:].unsqueeze(2).to_broadcast([128, NT, 16]),
                                        in1=invf[:].unsqueeze(1).to_broadcast([128, NT, 16]), op=ALU.mult),
              r=["pos", "invf"], w=["ang"])
        s.dve(lambda e: e.tensor_scalar(out=ang[:, 1], in0=ang[:, 0], scalar1=math.pi / 2, scalar2=None, op0=ALU.add),
              r=["ang"], w=["ang"])
        angf = ang[:].rearrange("p a t j -> p (a t j)")
        s.dve(lambda e: e.tensor_scalar(out=kq[:], in0=angf, scalar1=1.0 / (2 * math.pi), scalar2=None, op0=ALU.mult),
              r=["ang"], w=["kq"])
        s.dve(lambda e: e.tensor_copy(out=kqi[:], in_=kq[:]), r=["kq"], w=["kqi"])
        s.dve(lambda e: e.tensor_copy(out=kq[:], in_=kqi[:]), r=["kqi"], w=["kq"])
        C1 = 6.28125
        C2 = 2 * math.pi - C1
        s.dve(lambda e: e.scalar_tensor_tensor(out=angf, in0=kq[:], scalar=-C1, in1=angf, op0=ALU.mult, op1=ALU.add),
              r=["kq", "ang"], w=["ang"])
        s.dve(lambda e: e.scalar_tensor_tensor(out=angf, in0=kq[:], scalar=-C2, in1=angf, op0=ALU.mult, op1=ALU.add),
              r=["kq", "ang"], w=["ang"])
        s.dve(lambda e: e.tensor_scalar(out=kq[:], in0=angf, scalar1=math.pi, scalar2=-2 * math.pi, op0=ALU.is_gt, op1=ALU.mult),
              r=["ang"], w=["kq"])
        s.dve(lambda e: e.tensor_tensor(out=angf, in0=angf, in1=kq[:], op=ALU.add), r=["ang", "kq"], w=["ang"])
        s.dve(lambda e: e.tensor_scalar(out=kq[:], in0=angf, scalar1=-math.pi, scalar2=2 * math.pi, op0=ALU.is_lt, op1=ALU.mult),
              r=["ang"], w=["kq"])
        s.dve(lambda e: e.tensor_tensor(out=angf, in0=angf, in1=kq[:], op=ALU.add), r=["ang", "kq"], w=["ang"])
        s.dve(lambda e: e.tensor_scalar(out=angf, in0=angf, scalar1=-math.pi, scalar2=math.pi, op0=ALU.max, op1=ALU.min),
              r=["ang"], w=["ang"])
        s.act(lambda e: e.activation(out=SS[:, :, 16:32], in_=ang[:, 0], func=AF.Sin), r=["ang"], w=["SS"])
        s.act(lambda e: e.activation(out=CC[:, :, 0:16], in_=ang[:, 1], func=AF.Sin), r=["ang"], w=["CC"])
        s.dve(lambda e: e.tensor_copy(out=CC[:, :, 16:32], in_=CC[:, :, 0:16]), r=["CC"], w=["CC"])
        s.dve(lambda e: e.tensor_scalar(out=SS[:, :, 0:16], in0=SS[:, :, 16:32], scalar1=-1.0, scalar2=None, op0=ALU.mult),
              r=["SS"], w=["SS"])

        gckv_bc = T("gckv_bc", [128, 256])
        s.dma("c0", lambda e: e.dma_start(out=gckv_bc[:], in_=g_ckv[0:1, :].partition_broadcast(128)), w=["gckv_bc"])
        normg_c = T("normg_c", [128, 8])
        gcq_c = T("gcq_c", [128, 3])
        gq2_c = T("gq2_c", [96, 2])
        gbn_c = T("gbn_c", [128, 1])
        lb_c = T("lb_c", [128, 2, 4])
        s.dma("c1", lambda e: e.dma_start(out=normg_c[:], in_=norm_g.rearrange("o (k p) -> p (o k)", p=128),
                                          allow_slow_non_contiguous=True), w=["normg_c"])
        s.dma("c2", lambda e: e.dma_start(out=gcq_c[:], in_=g_cq.rearrange("o (k p) -> p (o k)", p=128),
                                          allow_slow_non_contiguous=True), w=["gcq_c"])
        s.dma("c3", lambda e: e.dma_start(out=gq2_c[:, 0:1], in_=g_qn.rearrange("o p -> p o"),
                                          allow_slow_non_contiguous=True), w=["gq2_c"])
        s.dma("c4", lambda e: e.dma_start(out=gq2_c[:, 1:2], in_=g_kn.rearrange("o p -> p o"),
                                          allow_slow_non_contiguous=True), r=["gq2_c"], w=["gq2_c"])
        s.dma("c5", lambda e: e.dma_start(out=gbn_c[:], in_=g_bn.rearrange("o p -> p o"),
                                          allow_slow_non_contiguous=True), w=["gbn_c"])
        s.dma("c6", lambda e: e.dma_start(out=lb_c[:], in_=lb.rearrange("r (h p) -> p r h", p=128),
                                          allow_slow_non_contiguous=True), w=["lb_c"])
        gq2 = T("gq2", [96, 1])
        s.dve(lambda e: e.tensor_tensor(out=gq2[:], in0=gq2_c[:, 0:1], in1=gq2_c[:, 1:2], op=ALU.mult), r=["gq2_c"], w=["gq2"])
        hg_a = T("hg_a", [128, 4]); hg_nb = T("hg_nb", [128, 4]); hg_nnb = T("hg_nnb", [128, 4]); hg_t = T("hg_t", [128, 4])
        s.dve(lambda e: e.tensor_tensor(out=hg_t[:], in0=lb_c[:, 0, :], in1=lb_c[:, 1, :], op=ALU.subtract), r=["lb_c"], w=["hg_t"])
        s.act(lambda e: e.activation(out=hg_t[:], in_=hg_t[:], func=AF.Tanh, scale=0.5), r=["hg_t"], w=["hg_t"])
        s.dve(lambda e: e.tensor_scalar(out=hg_a[:], in0=hg_t[:], scalar1=0.25, scalar2=0.75, op0=ALU.mult, op1=ALU.add), r=["hg_t"], w=["hg_a"])
        s.dve(lambda e: e.tensor_scalar(out=hg_nb[:], in0=hg_t[:], scalar1=-0.25, scalar2=0.25, op0=ALU.mult, op1=ALU.add), r=["hg_t"], w=["hg_nb"])
        s.dve(lambda e: e.tensor_scalar(out=hg_nnb[:], in0=hg_t[:], scalar1=0.25, scalar2=-0.25, op0=ALU.mult, op1=ALU.add), r=["hg_t"], w=["hg_nnb"])

        xnT = T("xnT", [128, 8, NTOK], BF16)
        aA = Bump(0, 84)
        QT = aA([128, 8, NTOK], BF16)
        KT = aA([128, 8, NTOK], BF16)
        VA = aA([128, 17, 8, 65], BF16)
        Wukv = T("Wukv", [128, 2, 1024], BF16)
        s.pool(lambda e: e.memset(VA[:, :, :, 64:65], 1.0), w=["VA1"])

        a1 = Bump(84, 150)
        W1 = a1([128, 8, 672], BF16)
        Wuq = a1([128, 3, 768], BF16)
        wstf = [a1([128, 2048]) for i in range(2)]
        w_in_v = w_in.rearrange("(k p) c -> p k c", p=128)

        def load_w_in(dst, c_lo, c_hi, wst, chunk=256, engs=("pool",)):
            ci = 0
            res = ("W", c_lo)
            for c0 in range(c_lo, c_hi, chunk):
                cw = min(chunk, c_hi - c0)
                b = ci % 2
                stv = wst[b][:, 0:8 * cw].rearrange("p (k c) -> p k c", c=cw)
                s.dma("wst%d" % b, lambda e, stv=stv, c0=c0, cw=cw: e.dma_start(out=stv, in_=w_in_v[:, :, c0:c0 + cw]), w=["wst%d" % b])
                eng = engs[ci % len(engs)]
                dv = dst[:, :, c0 - c_lo:c0 - c_lo + cw]
                s.add(eng, lambda e, stv=stv, dv=dv, cw=cw: e.tensor_tensor(out=dv, in0=stv, in1=normg_c[:].unsqueeze(2).to_broadcast([128, 8, cw]), op=ALU.mult),
                      ["wst%d" % b, "normg_c"], [res])
                ci += 1
            return res

        rW1 = load_w_in(W1, 0, 672, wstf)
        for k3 in range(3):
            stv = wstf[1][:, 0:768]
            s.dma("wst1", lambda e, k3=k3, stv=stv: e.dma_start(out=stv, in_=w_uq[k3 * 128:(k3 + 1) * 128, :]), w=["wst1"])
            s.pool(lambda e, k3=k3, stv=stv: e.tensor_scalar(out=Wuq[:, k3, :], in0=stv, scalar1=gcq_c[:, k3:k3 + 1], scalar2=None, op0=ALU.mult),
                   r=["wst1", "gcq_c"], w=["Wuq"])
        stkv = wstf[0][:, 0:2048].rearrange("p (k c) -> p k c", c=1024)
        s.dma("wst0", lambda e, stkv=stkv: e.dma_start(out=stkv[:, :, 0:512], in_=w_uk.rearrange("(k p) c -> p k c", p=128)), w=["wst0"])
        s.dma("wst0b", lambda e, stkv=stkv: e.dma_start(out=stkv[:, :, 512:1024], in_=w_uv.rearrange("(k p) c -> p k c", p=128)), r=["wst0"], w=["wst0"])
        s.dve(lambda e, stkv=stkv: e.tensor_copy(out=Wukv[:], in_=stkv), r=["wst0"], w=["Wukv"])

        tiles = [(0, NMETA, meta[:, :], SEQ)]
        tiles += [(1 + i, 128, xp[i * 128:(i + 1) * 128, :], i * 128) for i in range(16)]
        tiles += [(17, NS, xs[:, :], TP)]
        xst = [a1([128, D]) for i in range(2)]
        xnb = [a1([128, D], BF16) for i in range(2)]
        junk = a1([128, D], BF16)
        junk2 = junk
        ssx = T("ssx", [128, NT])
        rsx = T("rsx", [128, NT])
        for n, (ti, R, src, cb) in enumerate(tiles):
            xb = n % 2
            nb = n % 2
            pb = n % 2
            s.dma("xst%d" % xb, lambda e, xb=xb, R=R, src=src: e.dma_start(out=xst[xb][:R, :], in_=src), w=["xst%d" % xb])
            s.act(lambda e, xb=xb, R=R, ti=ti: e.activation(out=junk[:R, :], in_=xst[xb][:R, :], func=AF.Square,
                                                           accum_out=ssx[:R, ti:ti + 1]),
                  r=["xst%d" % xb], w=["junk2", ("ssx", ti)])
            s.dve(lambda e, R=R, ti=ti: e.tensor_scalar(out=rsx[:R, ti:ti + 1], in0=ssx[:R, ti:ti + 1], scalar1=1.0 / D, scalar2=EPS,
                                                       op0=ALU.mult, op1=ALU.add), r=[("ssx", ti)], w=[("rsx", ti)])
            s.act(lambda e, R=R, ti=ti: e.activation(out=rsx[:R, ti:ti + 1], in_=rsx[:R, ti:ti + 1], func=AF.Sqrt), r=[("rsx", ti)], w=[("rsx", ti)])
            s.dve(lambda e, R=R, ti=ti: e.reciprocal(out=rsx[:R, ti:ti + 1], in_=rsx[:R, ti:ti + 1]), r=[("rsx", ti)], w=[("rsx", ti)])
            s.dve(lambda e, xb=xb, nb=nb, R=R, ti=ti: e.tensor_scalar(out=xnb[nb][:R, :], in0=xst[xb][:R, :], scalar1=rsx[:R, ti:ti + 1],
                                                                     scalar2=None, op0=ALU.mult),
                  r=["xst%d" % xb, ("rsx", ti)], w=["xnb%d" % nb])
            psT = bkb[pb]
            for k in range(8):
                s.pe(lambda e, k=k, nb=nb, R=R, psT=psT: e.transpose(out=psT[:, k * 128:k * 128 + R], in_=xnb[nb][:R, k * 128:(k + 1) * 128],
                                                                   identity=ident[:R, :R]),
                     r=["xnb%d" % nb, "ident"], w=[bk(pb)])
            src_v = psT.rearrange("p (k r) -> p k r", r=128)[:, :, 0:R]
            if n % 2 == 0:
                s.act(lambda e, src_v=src_v, cb=cb, R=R: e.activation(out=xnT[:, :, cb:cb + R], in_=src_v, func=AF.Copy),
                      r=[bk(pb)], w=[("xnT", ti)])
            else:
                s.dve(lambda e, src_v=src_v, cb=cb, R=R: e.tensor_copy(out=xnT[:, :, cb:cb + R], in_=src_v),
                      r=[bk(pb)], w=[("xnT", ti)])

        ckn = [a1([128, 256]) for i in range(2)]
        cknb = [a1([128, 256], BF16) for i in range(2)]
        ckT = [a1([128, 2, 128], BF16) for i in range(2)]
        krr = [T("krr%d" % i, [128, 32]) for i in range(2)]
        rtmp = [T("rtmp%d" % i, [128, 2, 32]) for i in range(2)]
        st1 = T("st1", [128, NT, 4])
        ssk = T("ssk", [128, NT, 8])
        rsk = T("rsk", [128, NT, 8])
        kc = [a1([128, 8, 96], BF16)] * 2
        def s1b(n, ti, R, src, cb):
            b2 = n % 2
            pA, pB, pC, pD = (2, 7)[n % 2], 3, 4, 5
            res_tile = ("xnT", ti)
            for k in range(8):
                s.pe(lambda e, k=k, cb=cb, R=R: e.matmul(banks[pA][:R, 0:288], lhsT=xnT[:, k, cb:cb + R], rhs=W1[:, k, 384:672],
                                                        start=(k == 0), stop=(k == 7)),
                     r=[res_tile, rW1], w=[bk(pA)])
            s.act(lambda e, R=R, ti=ti: e.activation(out=junk2[:R, 0:256], in_=banks[pA][:R, 0:256], func=AF.Square,
                                                    accum_out=st1[:R, ti, 0:1]), r=[bk(pA)], w=["junk2", ("st1", ti)])
            s.dve(lambda e, R=R, ti=ti: e.tensor_scalar(out=st1[:R, ti, 1:2], in0=st1[:R, ti, 0:1], scalar1=1.0 / 256, scalar2=EPS,
                                                       op0=ALU.mult, op1=ALU.add), r=[("st1", ti)], w=[("st1", ti)])
            s.act(lambda e, R=R, ti=ti: e.activation(out=st1[:R, ti, 1:2], in_=st1[:R, ti, 1:2], func=AF.Sqrt), r=[("st1", ti)], w=[("st1", ti)])
            s.dve(lambda e, R=R, ti=ti: e.reciprocal(out=st1[:R, ti, 1:2], in_=st1[:R, ti, 1:2]), r=[("st1", ti)], w=[("st1", ti)])
            s.dve(lambda e, R=R, ti=ti, b2=b2: e.scalar_tensor_tensor(out=ckn[b2][:R, :], in0=banks[pA][:R, 0:256], scalar=st1[:R, ti, 1:2],
                                                                     in1=gckv_bc[:R, :], op0=ALU.mult, op1=ALU.mult),
                  r=[bk(pA), ("st1", ti), "gckv_bc"], w=["ckn%d" % b2])
            s.dve(lambda e, R=R, ti=ti, b2=b2: e.tensor_tensor(out=rtmp[b2][:R, 0, :], in0=banks[pA][:R, 256:288], in1=CC[:R, ti, :], op=ALU.mult),
                  r=[bk(pA), "CC"], w=["rtmp%d" % b2])
            s.dve(lambda e, R=R, ti=ti, b2=b2: e.tensor_tensor(out=rtmp[b2][:R, 1, 0:16], in0=banks[pA][:R, 272:288], in1=SS[:R, ti, 0:16], op=ALU.mult),
                  r=[bk(pA), "SS"], w=["rtmp%d" % b2])
            s.dve(lambda e, R=R, ti=ti, b2=b2: e.tensor_tensor(out=rtmp[b2][:R, 1, 16:32], in0=banks[pA][:R, 256:272], in1=SS[:R, ti, 16:32], op=ALU.mult),
                  r=[bk(pA), "SS", "rtmp%d" % b2], w=["rtmp%d" % b2])
            s.dve(lambda e, R=R, b2=b2: e.tensor_tensor(out=krr[b2][:R, :], in0=rtmp[b2][:R, 0, :], in1=rtmp[b2][:R, 1, :], op=ALU.add),
                  r=["rtmp%d" % b2], w=["krr%d" % b2])
            if ti == 0:
                dl, dk = lat_p[0:NMETA, :], kr_p[0:NMETA, :]
            elif ti == 17:
                dl, dk = lat_s[:, :], kr_s[:, :]
            else:
                r0 = NMETA + (ti - 1) * 128
                dl, dk = lat_p[r0:r0 + 128, :], kr_p[r0:r0 + 128, :]
            wl = ["lat_s_dram"] if ti == 17 else []
            s.dma("olat%d" % b2, lambda e, dl=dl, b2=b2, R=R: e.dma_start(out=dl, in_=ckn[b2][:R, :]), r=["ckn%d" % b2], w=wl)
            wl = ["kr_s_dram"] if ti == 17 else []
            s.dma("okr%d" % b2, lambda e, dk=dk, b2=b2, R=R: e.dma_start(out=dk, in_=krr[b2][:R, :]), r=["krr%d" % b2], w=wl)

        def s2b(n, ti, R, src, cb):
            b2 = n % 2
            pA, pB, pC, pD = (2, 7)[n % 2], 3, 4, 5
            res_tile = ("xnT", ti)
            s.act(lambda e, R=R, b2=b2: e.activation(out=cknb[b2][:R, :], in_=ckn[b2][:R, :], func=AF.Copy),
                  r=["ckn%d" % b2], w=["cknb%d" % b2])
            psT = bkb[pD]
            for c in range(2):
                s.pe(lambda e, c=c, R=R, b2=b2, psT=psT: e.transpose(out=psT[:, c * 128:c * 128 + R], in_=cknb[b2][:R, c * 128:(c + 1) * 128],
                                                                   identity=ident[:R, :R]),
                     r=["cknb%d" % b2, "ident"], w=[bk(pD)])
            s.dve(lambda e, R=R, b2=b2, psT=psT: e.tensor_copy(out=ckT[b2][:, :, 0:R], in_=psT[:, 0:256].rearrange("p (c r) -> p c r", r=128)[:, :, 0:R]),
                  r=[bk(pD)], w=["ckT%d" % b2])
            for half, pbk in ((0, pB), (1, pC)):
                for c in range(2):
                    s.pe(lambda e, c=c, R=R, b2=b2, half=half, pbk=pbk: e.matmul(banks[pbk][:R, :], lhsT=ckT[b2][:, c, 0:R],
                                                                                rhs=Wukv[:, c, half * 512:(half + 1) * 512],
                                                                                start=(c == 0), stop=(c == 1)),
                         r=["ckT%d" % b2, "Wukv"], w=[bk(pbk)])
            if ti != 17:
                blk = 16 if ti == 0 else ti - 1
                s.act(lambda e, R=R, blk=blk: e.activation(out=VA[:R, blk, :, 0:64], in_=banks[pC][:R, :].rearrange("p (h d) -> p h d", d=64),
                                                          func=AF.Copy), r=[bk(pC)], w=[("VA", blk)])
            s.act(lambda e, R=R: e.activation(out=junk2[:R, 0:512], in_=banks[pB][:R, :], func=AF.Square), r=[bk(pB)], w=["junk2"])
            s.dve(lambda e, R=R, ti=ti: e.tensor_reduce(out=ssk[:R, ti, :], in_=junk2[:R, 0:512].rearrange("p (h d) -> p h d", d=64),
                                                       axis=AX.X, op=ALU.add), r=["junk2"], w=[("ssk", ti)])
            s.act(lambda e, R=R, ti=ti, b2=b2: e.activation(out=junk2[:R, 512:544], in_=krr[b2][:R, :], func=AF.Square,
                                                           accum_out=st1[:R, ti, 2:3]), r=["krr%d" % b2], w=["junk2", ("st1b", ti)])
            s.dve(lambda e, R=R, ti=ti: e.tensor_scalar(out=rsk[:R, ti, :], in0=ssk[:R, ti, :], scalar1=st1[:R, ti, 2:3], scalar2=1.0 / 96,
                                                       op0=ALU.add, op1=ALU.mult), r=[("ssk", ti), ("st1b", ti)], w=[("rsk", ti)])
            s.pool(lambda e, R=R, ti=ti: e.tensor_scalar(out=rsk[:R, ti, :], in0=rsk[:R, ti, :], scalar1=EPS, scalar2=None, op0=ALU.add),
                   r=[("rsk", ti)], w=[("rsk", ti)])
            s.act(lambda e, R=R, ti=ti: e.activation(out=rsk[:R, ti, :], in_=rsk[:R, ti, :], func=AF.Sqrt), r=[("rsk", ti)], w=[("rsk", ti)])
            s.dve(lambda e, R=R, ti=ti: e.reciprocal(out=rsk[:R, ti, :], in_=rsk[:R, ti, :]), r=[("rsk", ti)], w=[("rsk", ti)])
            s.dve(lambda e, R=R, ti=ti, b2=b2: e.tensor_tensor(out=kc[b2][:R, :, 0:64], in0=banks[pB][:R, :].rearrange("p (h d) -> p h d", d=64),
                                                              in1=rsk[:R, ti, :].unsqueeze(2).to_broadcast([R, 8, 64]), op=ALU.mult),
                  r=[bk(pB), ("rsk", ti)], w=["kc"])
            s.dve(lambda e, R=R, ti=ti, b2=b2: e.tensor_tensor(out=kc[b2][:R, :, 64:96], in0=krr[b2][:R, :].unsqueeze(1).to_broadcast([R, 8, 32]),
                                                              in1=rsk[:R, ti, :].unsqueeze(2).to_broadcast([R, 8, 32]), op=ALU.mult),
                  r=["krr%d" % b2, ("rsk", ti), "kc"], w=["kc"])
            psK = bkb[6]
            for h in range(8):
                s.pe(lambda e, h=h, R=R, b2=b2, psK=psK: e.transpose(out=psK[0:96, h * 128:h * 128 + R], in_=kc[b2][:R, h, :], identity=ident[:R, :R]),
                     r=["kc", "ident"], w=[bk(6)])
            s.act(lambda e, R=R, cb=cb, psK=psK: e.activation(out=KT[0:96, :, cb:cb + R], in_=psK[0:96, :].rearrange("p (h r) -> p h r", r=128)[:, :, 0:R],
                                                             func=AF.Copy), r=[bk(6)], w=[("KT", ti)])

        for n in range(len(tiles) + 1):
            if n < len(tiles):
                s1b(n, *tiles[n])
            if n >= 1:
                s2b(n - 1, *tiles[n - 1])

        cqb = a1([128, 3, 512], BF16)
        cqsq = a1([128, 3, 512], BF16)
        qs = [a1([128, 8, 96])] * 2
        rq = a1([128, 2, 8, 32])
        qst = T("qst", [128, NT, 2])
        ssq = T("ssq", [128, NT, 8])
        rsq = T("rsq", [128, NT, 8])
        qnb = [a1([128, 8, 96], BF16)] * 2
        qgroups = [(512 * g, 512, [(1 + 4 * g + i, 128, 128 * i) for i in range(4)]) for g in range(4)]
        qgroups.append((TP, NS, [(17, NS, 0)]))
        for gi, (c0, n, gt) in enumerate(qgroups):
            rx = [("xnT", ti) for ti, _, _ in gt]
            for c in range(3):
                for k in range(8):
                    s.pe(lambda e, c=c, k=k, c0=c0, n=n: e.matmul(banks[c][:, 0:n], lhsT=W1[:, k, c * 128:(c + 1) * 128], rhs=xnT[:, k, c0:c0 + n],
                                                                 start=(k == 0), stop=(k == 7)), r=rx + [rW1], w=[bk(c)])
                s.act(lambda e, c=c, n=n: e.activation(out=cqb[:, c, 0:n], in_=banks[c][:, 0:n], func=AF.Copy), r=[bk(c)], w=["cqb"])
                s.act(lambda e, c=c, n=n: e.activation(out=cqsq[:, c, 0:n], in_=banks[c][:, 0:n], func=AF.Square), r=[bk(c)], w=["cqsq"])
            for tn, (ti, R, lo) in enumerate(gt):
                b2 = tn % 2
                cb = c0 + lo
                for c in range(3):
                    s.pe(lambda e, c=c, R=R, lo=lo: e.matmul(banks[7][:R, 0:1], lhsT=cqsq[:, c, lo:lo + R], rhs=ones_bf[:, 0:1],
                                                            start=(c == 0), stop=(c == 2)), r=["cqsq", "ones_bf"], w=[bk(7)])
                s.dve(lambda e, R=R, ti=ti: e.tensor_scalar(out=qst[:R, ti, 0:1], in0=banks[7][:R, 0:1], scalar1=1.0 / 384, scalar2=EPS,
                                                           op0=ALU.mult, op1=ALU.add), r=[bk(7)], w=[("qst", ti)])
                s.act(lambda e, R=R, ti=ti: e.activation(out=qst[:R, ti, 0:1], in_=qst[:R, ti, 0:1], func=AF.Sqrt), r=[("qst", ti)], w=[("qst", ti)])
                s.dve(lambda e, R=R, ti=ti: e.reciprocal(out=qst[:R, ti, 0:1], in_=qst[:R, ti, 0:1]), r=[("qst", ti)], w=[("qst", ti)])
                for half, (pbk, w0, wn) in enumerate(((3, 0, 512), (4, 512, 256))):
                    for c in range(3):
                        s.pe(lambda e, c=c, R=R, lo=lo, pbk=pbk, w0=w0, wn=wn: e.matmul(banks[pbk][:R, 0:wn], lhsT=cqb[:, c, lo:lo + R],
                                                                                      rhs=Wuq[:, c, w0:w0 + wn], start=(c == 0), stop=(c == 2)),
                             r=["cqb", "Wuq"], w=[bk(pbk)])
                    qsf = qs[b2][:].rearrange("p h d -> p (h d)")
                    s.act(lambda e, R=R, ti=ti, pbk=pbk, w0=w0, wn=wn, qsf=qsf: e.activation(out=qsf[:R, w0:w0 + wn], in_=banks[pbk][:R, 0:wn],
                                                                                            func=AF.Copy, scale=qst[:R, ti, 0:1]),
                          r=[bk(pbk), ("qst", ti)], w=["qs"])
                qr = qs[b2][:, :, 64:96]
                s.dve(lambda e, R=R, ti=ti, qr=qr: e.tensor_tensor(out=rq[:R, 0], in0=qr[:R], in1=CC[:R, ti, :].unsqueeze(1).to_broadcast([R, 8, 32]), op=ALU.mult),
                      r=["qs", "CC"], w=["rq"])
                s.dve(lambda e, R=R, ti=ti, qr=qr: e.tensor_tensor(out=rq[:R, 1, :, 0:16], in0=qr[:R, :, 16:32],
                                                                  in1=SS[:R, ti, 0:16].unsqueeze(1).to_broadcast([R, 8, 16]), op=ALU.mult),
                      r=["qs", "SS", "rq"], w=["rq"])
                s.dve(lambda e, R=R, ti=ti, qr=qr: e.tensor_tensor(out=rq[:R, 1, :, 16:32], in0=qr[:R, :, 0:16],
                                                                  in1=SS[:R, ti, 16:32].unsqueeze(1).to_broadcast([R, 8, 16]), op=ALU.mult),
                      r=["qs", "SS", "rq"], w=["rq"])
                s.dve(lambda e, R=R, qr=qr: e.tensor_tensor(out=qr[:R], in0=rq[:R, 0], in1=rq[:R, 1], op=ALU.add), r=["rq"], w=["qs"])
                s.act(lambda e, R=R, b2=b2: e.activation(out=junk2[:R, 0:768], in_=qs[b2][:R].rearrange("p h d -> p (h d)"), func=AF.Square),
                      r=["qs"], w=["junk2"])
                s.dve(lambda e, R=R, ti=ti: e.tensor_reduce(out=ssq[:R, ti, :], in_=junk2[:R, 0:768].rearrange("p (h d) -> p h d", d=96),
                                                           axis=AX.X, op=ALU.add), r=["junk2"], w=[("ssq", ti)])
                s.dve(lambda e, R=R, ti=ti: e.tensor_scalar(out=rsq[:R, ti, :], in0=ssq[:R, ti, :], scalar1=1.0 / 96, scalar2=EPS,
                                                           op0=ALU.mult, op1=ALU.add), r=[("ssq", ti)], w=[("rsq", ti)])
                s.act(lambda e, R=R, ti=ti: e.activation(out=rsq[:R, ti, :], in_=rsq[:R, ti, :], func=AF.Sqrt), r=[("rsq", ti)], w=[("rsq", ti)])
                s.dve(lambda e, R=R, ti=ti: e.reciprocal(out=rsq[:R, ti, :], in_=rsq[:R, ti, :]), r=[("rsq", ti)], w=[("rsq", ti)])
                s.dve(lambda e, R=R, ti=ti, b2=b2: e.tensor_tensor(out=qnb[b2][:R], in0=qs[b2][:R],
                                                                  in1=rsq[:R, ti, :].unsqueeze(2).to_broadcast([R, 8, 96]), op=ALU.mult),
                      r=["qs", ("rsq", ti)], w=["qnb"])
                psQ = bkb[5]
                for h in range(8):
                    s.pe(lambda e, h=h, R=R, b2=b2, psQ=psQ: e.transpose(out=psQ[0:96, h * 128:h * 128 + R], in_=qnb[b2][:R, h, :], identity=ident[:R, :R]),
                         r=["qnb", "ident"], w=[bk(5)])
                s.act(lambda e, R=R, cb=cb, psQ=psQ: e.activation(out=QT[0:96, :, cb:cb + R], in_=psQ[0:96, :].rearrange("p (h r) -> p h r", r=128)[:, :, 0:R],
                                                                 func=AF.Copy, scale=gq2[:, 0:1]), r=[bk(5), "gq2"], w=[("QT", ti)])

        def warm(bank_i, n=10):
            for _ in range(n):
                s.pe(lambda e: e.matmul(banks[bank_i][:, :], lhsT=ident[:, :], rhs=Wukv[:, 0, 0:512], start=True, stop=True), r=["ident", "Wukv"], w=[bk(bank_i)])

        barrier()
        aB = Bump(84, 117)
        oagT = aB([128, 4, NTOK], BF16)
        obgT = aB([128, 4, NTOK], BF16)
        a2 = Bump(117, 150)
        PT = [a2([128, 512], BF16) for i in range(4)]
        Osb = [a2([65, 512]) for i in range(2)]
        if stop_after >= 2:
            pti = 0
            hg = 0
            pend = []

            def finalize(ob, bb, osb, rs, h, g):
                s.act(lambda e: e.activation(out=osb[:, :], in_=banks[ob][0:65, :], func=AF.Copy), r=[bk(ob)], w=[rs])
                s.dve(lambda e: e.reciprocal(out=osb[64:65, :], in_=osb[64:65, :]), r=[rs], w=[rs])
                s.pe(lambda e: e.matmul(banks[bb][0:64, :], lhsT=selrow[:, :], rhs=osb[:, :], start=True, stop=True), r=[rs, "selrow"], w=[bk(bb)])
                po = 64 * (h % 2)
                s.dve(lambda e: e.tensor_tensor(out=oagT[po:po + 64, h // 2, 512 * g:512 * g + 512], in0=osb[0:64, :], in1=banks[bb][0:64, :], op=ALU.mult),
                      r=[rs, bk(bb)], w=[("oagT", g)])

            for g in range(4):
                for h in range(8):
                    ob = 4 + hg % 2
                    bb = 6 + hg % 2
                    osb = Osb[hg % 2]
                    rs = "Osb%d" % (hg % 2)
                    hg += 1
                    blocks = [(16, NMETA, SEQ, 0, False)] + [(j, 128, j * 128, max(0, j - 4 * g) * 128, j >= 4 * g) for j in range(4 * g + 4)]
                    for bi_, (blk, nk, kc0, qlo, diag) in enumerate(blocks):
                        sb = pti % 4
                        pb_ = pti % 4
                        pti += 1
                        qn = 512 - qlo
                        tk = 0 if blk == 16 else blk + 1
                        s.pe(lambda e, sb=sb, nk=nk, kc0=kc0, qlo=qlo, qn=qn, h=h, g=g: e.matmul(
                            banks[sb][:nk, 0:qn], lhsT=KT[0:96, h, kc0:kc0 + nk], rhs=QT[0:96, h, 512 * g + qlo:512 * g + 512], start=True, stop=True),
                            r=[("KT", tk)] + [("QT", 1 + 4 * g + i) for i in range(4)], w=[bk(sb)])
                        s.act(lambda e, sb=sb, nk=nk, qlo=qlo, qn=qn, pb_=pb_: e.activation(out=PT[pb_][:nk, qlo:512], in_=banks[sb][:nk, 0:qn], func=AF.Exp,
                                                                                          scale=SCALE, bias=sbias[:nk, 0:1]),
                              r=[bk(sb), "sbias"], w=["PT%d" % pb_])
                        if diag:
                            s.pool(lambda e, nk=nk, qlo=qlo, pb_=pb_: e.tensor_tensor(out=PT[pb_][:nk, qlo:qlo + 128], in0=PT[pb_][:nk, qlo:qlo + 128],
                                                                                    in1=tri[:nk, :], op=ALU.mult), r=["PT%d" % pb_, "tri"], w=["PT%d" % pb_])
                        if len(pend) >= 3:
                            pend.pop(0)()

                        def pv(ob=ob, bb=bb, osb=osb, rs=rs, nk=nk, blk=blk, qlo=qlo, pb_=pb_, h=h, g=g, first=(bi_ == 0), last=(bi_ == len(blocks) - 1)):
                            s.pe(lambda e: e.matmul(banks[ob][0:65, qlo:512], lhsT=VA[:nk, blk, h, :], rhs=PT[pb_][:nk, qlo:512], start=first, stop=last),
                                 r=["PT%d" % pb_, ("VA", blk), "VA1"], w=[bk(ob)])
                            if last:
                                finalize(ob, bb, osb, rs, h, g)
                        pend.append(pv)
            while pend:
                pend.pop(0)()

        if dbg and stop_after == 2:
            dbgo["d_QT"] = dout("d_QT", [96, 8 * NTOK], BF16)
            s.dma("dbg", lambda e: e.dma_start(out=dbgo["d_QT"][:, :], in_=QT[0:96].rearrange("p k t -> p (k t)")), r=[("QT", t) for t in range(NT)])
            dbgo["d_oa"] = dout("d_oa", [128, 4 * NTOK], BF16)
            s.dma("dbg", lambda e: e.dma_start(out=dbgo["d_oa"][:, :], in_=oagT[:].rearrange("p k t -> p (k t)")), r=[("oagT", t) for t in range(4)])

        barrier()
        if stop_after >= 3:
            a3 = Bump(33, 84)
            a3c = Bump(117, 150)
            WukT = a3([64, 8, 256], BF16)
            qabs = a3([128, 2, NS, 8], BF16)
            qrope = a3([32, NS, 8], BF16)
            NGRP = NPAGES // 4
            ptf = a3c([128, NS * NPAGES])
            pti32 = a3c([128, NS * NPAGES], I32)
            psel = a3c([128, NS * NGRP])
            gidx = a3c([128, NS * NGRP], I32)
            pmask = a3c([128, 4])
            qcol = a3c([128, 1])
            Cf = [a3([128, 4, 288]) for i in range(3)]
            Cb = [a3([128, 4, 296], BF16) for i in range(3)]
            CTs = [a3([128, 4, 384], BF16) for i in range(2)]
            sqj = a3([128, 2048], BF16)
            sqr = a3([128, 128], BF16)
            sst = [a3([128, 4, 32]) for i in range(2)]
            pbf = [a3([128, 4, 8], BF16) for i in range(2)]
            accs = a3([8, 260])
            olb = a3([8, 256], BF16)
            olT = a3([128, 2, 8], BF16)
            for h in range(8):
                for c in range(2):
                    s.pe(lambda e, h=h, c=c: e.transpose(out=bkb[0][0:64, (h * 2 + c) * 128:(h * 2 + c) * 128 + 128] if h < 4 else
                                                         bkb[1][0:64, ((h - 4) * 2 + c) * 128:((h - 4) * 2 + c) * 128 + 128],
                                                         in_=Wukv[:, c, h * 64:(h + 1) * 64], identity=ident[:, :]), r=["Wukv", "ident"], w=[bk(0 if h < 4 else 1)])
            s.dve(lambda e: e.tensor_copy(out=WukT[:, 0:4, :].rearrange("p h c -> p (h c)"), in_=bkb[0][0:64, :]), r=[bk(0)], w=["WukT"])
            s.dve(lambda e: e.tensor_copy(out=WukT[:, 4:8, :].rearrange("p h c -> p (h c)"), in_=bkb[1][0:64, :]), r=[bk(1)], w=["WukT"])
            for h in range(8):
                for c in range(2):
                    s.pe(lambda e, h=h, c=c: e.matmul(banks[2][:, (h * 2 + c) * 4:(h * 2 + c) * 4 + 4], lhsT=WukT[:, h, c * 128:(c + 1) * 128],
                                                     rhs=QT[0:64, h, TP:TP + NS], start=True, stop=True), r=["WukT", ("QT", 17)], w=[bk(2)])
            s.dve(lambda e: e.tensor_copy(out=qabs[:].rearrange("p c b h -> p h c b"), in_=banks[2][:, 0:64].rearrange("p (h c b) -> p h c b", c=2, b=NS)),
                  r=[bk(2)], w=["qabs"])
            s.dve(lambda e: e.tensor_copy(out=qrope[:].rearrange("p b h -> p h b"), in_=QT[64:96, :, TP:TP + NS]), r=[("QT", 17)], w=["qrope"])
            s.dma("c7", lambda e: e.dma_start(out=pti32[:], in_=pt.rearrange("(o b) n -> o (b n)", o=1).partition_broadcast(128)), w=["pti32"])
            s.dve(lambda e: e.tensor_copy(out=ptf[:], in_=pti32[:]), r=["pti32"], w=["ptf"])
            s.pool(lambda e: e.memset(pmask[:], 0.0), w=["pmask"])
            for sl in range(4):
                s.pool(lambda e, sl=sl: e.memset(pmask[32 * sl:32 * sl + 32, sl:sl + 1], 1.0), r=["pmask"], w=["pmask"])
                s.pool(lambda e, sl=sl: e.iota(qcol[32 * sl:32 * sl + 32, :], pattern=[[0, 1]], base=0, channel_multiplier=1,
                                               allow_small_or_imprecise_dtypes=True), r=["qcol"], w=["qcol"])
            s.dve(lambda e: e.tensor_tensor(out=ptf[:].rearrange("p (g s) -> p g s", s=4), in0=ptf[:].rearrange("p (g s) -> p g s", s=4),
                                            in1=pmask[:].unsqueeze(1).to_broadcast([128, NS * NGRP, 4]), op=ALU.mult), r=["ptf", "pmask"], w=["ptf"])
            s.dve(lambda e: e.tensor_reduce(out=psel[:], in_=ptf[:].rearrange("p (g s) -> p g s", s=4), axis=AX.X, op=ALU.add), r=["ptf"], w=["psel"])
            s.dve(lambda e: e.tensor_scalar(out=psel[:], in0=psel[:], scalar1=32.0, scalar2=qcol[:, 0:1], op0=ALU.mult, op1=ALU.add), r=["psel", "qcol"], w=["psel"])
            s.dve(lambda e: e.tensor_copy(out=gidx[:], in_=psel[:]), r=["psel"], w=["gidx"])
            for i in range(3):
                s.pool(lambda e, i=i: e.memset(Cb[i][:, :, 256:264], 1.0), w=[("Cb", i, 0), ("Cb", i, 1)])
            sqj2 = [sqj[:, 0:1024], sqj[:, 1024:2048]]
            sst3 = [a3([128, 2, 32]) for i in range(4)]
            numsb = [a3([128, 16]) for i in range(4)]
            pbf2 = [a3([128, 2, 8], BF16) for i in range(2)]
            CT2 = [a3([128, 2, 384], BF16) for i in range(2)]

            def st_A1(u):
                b, n, hf, R, NTL, gi_, cbj = u[:7]
                if hf == 0:
                    i = gi_ % 3
                    if n < NGRP:
                        col = b * NGRP + n
                        s.dma("gc%d" % i, lambda e: e.indirect_dma_start(out=Cf[i][:].rearrange("p t c -> p (t c)"), out_offset=None, in_=ccomb[:, :],
                              in_offset=bass.IndirectOffsetOnAxis(ap=gidx[:, col:col + 1], axis=0)), r=["gidx"], w=["Cf%d" % i], q="pool")
                    else:
                        s.dma("gs%d" % i, lambda e: e.dma_start(out=Cf[i][0:1, 0, 0:256], in_=lat_s[b:b + 1, :]), r=["lat_s_dram"], w=["Cf%d" % i])
                        s.dma("gr%d" % i, lambda e: e.dma_start(out=Cf[i][0:1, 0, 256:288], in_=kr_s[b:b + 1, :]), r=["kr_s_dram", "Cf%d" % i], w=["Cf%d" % i])
                i = gi_ % 3
                t0 = 2 * hf
                rcb = ("Cb", cbj, hf)
                s.pool(lambda e: e.tensor_copy(out=Cb[cbj][:R, t0:t0 + NTL, 0:256], in_=Cf[i][:R, t0:t0 + NTL, 0:256]), r=["Cf%d" % i], w=[rcb])
                s.pool(lambda e: e.tensor_copy(out=Cb[cbj][:R, t0:t0 + NTL, 264:296], in_=Cf[i][:R, t0:t0 + NTL, 256:288]), r=["Cf%d" % i, rcb], w=[rcb])
                uj = u[7] % 2
                u3 = u[7] % 4
                s.act(lambda e: e.activation(out=sqr[:R, 0:NTL * 32].rearrange("p (t c) -> p t c", c=32), in_=Cf[i][:R, t0:t0 + NTL, 256:288], func=AF.Square),
                      r=["Cf%d" % i], w=["sqr"])
                s.dve(lambda e: e.tensor_reduce(out=sst3[u3][:R, 0:NTL, 8:9], in_=sqr[:R, 0:NTL * 32].rearrange("p (t c) -> p t c", c=32), axis=AX.X, op=ALU.add),
                      r=["sqr"], w=["sst%d" % u3])
                for t in range(NTL):
                    to = t * 384
                    for c in range(2):
                        s.pe(lambda e, c=c, t=t, to=to: e.transpose(out=bkb[uj][:, to + c * 128:to + c * 128 + R], in_=Cb[cbj][:R, t0 + t, c * 128:(c + 1) * 128],
                                                                    identity=ident[:R, :R]), r=[rcb, "ident"], w=[bk(uj)])
                    s.pe(lambda e, t=t, to=to: e.transpose(out=bkb[uj][0:32, to + 256:to + 256 + R], in_=Cb[cbj][:R, t0 + t, 264:296], identity=ident[:R, :R]),
                         r=[rcb, "ident"], w=[bk(uj)])
                pv_ = bkb[uj][:, 0:384 * NTL].rearrange("p (t c) -> p t c", c=384)
                s.dve(lambda e: e.tensor_copy(out=CT2[uj][:, 0:NTL, 0:256].rearrange("p t (c r) -> p t c r", r=128)[:, :, :, 0:R],
                                              in_=pv_[:, :, 0:256].rearrange("p t (c r) -> p t c r", r=128)[:, :, :, 0:R]),
                      r=[bk(uj)], w=["CT2%d" % uj])
                s.dve(lambda e: e.tensor_copy(out=CT2[uj][0:32, 0:NTL, 256:256 + R], in_=pv_[0:32, :, 256:256 + R]),
                      r=[bk(uj), "CT2%d" % uj], w=["CT2%d" % uj])

            def st_A2(u):
                b, n, hf, R, NTL, gi_, cbj = u[:7]
                uj = u[7] % 2
                u3 = u[7] % 4
                kb0 = 4 + 2 * uj
                for t in range(NTL):
                    for c in range(2):
                        s.pe(lambda e, c=c, t=t: e.matmul(banks[kb0 + t][:R, :], lhsT=CT2[uj][:, t, c * 128:c * 128 + R], rhs=Wukv[:, c, 0:512],
                                                         start=(c == 0), stop=(c == 1)), r=["CT2%d" % uj, "Wukv"], w=[bk(kb0 + t)])
                for t in range(NTL):
                    nc0 = u3 * 16 + t * 8
                    for c in range(2):
                        s.pe(lambda e, c=c, t=t, nc0=nc0: e.matmul(banks[3][:R, nc0:nc0 + 8], lhsT=CT2[uj][:, t, c * 128:c * 128 + R], rhs=qabs[:, c, b, :],
                                                                  start=(c == 0), stop=False), r=["CT2%d" % uj, "qabs"], w=[bk(3)])
                    s.pe(lambda e, t=t, nc0=nc0: e.matmul(banks[3][:R, nc0:nc0 + 8], lhsT=CT2[uj][0:32, t, 256:256 + R], rhs=qrope[:, b, :],
                                                         start=False, stop=True), r=["CT2%d" % uj, "qrope"], w=[bk(3)])
                s.dve(lambda e: e.tensor_copy(out=numsb[u3][:R, 0:8 * NTL], in_=banks[3][:R, u3 * 16:u3 * 16 + 8 * NTL]), r=[bk(3)], w=[("numsb", u3)])
                s.act(lambda e: e.activation(out=sqj2[uj][:R, 0:512 * NTL], in_=psum_all[:R, 512 * kb0:512 * (kb0 + NTL)], func=AF.Square),
                      r=[bk(kb0 + t) for t in range(NTL)], w=["sqj%d" % uj])
                s.dve(lambda e: e.tensor_reduce(out=sst3[u3][:R, 0:NTL, 0:8], in_=sqj2[uj][:R, 0:512 * NTL].rearrange("p (t h d) -> p t h d", h=8, d=64),
                                                axis=AX.X, op=ALU.add), r=["sqj%d" % uj], w=["sst%d" % u3])
                sv_ = sst3[u3]
                s.dve(lambda e: e.tensor_tensor(out=sv_[:R, 0:NTL, 16:24], in0=sv_[:R, 0:NTL, 0:8], in1=sv_[:R, 0:NTL, 8:9].to_broadcast([R, NTL, 8]), op=ALU.add),
                      r=["sst%d" % u3], w=["sst%d" % u3])

            def st_B1(u):
                b, n, hf, R, NTL, gi_, cbj = u[:7]
                uj = u[7] % 2
                u3 = u[7] % 4
                rs_ = "sst%d" % u3
                sv = sst3[u3]
                s.act(lambda e: e.activation(out=sv[:R, 0:NTL, 16:24], in_=sv[:R, 0:NTL, 16:24], func=AF.Ln, bias=epsc[:R, 0:1], scale=1.0 / 96), r=[rs_, "epsc"], w=[rs_])
                s.act(lambda e: e.activation(out=sv[:R, 0:NTL, 16:24], in_=sv[:R, 0:NTL, 16:24], func=AF.Exp, scale=-0.5), r=[rs_], w=[rs_])
                s.dve(lambda e: e.tensor_tensor(out=sv[:R, 0:NTL, 24:32], in0=numsb[u3][:R, 0:8 * NTL].rearrange("p (t h) -> p t h", h=8),
                                                in1=sv[:R, 0:NTL, 16:24], op=ALU.mult), r=[("numsb", u3), rs_], w=[rs_])
                s.act(lambda e: e.activation(out=pbf2[uj][:R, 0:NTL, :], in_=sv[:R, 0:NTL, 24:32], func=AF.Exp, scale=SCALE, bias=sbias[:R, 0:1]),
                      r=[rs_, "sbias"], w=["pbf%d" % uj])

            def st_B2(u):
                b, n, hf, R, NTL, gi_, cbj = u[:7]
                uj = u[7] % 2
                t0 = 2 * hf
                for t in range(NTL):
                    s.pe(lambda e, t=t, first=(n == 0 and hf == 0 and t == 0), last=(n == NGRP): e.matmul(
                        banks[2][0:8, 0:257], lhsT=pbf2[uj][:R, t, :], rhs=Cb[cbj][:R, t0 + t, 0:257], start=first, stop=last, skip_group_check=True),
                        r=["pbf%d" % uj, ("Cb", cbj, hf)], w=[bk(2)])

            units = []
            gcount = 0
            for b in range(NS):
                for n in range(NGRP + 1):
                    for hf in range(2 if n < NGRP else 1):
                        R, NTL = (128, 2) if n < NGRP else (1, 1)
                        units.append((b, n, hf, R, NTL, gcount, gcount % 3, len(units)))
                    gcount += 1
            for b in range(NS):
                ub = [u for u in units if u[0] == b]
                for k in range(len(ub) + 3):
                    if 3 <= k:
                        st_B1(ub[k - 3])
                    if k < len(ub):
                        st_A1(ub[k])
                    if 1 <= k <= len(ub):
                        st_A2(ub[k - 1])
                    if 3 <= k:
                        st_B2(ub[k - 3])
                s.act(lambda e: e.activation(out=accs[:, 0:257], in_=banks[2][0:8, 0:257], func=AF.Copy), r=[bk(2)], w=["accs"])
                s.dve(lambda e: e.reciprocal(out=accs[:, 258:259], in_=accs[:, 256:257]), r=["accs"], w=["accs"])
                s.dve(lambda e: e.tensor_scalar(out=olb[:, :], in0=accs[:, 0:256], scalar1=accs[:, 258:259], scalar2=None, op0=ALU.mult), r=["accs"], w=["olb"])
                for c in range(2):
                    s.pe(lambda e, c=c: e.transpose(out=bkb[0][:, c * 8:c * 8 + 8], in_=olb[:, c * 128:(c + 1) * 128], identity=ident[0:8, 0:8]), r=["olb", "ident"], w=[bk(0)])
                s.dve(lambda e: e.tensor_copy(out=olT[:].rearrange("p c h -> p (c h)"), in_=bkb[0][:, 0:16]), r=[bk(0)], w=["olT"])
                for jj in range(4):
                    for c in range(2):
                        s.pe(lambda e, jj=jj, c=c: e.matmul(banks[1][:, jj * 8:jj * 8 + 8], lhsT=Wukv[:, c, 512 + jj * 128:512 + (jj + 1) * 128], rhs=olT[:, c, :],
                                                           start=(c == 0), stop=(c == 1)), r=["olT", "Wukv"], w=[bk(1)])
                for jj in range(4):
                    s.dve(lambda e, jj=jj, b=b: e.tensor_copy(out=oagT[0:64, jj, TP + b:TP + b + 1], in_=banks[1][0:64, jj * 8 + 2 * jj:jj * 8 + 2 * jj + 1]),
                          r=[bk(1)], w=[("oagT", 4)])
                    s.dve(lambda e, jj=jj, b=b: e.tensor_copy(out=oagT[64:128, jj, TP + b:TP + b + 1], in_=banks[1][64:128, jj * 8 + 2 * jj + 1:jj * 8 + 2 * jj + 2]),
                          r=[bk(1)], w=[("oagT", 4)])

        barrier()
        if stop_after >= 4:
            a4 = Bump(0, 84)
            a4c = Bump(117, 150)
            wstf = [a4c([128, 2048]) for i in range(2)]
            W4 = a4([128, 8, 2560], BF16)
            rW4 = load_w_in(W4, 672, 3232, wstf, engs=("pool", "dve"))
            OBQ, OBF, OBI, OGA, OGB = 0, 512, 1024, 1536, 2048
            qS = a4([128, 4, 512], BF16); fS = a4([128, 4, 512]); kS = a4([128, 4, 512]); thb = a4([128, 512])
            vbf = [a4([128, 512], BF16) for i in range(2)]
            sgb = [a4([128, 512], BF16) for i in range(2)]
            PA = a4([128, 4, 128]); rPA = a4([128, 4, 128]); rk4 = a4([128, 4, 128])
            Pt4s = [a4([128, 4, 2]) for i in range(2)]
            qe4 = a4([128, 4, 128], BF16); ke4 = a4([128, 4, 128], BF16); qPt4 = a4([128, 4, 128], BF16); kd24 = a4([128, 4, 128], BF16)
            qPB4 = a4([128, 4, 64], BF16); kdA4 = a4([128, 4, 64], BF16); am4 = a4([128, 4, 128], BF16); kdt4 = a4([128, 4, 128], BF16)
            S32 = a4([128, 4, 128]); Sbf = a4([128, 4, 128], BF16)
            zer = a4([128, 64])
            hst = a4([128, 8]); otmp = a4([128, 512], BF16)
            s.pool(lambda e: e.memset(zer[:], 0.0), w=["zer"])
            s.pool(lambda e: e.memset(S32[:], 0.0), w=[("S32", h) for h in range(4)])
            s.pool(lambda e: e.memset(Sbf[:], 0.0), w=[("Sbf", h) for h in range(4)])
            Sold = a4c([128, NS, 4, 128])
            vsb = a4c([NS, 512])
            onesel = a4c([NS, NS, 128])
            qsel = a4c([128, 4, NS, NS])
            Snew = [a4c([128, 128]) for i in range(2)]
            stmp = a4c([128, 128])
            junk4 = a4c([128, 512], BF16)
            sga = a4c([128, 512], BF16)
            obg = a4c([128, 512], BF16)
            for b in range(NS):
                s.dma("so%d" % b, lambda e, b=b: e.dma_start(out=Sold[:, b], in_=st_in[b].rearrange("h k v -> k h v")), w=[("Sold", b)])
                s.dve(lambda e, b=b: e.tensor_copy(out=onesel[:, b, :], in_=identf[0:NS, b:b + 1].to_broadcast([NS, 128])), r=["identf"], w=["onesel"])
            s.pool(lambda e: e.memset(qsel[:], 0.0), w=["qsel"])

            hgroups = [(SEQ, NMETA, [(0, NMETA, 0)], "meta")]
            hgroups += [(512 * g, 512, [(1 + 4 * g + i, 128, 128 * i) for i in range(4)], "x") for g in range(4)]
            hgroups.append((TP, NS, [(17, NS, 0)], "sample"))
            first_chunk = True
            tn = 0
            for gi, (c0, n, gt, kind) in enumerate(hgroups):
                rx = [("xnT", ti) for ti, _, _ in gt]
                for h in range(4):
                    for (off, pb_, which) in ((OBQ, 0, "q"), (OBF, 1, "f")):
                        for k in range(8):
                            s.pe(lambda e, k=k, h=h, off=off, pb_=pb_, c0=c0, n=n: e.matmul(banks[pb_][:, 0:n], lhsT=W4[:, k, off + h * 128:off + (h + 1) * 128],
                                                                                          rhs=xnT[:, k, c0:c0 + n], start=(k == 0), stop=(k == 7)),
                                 r=rx + [rW4], w=[bk(pb_)])
                        if which == "q":
                            s.act(lambda e, h=h, n=n: e.activation(out=qS[:, h, 0:n], in_=banks[0][:, 0:n], func=AF.Silu), r=[bk(0)], w=["qS"])
                        else:
                            s.act(lambda e, n=n: e.activation(out=thb[:, 0:n], in_=banks[1][:, 0:n], func=AF.Tanh, scale=0.5), r=[bk(1)], w=["thb"])
                            s.dve(lambda e, h=h, n=n: e.tensor_scalar(out=fS[:, h, 0:n], in0=thb[:, 0:n], scalar1=hg_nb[:, h:h + 1], scalar2=hg_a[:, h:h + 1],
                                                                     op0=ALU.mult, op1=ALU.add), r=["thb", "hg_nb", "hg_a"], w=["fS"])
                            s.dve(lambda e, h=h, n=n: e.tensor_scalar(out=kS[:, h, 0:n], in0=thb[:, 0:n], scalar1=hg_nnb[:, h:h + 1], scalar2=hg_nb[:, h:h + 1],
                                                                     op0=ALU.mult, op1=ALU.add), r=["thb", "hg_nb", "hg_nnb"], w=["kS"])
                for (ti, R, lo) in gt:
                    b2 = tn % 2
                    tn += 1
                    cb = c0 + lo
                    for (off, pb_) in ((OBI, 2), (OGB, 3)):
                        for k in range(8):
                            s.pe(lambda e, k=k, off=off, pb_=pb_, cb=cb, R=R: e.matmul(banks[pb_][:R, :], lhsT=xnT[:, k, cb:cb + R], rhs=W4[:, k, off:off + 512],
                                                                                     start=(k == 0), stop=(k == 7)), r=[("xnT", ti), rW4], w=[bk(pb_)])
                    if kind != "sample":
                        s.act(lambda e, R=R, b2=b2: e.activation(out=vbf[b2][:R, :], in_=banks[2][:R, :], func=AF.Copy), r=[bk(2)], w=["vbf%d" % b2])
                    else:
                        s.act(lambda e, R=R: e.activation(out=vsb[:R, :], in_=banks[2][:R, :], func=AF.Copy), r=[bk(2)], w=["vsb"])
                    if kind != "meta":
                        s.act(lambda e, R=R, b2=b2: e.activation(out=sgb[b2][:R, :], in_=banks[3][:R, :], func=AF.Silu), r=[bk(3)], w=["sgb%d" % b2])
                    if kind != "sample":
                        chunks = [(0, R)] if kind == "meta" else [(0, 64), (64, 64)]
                        tpar = tn % 2
                        Pt4 = Pt4s[tpar]
                        rPt = "Pt4_%d" % tpar
                        fv = fS[:, :, lo:lo + R]; qv = qS[:, :, lo:lo + R]; kv = kS[:, :, lo:lo + R]
                        for h in range(4):
                            for (cs, L) in chunks:
                                s.dve(lambda e, cs=cs, L=L, h=h, lo=lo: e.tensor_tensor_scan(out=PA[:, h, cs:cs + L], data0=fS[:, h, lo + cs:lo + cs + L], data1=zer[:, 0:L],
                                                                                          initial=1.0, op0=ALU.mult, op1=ALU.add), r=["fS", "zer"], w=["PA"])
                        s.dve(lambda e, R=R: e.reciprocal(out=rPA[:, :, 0:R], in_=PA[:, :, 0:R]), r=["PA"], w=["rPA"])
                        if kind == "meta":
                            s.dve(lambda e, R=R, Pt4=Pt4: e.tensor_copy(out=Pt4[:, :, 0:1], in_=PA[:, :, R - 1:R]), r=["PA"], w=[rPt])
                            s.dve(lambda e, R=R, kv=kv: e.tensor_tensor(out=rk4[:, :, 0:R], in0=rPA[:, :, 0:R], in1=kv, op=ALU.mult), r=["rPA", "kS"], w=["rk4"])
                            s.dve(lambda e, R=R, Pt4=Pt4: e.tensor_tensor(out=kd24[:, :, 0:R], in0=rk4[:, :, 0:R], in1=Pt4[:, :, 0:1].to_broadcast([128, 4, R]), op=ALU.mult),
                                  r=["rk4", rPt], w=["kd24"])
                        else:
                            c4 = lambda ap: ap.rearrange("p h (c t) -> p h c t", t=64)
                            s.dve(lambda e, Pt4=Pt4: e.tensor_tensor(out=Pt4[:, :, 0:1], in0=PA[:, :, 63:64], in1=PA[:, :, 127:128], op=ALU.mult), r=["PA"], w=[rPt])
                            s.dve(lambda e, Pt4=Pt4: e.tensor_copy(out=Pt4[:, :, 1:2], in_=PA[:, :, 127:128]), r=["PA", rPt], w=[rPt])
                            s.dve(lambda e, qv=qv: e.tensor_tensor(out=qPt4[:, :, :], in0=PA[:, :, :], in1=qv, op=ALU.mult), r=["PA", "qS"], w=["qPt4"])
                            s.dve(lambda e: e.tensor_tensor(out=c4(qe4[:, :, :]), in0=c4(qPt4[:, :, :]), in1=c4(rPA[:, :, :])[:, :, :, 31:32].to_broadcast([128, 4, 2, 64]), op=ALU.mult),
                                  r=["qPt4", "rPA"], w=["qe4"])
                            s.dve(lambda e: e.tensor_copy(out=qPB4[:, :, :], in_=qPt4[:, :, 64:128]), r=["qPt4"], w=["qPB4"])
                            s.dve(lambda e: e.tensor_tensor(out=qPt4[:, :, 64:128], in0=qPB4[:, :, :], in1=PA[:, :, 63:64].to_broadcast([128, 4, 64]), op=ALU.mult),
                                  r=["qPB4", "PA", "qe4"], w=["qPt4"])
                            s.dve(lambda e, kv=kv: e.tensor_tensor(out=rk4[:, :, :], in0=rPA[:, :, :], in1=kv, op=ALU.mult), r=["rPA", "kS"], w=["rk4"])
                            s.dve(lambda e: e.tensor_tensor(out=c4(ke4[:, :, :]), in0=c4(rk4[:, :, :]), in1=c4(PA[:, :, :])[:, :, :, 31:32].to_broadcast([128, 4, 2, 64]), op=ALU.mult),
                                  r=["rk4", "PA"], w=["ke4"])
                            s.dve(lambda e: e.tensor_tensor(out=kdA4[:, :, :], in0=rk4[:, :, 0:64], in1=PA[:, :, 63:64].to_broadcast([128, 4, 64]), op=ALU.mult),
                                  r=["rk4", "PA"], w=["kdA4"])
                            s.dve(lambda e, Pt4=Pt4: e.tensor_tensor(out=c4(kd24[:, :, :]), in0=c4(rk4[:, :, :]), in1=Pt4[:, :, 0:2].unsqueeze(3).to_broadcast([128, 4, 2, 64]), op=ALU.mult),
                                  r=["rk4", rPt], w=["kd24"])
                        for h in range(4):
                            if kind != "meta":
                                s.pe(lambda e, h=h: e.matmul(banks[4][:, h * 128:(h + 1) * 128], lhsT=ke4[:, h, :], rhs=qe4[:, h, :], start=True, stop=True),
                                     r=["ke4", "qe4"], w=[bk(4)])
                                s.pe(lambda e, h=h: e.matmul(banks[4][0:64, h * 128 + 64:(h + 1) * 128], lhsT=kdA4[:, h, :], rhs=qPB4[:, h, :], start=True, stop=True,
                                                                   skip_group_check=True), r=["kdA4", "qPB4"], w=[bk(4)])
                                s.dve(lambda e, h=h: e.tensor_tensor(out=am4[:, h, :], in0=banks[4][:, h * 128:(h + 1) * 128], in1=tri[:, :], op=ALU.mult),
                                      r=[bk(4), "tri"], w=["am%d" % h])
                            s.pe(lambda e, h=h, R=R: e.transpose(out=bkb[5][:R, h * 128:(h + 1) * 128], in_=kd24[:, h, 0:R], identity=ident[:, :]),
                                 r=["kd24", "ident"], w=[bk(5)])
                            s.act(lambda e, h=h, R=R: e.activation(out=kdt4[:R, h, :], in_=bkb[5][:R, h * 128:(h + 1) * 128], func=AF.Copy), r=[bk(5)], w=["kdt%d" % h])
                        if kind != "meta":
                            for h in range(4):
                                s.pe(lambda e, h=h, b2=b2: e.matmul(banks[6][:, h * 128:(h + 1) * 128], lhsT=am4[:, h, :], rhs=vbf[b2][:, h * 128:(h + 1) * 128],
                                                                          start=True, stop=False), r=["am%d" % h, "vbf%d" % b2], w=[bk(6)])
                                s.pe(lambda e, h=h: e.matmul(banks[6][:, h * 128:(h + 1) * 128], lhsT=qPt4[:, h, :], rhs=Sbf[:, h, :], start=False, stop=True),
                                     r=["qPt4", ("Sbf", h)], w=[bk(6)])
                        for h in range(4):
                            s.pe(lambda e, h=h, b2=b2, R=R: e.matmul(banks[7][:, h * 128:(h + 1) * 128], lhsT=kdt4[:R, h, :], rhs=vbf[b2][:R, h * 128:(h + 1) * 128],
                                                                           start=True, stop=True), r=["kdt%d" % h, "vbf%d" % b2], w=[bk(7)])
                            s.dve(lambda e, h=h, Pt4=Pt4: e.scalar_tensor_tensor(out=S32[:, h, :], in0=S32[:, h, :], scalar=Pt4[:, h, 0:1], in1=banks[7][:, h * 128:(h + 1) * 128],
                                                                              op0=ALU.mult, op1=ALU.add), r=[("S32", h), rPt, bk(7)], w=[("S32", h)])
                            s.act(lambda e, h=h: e.activation(out=Sbf[:, h, :], in_=S32[:, h, :], func=AF.Copy), r=[("S32", h)], w=[("Sbf", h)])
                    else:
                        for b in range(NS):
                            s.dve(lambda e, b=b: e.tensor_copy(out=qsel[:, :, b, b], in_=qS[:, :, b]), r=["qS", "qsel"], w=["qsel"])
                        for b in range(NS):
                            s.pe(lambda e, b=b: e.matmul(banks[4][:, :], lhsT=onesel[:, b, :], rhs=vsb[:, :], start=True, stop=True), r=["onesel", "vsb"], w=[bk(4)])
                            for h in range(4):
                                sn = Snew[(b * 4 + h) % 2]
                                rs = "Snew%d" % ((b * 4 + h) % 2)
                                s.dve(lambda e, b=b, h=h: e.tensor_scalar(out=stmp[:, :], in0=Sold[:, b, h, :], scalar1=fS[:, h, b:b + 1], scalar2=None, op0=ALU.mult),
                                      r=[("Sold", b), "fS"], w=["stmp"])
                                s.dve(lambda e, b=b, h=h, sn=sn: e.scalar_tensor_tensor(out=sn[:, :], in0=banks[4][:, h * 128:(h + 1) * 128], scalar=kS[:, h, b:b + 1], in1=stmp[:, :],
                                                                                       op0=ALU.mult, op1=ALU.add), r=[bk(4)] + ["kS", "stmp"], w=[rs])
                                s.dma("hs%d" % ((b * 4 + h) % 2), lambda e, b=b, h=h, sn=sn: e.dma_start(out=hg_s[b, h], in_=sn[:, :]), r=[rs])
                                s.pe(lambda e, b=b, h=h, sn=sn: e.matmul(banks[6][0:NS, h * 128:(h + 1) * 128], lhsT=qsel[:, h, b, :], rhs=sn[:, :],
                                                                        start=(b == 0 and h == 0), stop=(b == NS - 1 and h == 3), skip_group_check=True), r=["qsel", rs], w=[bk(6)])
                    if kind == "meta":
                        continue
                    s.act(lambda e, R=R: e.activation(out=junk4[:R, :], in_=banks[6][:R, :], func=AF.Square), r=[bk(6)], w=["junk4"])
                    s.dve(lambda e, R=R: e.tensor_reduce(out=hst[:R, 0:4], in_=junk4[:R, :].rearrange("p (h d) -> p h d", d=128), axis=AX.X, op=ALU.add), r=["junk4"], w=["hst"])
                    s.dve(lambda e, R=R: e.tensor_scalar(out=hst[:R, 4:8], in0=hst[:R, 0:4], scalar1=1.0 / 128, scalar2=EPS, op0=ALU.mult, op1=ALU.add), r=["hst"], w=["hst"])
                    s.pool(lambda e, R=R: e.tensor_tensor(out=hst[:R, 4:8], in0=hst[:R, 4:8], in1=neghalf[:R, 0:4], op=ALU.pow), r=["hst", "neghalf"], w=["hst"])
                    s.dve(lambda e, R=R: e.tensor_tensor(out=otmp[:R, :].rearrange("p (h d) -> p h d", d=128), in0=banks[6][:R, :].rearrange("p (h d) -> p h d", d=128),
                                                        in1=hst[:R, 4:8].unsqueeze(2).to_broadcast([R, 4, 128]), op=ALU.mult), r=[bk(6), "hst"], w=["otmp"])
                    s.dve(lambda e, R=R, b2=b2: e.tensor_tensor(out=obg[:R, :], in0=otmp[:R, :], in1=sgb[b2][:R, :], op=ALU.mult), r=["otmp", "sgb%d" % b2], w=["obg"])
                    for c in range(4):
                        s.pe(lambda e, R=R, c=c: e.transpose(out=bkb[5][:, c * 128:c * 128 + R], in_=obg[:R, c * 128:(c + 1) * 128], identity=ident[:R, :R]), r=["obg", "ident"], w=[bk(5)])
                    s.act(lambda e, R=R, cb=cb: e.activation(out=obgT[:, :, cb:cb + R], in_=bkb[5][:, 0:512].rearrange("p (c r) -> p c r", r=128)[:, :, 0:R], func=AF.Copy),
                          r=[bk(5)], w=[("obgT", gi)])
                if kind != "meta":
                    for c in range(4):
                        for k in range(8):
                            s.pe(lambda e, k=k, c=c, c0=c0, n=n: e.matmul(banks[c % 2][:, 0:n], lhsT=W4[:, k, OGA + c * 128:OGA + (c + 1) * 128], rhs=xnT[:, k, c0:c0 + n],
                                                                         start=(k == 0), stop=(k == 7)), r=rx + [rW4], w=[bk(c % 2)])
                        s.act(lambda e, c=c, n=n: e.activation(out=sga[:, 0:n], in_=banks[c % 2][:, 0:n], func=AF.Silu), r=[bk(c % 2)], w=["sga"])
                        s.dve(lambda e, c=c, n=n, c0=c0: e.tensor_tensor(out=oagT[:, c, c0:c0 + n], in0=oagT[:, c, c0:c0 + n], in1=sga[:, 0:n], op=ALU.mult),
                              r=["sga", ("oagT", gi - 1)], w=[("oagT", gi - 1)])
            s.dma("hp", lambda e: e.dma_start(out=hg_p.rearrange("h k v -> k h v"), in_=S32[:, :, :]), r=[("S32", h) for h in range(4)])

        barrier()
        if stop_after >= 5:
            a5 = Bump(0, 84)
            a5c = Bump(117, 150)
            wstf = [a5c([128, 2048]) for i in range(2)]
            W5 = a5([128, 8, 2048], BF16)
            rW5 = load_w_in(W5, 3232, 5280, wstf, engs=("pool", "dve"))
            Woa = a5([128, 4, 1024], BF16); Wob = a5([128, 4, 1024], BF16); Wo = a5([128, 8, 1024], BF16)
            zT = a5([128, 8, 512], BF16)
            tha = a5([128, 512]); thm = a5([128, 512]); t1 = a5([128, 512])
            xin = [a5c([128, 1024]) for i in range(2)]
            yout = [a5c([128, 1024]) for i in range(2)]
            wi = 0
            for (src_w, dst_w, nk, kind) in ((w_oa, Woa, 4, "plain"), (w_ob, Wob, 4, "gbn"), (w_o, Wo, 8, "half")):
                for k in range(nk):
                    for hf in range(2):
                        b = wi % 2
                        wi += 1
                        stv = wstf[b][:, 0:512]
                        s.dma("wst%d" % b, lambda e, stv=stv, src_w=src_w, k=k, hf=hf: e.dma_start(out=stv, in_=src_w[k * 128:(k + 1) * 128, hf * 512:(hf + 1) * 512]), w=["wst%d" % b])
                        dstv = dst_w[:, k, hf * 512:(hf + 1) * 512]
                        rsw = ("Wm", id(dst_w))
                        sc = None if kind == "plain" else (gbn_c[:, 0:1] if kind == "gbn" else 0.5)
                        eng = ("pool", "act", "dve")[wi % 3]
                        if eng == "act":
                            if sc is None:
                                s.act(lambda e, stv=stv, dstv=dstv: e.activation(out=dstv, in_=stv, func=AF.Copy), r=["wst%d" % b], w=[rsw])
                            else:
                                s.act(lambda e, stv=stv, dstv=dstv, sc=sc: e.activation(out=dstv, in_=stv, func=AF.Copy, scale=sc), r=["wst%d" % b, "gbn_c"], w=[rsw])
                        else:
                            if sc is None:
                                s.add(eng, lambda e, stv=stv, dstv=dstv: e.tensor_copy(out=dstv, in_=stv), ["wst%d" % b], [rsw])
                            else:
                                s.add(eng, lambda e, stv=stv, dstv=dstv, sc=sc: e.tensor_scalar(out=dstv, in0=stv, scalar1=sc, scalar2=None, op0=ALU.mult),
                                      ["wst%d" % b, "gbn_c"], [rsw])
            rWoa, rWob, rWo = ("Wm", id(Woa)), ("Wm", id(Wob)), ("Wm", id(Wo))
            mgroups = [(512 * g, 512, [(1 + 4 * g + i, 128, 128 * i) for i in range(4)], g, g + 1) for g in range(4)]
            mgroups.append((TP, NS, [(17, NS, 0)], 4, 5))
            tn = 0
            for (c0, n, gt, og, hgi) in mgroups:
                rx = [("xnT", ti) for ti, _, _ in gt]
                for c in range(8):
                    bs = 4 * (c % 2)
                    for j in range(4):
                        s.pe(lambda e, c=c, j=j, c0=c0, n=n, bs=bs: e.matmul(banks[bs][:, 0:n], lhsT=Woa[:, j, c * 128:(c + 1) * 128], rhs=oagT[:, j, c0:c0 + n], start=(j == 0), stop=(j == 3)),
                             r=[rWoa, ("oagT", og)], w=[bk(bs)])
                    for j in range(4):
                        s.pe(lambda e, c=c, j=j, c0=c0, n=n, bs=bs: e.matmul(banks[bs + 1][:, 0:n], lhsT=Wob[:, j, c * 128:(c + 1) * 128], rhs=obgT[:, j, c0:c0 + n], start=(j == 0), stop=(j == 3)),
                             r=[rWob, ("obgT", hgi)], w=[bk(bs + 1)])
                    for (off, pb_) in ((0, bs + 2), (1024, bs + 3)):
                        for k in range(8):
                            s.pe(lambda e, c=c, k=k, off=off, pb_=pb_, c0=c0, n=n: e.matmul(banks[pb_][:, 0:n], lhsT=W5[:, k, off + c * 128:off + (c + 1) * 128], rhs=xnT[:, k, c0:c0 + n],
                                                                                          start=(k == 0), stop=(k == 7)), r=rx + [rW5], w=[bk(pb_)])
                    s.act(lambda e, n=n, bs=bs: e.activation(out=tha[:, 0:n], in_=banks[bs + 2][:, 0:n], func=AF.Tanh, scale=0.5), r=[bk(bs + 2)], w=["tha"])
                    s.act(lambda e, n=n, bs=bs: e.activation(out=thm[:, 0:n], in_=banks[bs + 3][:, 0:n], func=AF.Tanh, scale=0.5), r=[bk(bs + 3)], w=["thm"])
                    s.dve(lambda e, n=n, bs=bs: e.scalar_tensor_tensor(out=t1[:, 0:n], in0=tha[:, 0:n], scalar=1.0, in1=banks[bs][:, 0:n], op0=ALU.add, op1=ALU.mult), r=["tha", bk(bs)], w=["t1"])
                    s.dve(lambda e, n=n, bs=bs: e.scalar_tensor_tensor(out=thm[:, 0:n], in0=thm[:, 0:n], scalar=1.0, in1=banks[bs + 1][:, 0:n], op0=ALU.add, op1=ALU.mult), r=["thm", bk(bs + 1)], w=["thm"])
                    s.dve(lambda e, c=c, n=n: e.tensor_tensor(out=zT[:, c, 0:n], in0=t1[:, 0:n], in1=thm[:, 0:n], op=ALU.add), r=["t1", "thm"], w=["zT"])
                for (ti, R, lo) in gt:
                    b2 = tn % 2
                    tn += 1
                    srcx = xs[:, :] if ti == 17 else xp[(ti - 1) * 128:ti * 128, :]
                    dsty = y_s[:, :] if ti == 17 else y_p[(ti - 1) * 128:ti * 128, :]
                    s.dma("xin%d" % b2, lambda e, b2=b2, R=R, srcx=srcx: e.dma_start(out=xin[b2][:R, :], in_=srcx), w=["xin%d" % b2])
                    for hf in range(2):
                        pb_ = 4 + hf
                        for c in range(8):
                            s.pe(lambda e, c=c, hf=hf, pb_=pb_, R=R, lo=lo: e.matmul(banks[pb_][:R, :], lhsT=zT[:, c, lo:lo + R], rhs=Wo[:, c, hf * 512:(hf + 1) * 512],
                                                                                   start=(c == 0), stop=(c == 7)), r=["zT", rWo], w=[bk(pb_)])
                        s.dve(lambda e, hf=hf, pb_=pb_, R=R, b2=b2: e.tensor_tensor(out=yout[b2][:R, hf * 512:(hf + 1) * 512], in0=banks[pb_][:R, :], in1=xin[b2][:R, hf * 512:(hf + 1) * 512],
                                                                                  op=ALU.add), r=[bk(pb_), "xin%d" % b2], w=["yout%d" % b2])
                    s.dma("yo%d" % b2, lambda e, b2=b2, R=R, dsty=dsty: e.dma_start(out=dsty, in_=yout[b2][:R, :]), r=["yout%d" % b2])

        s.emit(st)
    return nc


_NC = {}


def _get_nc(dbg=False):
    if dbg not in _NC:
        _NC[dbg] = build(dbg)
    return _NC[dbg]


def make_in_maps(x_prompt, x_sample, cache_latent, cache_krope, state_hgrn, page_table, meta_tokens,
                 norm_g, w_in, g_cq, w_uq, g_ckv, w_uk, w_uv, g_qn, g_kn, lb_logits, g_bn, w_oa, w_ob, w_o):
    f = lambda a: np.ascontiguousarray(np.asarray(a, dtype=np.float32))
    ccomb = np.concatenate([f(cache_latent)[0].reshape(NPOOL * 128, 256), f(cache_krope)[0].reshape(NPOOL * 128, 32)],
                           axis=1).reshape(NPOOL * 32, 4 * 288)
    shared = dict(meta=f(meta_tokens), ccomb=ccomb, norm_g=f(norm_g), w_in=f(w_in)[0], g_cq=f(g_cq),
                  w_uq=f(w_uq)[0], g_ckv=f(g_ckv), w_uk=f(w_uk)[0].reshape(256, 512), w_uv=f(w_uv)[0].reshape(256, 512),
                  g_qn=f(g_qn), g_kn=f(g_kn), lb=f(lb_logits), g_bn=f(g_bn), w_oa=f(w_oa)[0], w_ob=f(w_ob)[0], w_o=f(w_o)[0])
    xpf = f(x_prompt); xsf = f(x_sample); stf = f(state_hgrn)
    ptab = np.ascontiguousarray(np.asarray(page_table, dtype=np.int32))
    maps = []
    for c in range(NCORES):
        m = dict(shared)
        m["xp"] = xpf[c]
        m["xs"] = xsf[NS * c:NS * (c + 1), 0]
        m["st_in"] = stf[0, NS * c:NS * (c + 1)]
        m["pt"] = ptab[NS * c:NS * (c + 1)]
        maps.append(m)
    return maps


def kernel(**inputs):
    nc = _get_nc(False)
    maps = make_in_maps(**inputs)
    res = run_bass_kernel_spmd(nc, maps, core_ids=list(range(NCORES)))
    r = res.results
    cat = lambda k: np.stack([np.asarray(x[k]) for x in r], axis=0)
    y_p = cat("y_p")
    y_s = np.concatenate([np.asarray(x["y_s"]) for x in r], axis=0)[:, None, :]
    lat_p = cat("lat_p")[None]
    kr_p = cat("kr_p")[None]
    hg_p = cat("hg_p")[None]
    lat_s = np.concatenate([np.asarray(x["lat_s"]) for x in r], axis=0)[None, :, None, :]
    kr_s = np.concatenate([np.asarray(x["kr_s"]) for x in r], axis=0)[None, :, None, :]
    hg_s = np.concatenate([np.asarray(x["hg_s"]) for x in r], axis=0)[None]
    return (y_p, y_s, lat_p, kr_p, hg_p, lat_s, kr_s, hg_s)
```
